# Optimizing a Trainium2 kernel written in Bass

```python
import jax, jax.numpy as jnp
from jax import lax
import numpy as np

D_MODEL = 1024
BATCH = 16
SEQ = 4096
DEPTH = 1
DEC_BATCH = 32
DEC_SEQ = 64
PAST_LEN = 2048

CHUNK = 64
D_HG = 512
HG_HEADS = 4
HG_DK = 128
HG_DV = 128
D_SW = 512
SW_HEADS = 8
SW_KV = 2
SW_GROUP = SW_HEADS // SW_KV
SW_HD = 64
WINDOW = 128
N_WIN = WINDOW // CHUNK
D_FF = 2816
D_IN = 4 * D_HG + (SW_HEADS + 2 * SW_KV) * SW_HD
N_MOD = 9
EPS = 1e-6

kernel_name = 'hymba_hgrn2_swa_sink_macaron_adaln_stream'


def rmsnorm(x, g):
    xf = x.astype(jnp.float32)
    y = xf * lax.rsqrt(jnp.mean(xf * xf, axis=-1, keepdims=True) + EPS)
    return (y * g.astype(jnp.float32)).astype(x.dtype)


def adaln(c, w, b):
    m = jax.nn.silu(c) @ w + b
    return jnp.split(m[:, None, :], N_MOD, axis=-1)


def modulate(x, g, shift, scale):
    return rmsnorm(x, g) * (1.0 + scale) + shift


def swiglu(h, w_up, w_down):
    gate, up = jnp.split(h @ w_up, 2, axis=-1)
    return (jax.nn.silu(gate) * up) @ w_down


def hgrn_lower_bound(lb_logits, layer):
    p = jax.nn.softmax(lb_logits.astype(jnp.float32), axis=0)
    return jnp.cumsum(p, axis=0)[layer]


def alibi_slopes():
    return jnp.exp2(-8.0 * jnp.arange(1, SW_HEADS + 1, dtype=jnp.float32) / SW_HEADS)


def in_projection(h, w_in, lb):
    lead = h.shape[:-1]
    o1 = 4 * D_HG
    o2 = o1 + SW_HEADS * SW_HD
    o3 = o2 + SW_KV * SW_HD
    q, f, i, og, sq, sk, sv = jnp.split(h @ w_in, [D_HG, 2 * D_HG, 3 * D_HG, o1, o2, o3], axis=-1)
    forget = lb + (1.0 - lb) * jax.nn.sigmoid(f.astype(jnp.float32))
    heads = lambda t, d: t.reshape(*lead, -1, d)
    hq = heads(jax.nn.silu(q.astype(jnp.float32)), HG_DK)
    hk = heads(1.0 - forget, HG_DK)
    hg = heads(jnp.log(forget), HG_DK)
    hv = heads(i.astype(jnp.float32), HG_DV)
    return hq, hk, hv, hg, heads(og, HG_DV), heads(sq, SW_HD), heads(sk, SW_HD), heads(sv, SW_HD)


def hgrn_chunk(S, q, k, v, g):
    L = q.shape[2]
    b = jnp.cumsum(g, axis=2)
    inter = jnp.einsum('bhtk,bhkv->bhtv', q * jnp.exp(b), S)
    causal = jnp.tril(jnp.ones((L, L), dtype=bool))
    diff = b[:, :, :, None, :] - b[:, :, None, :, :]
    decay = jnp.exp(jnp.where(causal[:, :, None], diff, -jnp.inf))
    A = jnp.einsum('bhtk,bhsk,bhtsk->bhts', q, k, decay)
    o = inter + jnp.einsum('bhts,bhsv->bhtv', A, v)
    b_last = b[:, :, -1:, :]
    S_new = jnp.exp(b_last[:, :, 0, :])[..., None] * S + jnp.einsum('bhsk,bhsv->bhkv', k * jnp.exp(b_last - b), v)
    return S_new, o


def hgrn_prompt(q, k, v, g):
    B, S_len = q.shape[:2]
    nc = S_len // CHUNK
    blocks = lambda t: t.reshape(B, nc, CHUNK, HG_HEADS, t.shape[-1]).transpose(1, 0, 3, 2, 4)
    S0 = jnp.zeros((B, HG_HEADS, HG_DK, HG_DV), jnp.float32)
    S_fin, o = lax.scan(lambda S, inp: hgrn_chunk(S, *inp), S0, (blocks(q), blocks(k), blocks(v), blocks(g)))
    o = o.transpose(1, 0, 3, 2, 4).reshape(B, S_len, HG_HEADS, HG_DV)
    return o, S_fin


def hgrn_readout(o, og, g_norm):
    y = rmsnorm(o, g_norm) * jax.nn.silu(og.astype(jnp.float32))
    return y.reshape(*o.shape[:-2], D_HG)


def sink_attention(q, k, v, q_pos, k_pos, k_valid, sinks):
    B, N, Lq = q.shape[:3]
    qg = q.reshape(B, N, Lq, SW_KV, SW_GROUP, SW_HD)
    s = jnp.einsum('bnqhgd,bnshd->bnhgqs', qg, k).astype(jnp.float32) * (SW_HD ** -0.5)
    slopes = alibi_slopes().reshape(SW_KV, SW_GROUP)
    dist = jnp.abs(q_pos[:, :, None] - k_pos[:, None, :]).astype(jnp.float32)
    s = s - slopes[None, None, :, :, None, None] * dist[None, :, None, None, :, :]
    s = jnp.where(k_valid[None, :, None, None, None, :], s, -jnp.inf)
    sink = sinks.astype(jnp.float32).reshape(SW_KV, SW_GROUP)[None, None, :, :, None, None]
    m = jnp.maximum(jnp.max(s, axis=-1, keepdims=True), sink)
    p = jnp.exp(s - m)
    p = p / (jnp.sum(p, axis=-1, keepdims=True) + jnp.exp(sink - m))
    o = jnp.einsum('bnhgqs,bnshd->bnqhgd', p.astype(v.dtype), v)
    return o.reshape(B, N, Lq, SW_HEADS * SW_HD)


def band_blocks(t):
    B, S_len = t.shape[:2]
    nc = S_len // CHUNK
    tp = jnp.pad(t, ((0, 0), (WINDOW, 0), (0, 0), (0, 0))).reshape(B, nc + N_WIN, CHUNK, SW_KV, SW_HD)
    return jnp.concatenate([tp[:, j:j + nc] for j in range(N_WIN + 1)], axis=2)


def mixer_prompt(h, w_in, lb, g_hgrn, sinks, w_out):
    B, S_len, _ = h.shape
    nc = S_len // CHUNK
    hq, hk, hv, hg, og, sq, sk, sv = in_projection(h, w_in, lb)
    o_hg, S_fin = hgrn_prompt(hq, hk, hv, hg)
    y_hg = hgrn_readout(o_hg, og, g_hgrn)
    q_pos = jnp.arange(nc)[:, None] * CHUNK + jnp.arange(CHUNK)[None, :]
    k_pos = jnp.arange(nc)[:, None] * CHUNK - WINDOW + jnp.arange((N_WIN + 1) * CHUNK)[None, :]
    y_sw = sink_attention(sq.reshape(B, nc, CHUNK, SW_HEADS, SW_HD), band_blocks(sk), band_blocks(sv),
                          q_pos, k_pos, k_pos >= 0, sinks).reshape(B, S_len, D_SW)
    out = jnp.concatenate([y_hg.astype(h.dtype), y_sw.astype(h.dtype)], axis=-1) @ w_out
    return out, S_fin, sk[:, -WINDOW:], sv[:, -WINDOW:]


def mixer_sample(h, S0, k_cache, v_cache, w_in, lb, g_hgrn, sinks, w_out):
    B, L, _ = h.shape
    hq, hk, hv, hg, og, sq, sk, sv = in_projection(h, w_in, lb)
    tr = lambda t: t.transpose(0, 2, 1, 3)
    S_new, o = hgrn_chunk(S0.astype(jnp.float32), tr(hq), tr(hk), tr(hv), tr(hg))
    y_hg = hgrn_readout(tr(o), og, g_hgrn)
    n_cache = k_cache.shape[1]
    k_all = jnp.concatenate([k_cache.astype(sk.dtype), sk], axis=1)
    v_all = jnp.concatenate([v_cache.astype(sv.dtype), sv], axis=1)
    q_pos = (n_cache + jnp.arange(L))[None, :]
    k_pos = jnp.arange(n_cache + L)[None, :]
    y_sw = sink_attention(sq.reshape(B, 1, L, SW_HEADS, SW_HD), k_all[:, None], v_all[:, None],
                          q_pos, k_pos, k_pos >= 0, sinks).reshape(B, L, D_SW)
    out = jnp.concatenate([y_hg.astype(h.dtype), y_sw.astype(h.dtype)], axis=-1) @ w_out
    return out, S_new, k_all[:, -WINDOW:], v_all[:, -WINDOW:]


def block(x, c, mix, w_ada, b_ada, g_ffn1, w_up1, w_down1, g_mix, g_ffn2, w_up2, w_down2):
    sh1, sc1, gt1, sh2, sc2, gt2, sh3, sc3, gt3 = adaln(c, w_ada, b_ada)
    x = x + 0.5 * gt1 * swiglu(modulate(x, g_ffn1, sh1, sc1), w_up1, w_down1)
    m, S_new, k_new, v_new = mix(modulate(x, g_mix, sh2, sc2))
    x = x + gt2 * m
    x = x + 0.5 * gt3 * swiglu(modulate(x, g_ffn2, sh3, sc3), w_up2, w_down2)
    return x, S_new, k_new, v_new


def setup_inputs(seed: int = 0) -> dict:
    key = jax.random.key(seed)
    ks = jax.random.split(key, 24)
    nrm = lambda k, shape, s=1.0: jax.random.normal(k, shape, jnp.float32) * s
    n_win = min(WINDOW, PAST_LEN)
    return {
        'x_prompt': nrm(ks[0], (BATCH, SEQ, D_MODEL)),
        'x_sample': nrm(ks[1], (DEC_BATCH, DEC_SEQ, D_MODEL)),
        'state_hgrn': nrm(ks[2], (DEPTH, DEC_BATCH, HG_HEADS, HG_DK, HG_DV), 0.5),
        'cache_k': nrm(ks[3], (DEPTH, DEC_BATCH, n_win, SW_KV, SW_HD)),
        'cache_v': nrm(ks[4], (DEPTH, DEC_BATCH, n_win, SW_KV, SW_HD)),
        'c_prompt': nrm(ks[5], (BATCH, D_MODEL)),
        'c_sample': nrm(ks[6], (DEC_BATCH, D_MODEL)),
        'w_ada': nrm(ks[7], (DEPTH, D_MODEL, N_MOD * D_MODEL), D_MODEL ** -0.5),
        'b_ada': nrm(ks[8], (DEPTH, N_MOD * D_MODEL), 0.01),
        'g_ffn1': 1.0 + nrm(ks[9], (DEPTH, D_MODEL), 0.02),
        'w_up1': nrm(ks[10], (DEPTH, D_MODEL, 2 * D_FF), D_MODEL ** -0.5),
        'w_down1': nrm(ks[11], (DEPTH, D_FF, D_MODEL), D_FF ** -0.5),
        'g_mix': 1.0 + nrm(ks[12], (DEPTH, D_MODEL), 0.02),
        'w_in': nrm(ks[13], (DEPTH, D_MODEL, D_IN), D_MODEL ** -0.5),
        'lb_logits': nrm(ks[14], (DEPTH + 1, D_HG), 0.1),
        'g_hgrn': 1.0 + nrm(ks[15], (DEPTH, HG_DV), 0.02),
        'sinks': nrm(ks[16], (DEPTH, SW_HEADS), 0.5),
        'w_out': nrm(ks[17], (DEPTH, D_MODEL, D_MODEL), D_MODEL ** -0.5),
        'g_ffn2': 1.0 + nrm(ks[18], (DEPTH, D_MODEL), 0.02),
        'w_up2': nrm(ks[19], (DEPTH, D_MODEL, 2 * D_FF), D_MODEL ** -0.5),
        'w_down2': nrm(ks[20], (DEPTH, D_FF, D_MODEL), D_FF ** -0.5),
        'g_final': 1.0 + nrm(ks[21], (D_MODEL,), 0.02),
    }


def reference(x_prompt, x_sample, state_hgrn, cache_k, cache_v, c_prompt, c_sample,
              w_ada, b_ada, g_ffn1, w_up1, w_down1, g_mix, w_in, lb_logits, g_hgrn, sinks,
              w_out, g_ffn2, w_up2, w_down2, g_final):
    xp, xs = x_prompt, x_sample
    sp_list, kp_list, vp_list, ss_list, ks_list, vs_list = [], [], [], [], [], []
    for l in range(DEPTH):
        lb = hgrn_lower_bound(lb_logits, l)
        ffn = (w_ada[l], b_ada[l], g_ffn1[l], w_up1[l], w_down1[l], g_mix[l], g_ffn2[l], w_up2[l], w_down2[l])
        mix_p = lambda h: mixer_prompt(h, w_in[l], lb, g_hgrn[l], sinks[l], w_out[l])
        mix_s = lambda h: mixer_sample(h, state_hgrn[l], cache_k[l], cache_v[l], w_in[l], lb, g_hgrn[l], sinks[l], w_out[l])
        xp, sp, kp, vp = block(xp, c_prompt, mix_p, *ffn)
        xs, ss, kss, vss = block(xs, c_sample, mix_s, *ffn)
        sp_list.append(sp); kp_list.append(kp); vp_list.append(vp)
        ss_list.append(ss); ks_list.append(kss); vs_list.append(vss)
    y_prompt = rmsnorm(xp, g_final)
    y_sample = rmsnorm(xs, g_final)
    new_state_hgrn_prompt = jnp.stack(sp_list)
    new_cache_k_prompt = jnp.stack(kp_list)
    new_cache_v_prompt = jnp.stack(vp_list)
    new_state_hgrn_sample = jnp.stack(ss_list)
    new_cache_k_sample = jnp.stack(ks_list)
    new_cache_v_sample = jnp.stack(vs_list)
    return (y_prompt, y_sample, new_state_hgrn_prompt, new_cache_k_prompt, new_cache_v_prompt, new_state_hgrn_sample, new_cache_k_sample, new_cache_v_sample)
```

```python
import contextlib
import numpy as np
import concourse.bass as bass
import concourse.mybir as mybir
from concourse.bass_utils import run_bass_kernel_spmd

F32 = mybir.dt.float32
BF16 = mybir.dt.bfloat16
AF = mybir.ActivationFunctionType
ALU = mybir.AluOpType
AX = mybir.AxisListType

D = 1024
KC = 8
DFF = 2816
NHID = 22
EPS = 1e-6
NSLOT = 5
SLOTW = 4096
NEG = -30000.0
DBG = 0
import os as _os
SKIP = set(_os.environ.get("KSKIP", "").split(","))
KSTOP = int(_os.environ.get("KSTOP", "99"))
COARSE = [p for p in _os.environ.get("KCOARSE", "hT_").split(",") if p]

ENGS = ("pe", "act", "dve", "pool", "sp")


class Op:
    __slots__ = ("eng", "fn", "deps", "signal", "is_dma", "semkey", "count", "pos")

    def __init__(self, eng, fn, is_dma, semkey):
        self.eng = eng
        self.fn = fn
        self.deps = []
        self.signal = False
        self.is_dma = is_dma
        self.semkey = semkey
        self.count = 0


class _Rec:
    def __init__(self):
        self.calls = []

    def __getattr__(self, name):
        def f(*a, **k):
            self.calls.append((name, a, k))
            return self
        return f


class Sched:
    def __init__(self, same_engine_sync=True):
        self.ops = {e: [] for e in ENGS}
        self.last_w = {}
        self.readers = {}
        self.same_engine_sync = same_engine_sync
        self.dma_counts = {}

    def _m(self, name):
        import re
        for pat in COARSE:
            if pat and name.startswith(pat):
                if pat == "hT_":
                    return "_".join(name.split("_")[:2])
                return pat
        return name

    def add(self, eng, fn, reads=(), writes=(), dma=False, semkey=None):
        if COARSE:
            reads = [self._m(r) for r in reads]
            writes = [self._m(w) for w in writes]
        rec = _Rec()
        fn(rec)
        assert len(rec.calls) == 1
        op = Op(eng, rec.calls[0], dma, semkey)
        deps = []
        for r in reads:
            w = self.last_w.get(r)
            if w is not None:
                deps.append(w)
        for w_ in writes:
            w = self.last_w.get(w_)
            if w is not None:
                deps.append(w)
            deps.extend(self.readers.get(w_, ()))
        seen = set()
        latest = {}
        for d in deps:
            if id(d) in seen:
                continue
            seen.add(id(d))
            if d.is_dma:
                op.deps.append(d)
                continue
            if d.eng == eng and (eng == "pe" or not self.same_engine_sync):
                continue
            cur = latest.get(d.eng)
            if cur is None or d.pos > cur.pos:
                latest[d.eng] = d
        op.deps.extend(latest.values())
        for r in reads:
            self.readers.setdefault(r, []).append(op)
        for w_ in writes:
            self.last_w[w_] = op
            self.readers[w_] = []
        op.pos = len(self.ops[eng])
        self.ops[eng].append(op)
        if dma:
            c = self.dma_counts.get(semkey, 0) + 16
            self.dma_counts[semkey] = c
            op.count = c
        return op

    def fence(self, old, new):
        ops = []
        seen = set()
        for o in old:
            w = self.last_w.get(o)
            cand = ([w] if w is not None else []) + list(self.readers.get(o, ()))
            for c in cand:
                if id(c) not in seen:
                    seen.add(id(c))
                    ops.append(c)
        for n in new:
            self.last_w.pop(n, None)
            self.readers[n] = list(ops)

    def emit(self, nc, final_wait_eng="pool"):
        for e in ENGS:
            for op in self.ops[e]:
                for d in op.deps:
                    d.signal = True
        last_dma = {}
        for e in ENGS:
            c = 0
            for op in self.ops[e]:
                if op.is_dma:
                    last_dma[op.semkey] = op
                elif op.signal:
                    c += 1
                    op.count = c
        with contextlib.ExitStack() as stack:
            esem = {e: stack.enter_context(nc.semaphore("s_" + e)) for e in ENGS}
            dsem = {}
            for i, k in enumerate(self.dma_counts):
                dsem[k] = stack.enter_context(nc.semaphore("d%d" % i))
            block = stack.enter_context(nc.Block())

            def run(e, engine):
                seen_e = {}
                seen_d = {}
                for op in self.ops[e]:
                    need_d, need_e = {}, {}
                    for d in op.deps:
                        if d.is_dma:
                            need_d[d.semkey] = max(need_d.get(d.semkey, 0), d.count)
                        else:
                            need_e[d.eng] = max(need_e.get(d.eng, 0), d.count)
                    for k, c in need_d.items():
                        if seen_d.get(k, 0) < c:
                            seen_d[k] = c
                            engine.wait_ge(dsem[k], c)
                    for k, c in need_e.items():
                        if seen_e.get(k, 0) < c:
                            seen_e[k] = c
                            engine.wait_ge(esem[k], c)
                    name_, a_, k_ = op.fn
                    ins = getattr(engine, name_)(*a_, **k_)
                    if op.is_dma:
                        ins.then_inc(dsem[op.semkey], 16)
                    elif op.signal:
                        ins.then_inc(esem[e], 1)
                if e == final_wait_eng:
                    for k, op in last_dma.items():
                        if seen_d.get(k, 0) < op.count:
                            engine.wait_ge(dsem[k], op.count)

            @block.tensor
            def _(eng):
                run("pe", eng)

            @block.scalar
            def _(eng):
                run("act", eng)

            @block.vector
            def _(eng):
                run("dve", eng)

            @block.gpsimd
            def _(eng):
                run("pool", eng)

            @block.sync
            def _(eng):
                run("sp", eng)


def host_constants():
    ident = np.eye(128, dtype=np.float32)
    s = np.arange(128)
    same = (s[:, None] // 64) == (s[None, :] // 64)
    triU = (same & (s[:, None] <= s[None, :])).astype(np.float32)
    triR = (same & (s[:, None] > s[None, :])).astype(np.float32)
    chunkind = np.zeros((128, 2), np.float32)
    chunkind[:64, 0] = 1.0
    chunkind[64:, 1] = 1.0
    maskA4 = np.tile(triU, (1, 4)).astype(np.float32)
    le = (s[:, None] <= s[None, :])
    triM = (le.astype(np.float32) - (s[:, None] <= 63).astype(np.float32) * np.ones((1, 128), np.float32)).astype(np.float32)
    triRf = (s[:, None] > s[None, :]).astype(np.float32)
    maskAf4 = np.tile(le.astype(np.float32), (1, 4)).astype(np.float32)
    ind2 = np.stack([(s < 64).astype(np.float32), np.ones(128, np.float32)], axis=1)
    slopes = np.exp2(-8.0 * np.arange(1, 9, dtype=np.float32) / 8.0).astype(np.float32)
    i = np.arange(128)
    j = np.arange(256)
    dist = np.abs(i[:, None] - (j[None, :] - 128)).astype(np.float32)
    qc = i[:, None] // 64
    kc = j[None, :] // 64 - 2
    vis = (kc >= qc - 2) & (kc <= qc)
    tabP = np.where(vis[:, None, :], -slopes[None, :, None] * dist[:, None, :], NEG).astype(np.float32)
    j = np.arange(384)
    qpos = 128 + (i % 64)
    kpos = np.where(j < 128, j, np.where(j < 256, j - 128, 128 + ((j - 256) % 64)))
    kseq = np.where(j < 128, 0, np.where(j < 256, 1, (j - 256) // 64))
    qseq = i // 64
    vis = qseq[:, None] == kseq[None, :]
    dist = np.abs(qpos[:, None] - kpos[None, :]).astype(np.float32)
    tabS = np.where(vis[:, None, :], -slopes[None, :, None] * dist[:, None, :], NEG).astype(np.float32)
    tabP3 = np.full((128, 8, 384), NEG, np.float32)
    tabP3[:, :, :256] = tabP
    return dict(c_triM=triM, c_triRf=triRf, c_maskAf4=maskAf4, c_ind2=ind2, c_ident=ident, c_triU=triU, c_triR=triR, c_chunkind=chunkind, c_maskA4=maskA4,
                c_tabP=np.ascontiguousarray(tabP3), c_tabS=np.ascontiguousarray(tabS))


def build_program(NP, SEQ, NS):
    assert SEQ % 512 == 0 and NS % 2 == 0
    NSEQ = NP + NS
    nc = bass.Bass("TRN2", target_bir_lowering=False)

    def din(name, shape):
        return nc.dram_tensor(name, list(shape), F32, kind="ExternalInput").ap()

    def dout(name, shape):
        return nc.dram_tensor(name, list(shape), F32, kind="ExternalOutput").ap()

    xp = din("xp", [NP, SEQ, D])
    xs = din("xs", [NS * 64, D])
    st_in = din("st_in", [NS, 4, 128, 128])
    ck_in = din("ck_in", [NS, 128, 128])
    cv_in = din("cv_in", [NS, 128, 128])
    c_in = din("c_in", [NSEQ, D])
    w_ada = din("w_ada", [D, 9 * D])
    b_ada = din("b_ada", [72, 128])
    g_norms = din("g_norms", [24, 128])
    w_up = [din("w_up1", [D, 2 * DFF]), din("w_up2", [D, 2 * DFF])]
    w_down = [din("w_down1", [DFF, D]), din("w_down2", [DFF, D])]
    w_in = din("w_in", [D, 2816])
    lb_logits = din("lb_logits", [2, 512])
    g_hgrn = din("g_hgrn", [1, 128])
    sinks = din("sinks", [1, 8])
    w_out = din("w_out", [D, D])
    g_final = din("g_final", [1, D])
    c_ident = din("c_ident", [128, 128])
    c_triU = din("c_triU", [128, 128])
    c_triR = din("c_triR", [128, 128])
    c_chunkind = din("c_chunkind", [128, 2])
    c_maskA4 = din("c_maskA4", [128, 512])
    c_triM = din("c_triM", [128, 128])
    c_triRf = din("c_triRf", [128, 128])
    c_maskAf4 = din("c_maskAf4", [128, 512])
    c_ind2 = din("c_ind2", [128, 2])
    c_tabP = din("c_tabP", [128, 8, 384])
    c_tabS = din("c_tabS", [128, 8, 384])

    yp = dout("yp", [NP, SEQ, D])
    ys = dout("ys", [NS * 64, D])
    o_sp = dout("o_sp", [NP, 4, 128, 128])
    o_kp = dout("o_kp", [NP, 128, 128])
    o_vp = dout("o_vp", [NP, 128, 128])
    o_ss = dout("o_ss", [NS, 4, 128, 128])
    o_ks = dout("o_ks", [NS, 128, 128])
    o_vs = dout("o_vs", [NS, 128, 128])

    PIECES = []
    for f in range(2):
        if f == 1:
            PIECES.append(("inQ", 0, 2048, 512))
            PIECES.append(("inK", 0, 2560, 256))
            for i, (c0, n) in enumerate([(0, 512), (512, 512), (1024, 512), (1536, 512), (2560, 256)]):
                PIECES.append(("inT", i, c0, n))
            PIECES.append(("out", 0, 0, 512))
            PIECES.append(("out", 1, 512, 512))
        for p in range(11):
            PIECES.append(("up%d" % f, p, 0, 512))
        for half in range(2):
            for kg, (k0, nk) in enumerate([(0, 8), (8, 8), (16, 6)]):
                PIECES.append(("dn%d" % f, half * 3 + kg, k0, nk))
    NPIECE = len(PIECES)
    scr = nc.dram_tensor("scr", [NPIECE, 128, SLOTW], BF16, kind="Internal").ap()
    modD = nc.dram_tensor("modD", [NSEQ, 3, D], F32, kind="Internal").ap()

    S = Sched()
    st = contextlib.ExitStack()
    with st:
        def sb(name, shape, dt=F32):
            return st.enter_context(nc.sbuf_tensor(name, list(shape), dt))

        PB = [st.enter_context(nc.psum_tensor("pb%d" % i, [128, 512], F32)) for i in range(8)]

        def pbn(i):
            return "pb%d" % i

        XT = [sb("xt%d" % i, [128, 4, D]) for i in range(2)]
        WR = [sb("wr%d" % i, [128, SLOTW], BF16) for i in range(NSLOT)]
        hT = sb("hT", [128, 8, 512], BF16)
        yT = sb("yT", [128, 8, 512], BF16)
        hid = sb("hid", [128, NHID, 512], BF16)
        GT = sb("gt", [128, 4, D])
        TAB = sb("tab", [128, 8, 384])
        id_f = sb("id_f", [128, 128])
        id_b = sb("id_b", [128, 128], BF16)
        triU = sb("triU", [128, 128])
        triR = sb("triR", [128, 128])
        chunkind = sb("chunkind", [128, 2])
        maskA4 = sb("maskA4", [128, 512])
        triM = sb("triM", [128, 128])
        triRf = sb("triRf", [128, 128])
        maskAf4 = sb("maskAf4", [128, 512])
        ind2 = sb("ind2", [128, 2])
        oml = sb("oml", [128, 512])
        G4 = sb("G4", [128, 512])
        gfin = sb("gfin", [128, D])
        sinkb = sb("sinkb", [128, 8])
        gT_in = sb("gT_in", [24, 128])
        gfm = sb("gfm", [128, 24])
        bT_in = sb("bT_in", [72, 128])
        bT = sb("bT", [128, 72])
        scT = sb("scT", [128, 8, NSEQ], BF16)
        modFM = sb("modFM", [128, 72, NSEQ])
        GS = sb("GS", [128, 3, 8, NSEQ])
        epsb = sb("epsb", [128, 1])
        stat = sb("stat", [128, 8])
        xn4 = sb("xn4", [128, 4, D], BF16)
        statN = sb("statN", [128, 16])
        PTF = xn4[:, 0:2, :].rearrange("p a b -> p (a b)")
        junk = None
        hidF = hid[:].rearrange("p c n -> p (c n)").bitcast(F32)
        gateTM = hidF[0:NSEQ, 0:3072].rearrange("p (a b) -> p a b", a=3)
        c6 = hidF[0:NSEQ, 3072:4096]
        sc6 = hidF[0:NSEQ, 4096:5120]
        R_gateTM = ["hid%d" % c for c in range(0, 12)]
        R_c6 = ["hid%d" % c for c in range(12, 16)]
        R_sc6 = ["hid%d" % c for c in range(16, 20)]
        lbt = XT[1][:, 0, :].rearrange("p (a b) -> p a b", a=2)
        scsF = hidF[:, 0:2048]
        QT = hidF[:, 2048:3072].bitcast(BF16).rearrange("p (a b) -> p a b", a=4)
        hq = hidF[:, 3072:3584]
        hk = hidF[:, 3584:4096]
        hg = hidF[:, 4096:4608]
        pexF = hidF[:, 4608:5632].bitcast(BF16)
        HIDN = ["hid%d" % c for c in range(NHID)] + ["xn0", "xn1"]
        MIXN = ["scs%d" % q_ for q_ in range(8)] + ["QT", "hq", "hk", "hg"] + ["pexp%d" % q_ for q_ in range(8)] + ["PT%d" % q_ for q_ in range(8)]
        sg = [sb("sg%d" % i, [128, 512]) for i in range(2)]
        junk = sg[0][:].bitcast(BF16)
        tmpE = [sb("tmpE%d" % i, [128, 512]) for i in range(2)]
        ex = [sb("ex%d" % i, [128, 512]) for i in range(2)]
        v_b = sb("v_b", [128, 512], BF16)
        Gg = sb("Gg", [128, 512])
        qt_b = sb("qt_b", [128, 512], BF16)
        kt_b = sb("kt_b", [128, 512], BF16)
        kh_b = sb("kh_b", [128, 512], BF16)
        qTz = [sb("qTz%d" % i, [128, 4, 128], BF16) for i in range(2)]
        kT = sb("kT", [128, 4, 128], BF16)
        AT = sb("AT", [128, 4, 128], BF16)
        ebl = sb("ebl", [128, 8])
        Sf = [sb("Sf%d" % i, [128, 4, 128]) for i in range(2)]
        Sb = [sb("Sb%d" % i, [128, 4, 128], BF16) for i in range(2)]
        hst = sb("hst", [128, 8])
        yhg = sb("yhg", [128, 512], BF16)
        KTn = sb("KTn", [128, 5, 128], BF16)
        KTs = sb("KTs", [128, 5, 128], BF16)
        Vb = sb("Vb", [128, 5, 128], BF16)
        KSn = sb("KSn", [128, 3, 128], BF16)
        KSs = sb("KSs", [128, 3, 128], BF16)
        Vc = sb("Vc", [128, 2, 128], BF16)
        ck_f = sb("ck_f", [128, 2, 2, 128])
        cv_f = sb("cv_f", [128, 2, 128])
        kv_f = sb("kv_f", [128, 256])
        ast = sb("ast", [128, 32])
        ysw = sb("ysw", [128, 512], BF16)

        add = S.add
        OB = [5, 0, 1, 2]

        def ld(eng, dst, src, res, key):
            add(eng, lambda e: e.dma_start(out=dst, in_=src), writes=[res], dma=True, semkey=key)

        ld("act", id_f[:], c_ident, "id_f", "c0")
        ld("act", triU[:], c_triU, "triU", "c1")
        ld("act", triR[:], c_triR, "triR", "c2")
        ld("act", chunkind[:], c_chunkind, "chunkind", "c3")
        ld("act", maskA4[:], c_maskA4, "maskA4", "c4")
        ld("act", triM[:], c_triM, "triM", "c12")
        ld("act", triRf[:], c_triRf, "triRf", "c13")
        ld("act", maskAf4[:], c_maskAf4, "maskAf4", "c14")
        ld("act", ind2[:], c_ind2, "ind2", "c15")
        ld("act", lbt, lb_logits.partition_broadcast(128), "xt1_0", "c5")
        for h in range(4):
            ld("act", G4[:, h * 128:(h + 1) * 128], g_hgrn[0].partition_broadcast(128), "G4_%d" % h, "c6")
        ld("act", gfin[:], g_final[0].partition_broadcast(128), "gfin", "c7")
        ld("act", sinkb[:], sinks[0].partition_broadcast(128), "sinkb", "c8")
        ld("act", gT_in[:], g_norms, "gT_in", "c9")
        ld("act", bT_in[:], b_ada, "bT_in", "c10")
        add("act", lambda e: e.dma_start(out=c6, in_=c_in), writes=R_c6, dma=True, semkey="c11")
        G4r = ["G4_%d" % h for h in range(4)]

        add("dve", lambda e: e.tensor_copy(out=id_b[:], in_=id_f[:]), reads=["id_f"], writes=["id_b"])
        add("dve", lambda e: e.memset(epsb[:], EPS), writes=["epsb"])
        for i in range(2):
            add("pool", lambda e, i=i: e.memset(qTz[i][:], 0.0), writes=["qTz%d" % i])
        add("dve", lambda e: e.tensor_sub(out=oml[:], in0=lbt[:, 1, :], in1=lbt[:, 0, :]), reads=["xt1_0"], writes=["oml"])
        add("act", lambda e: e.activation(out=oml[:], in_=oml[:], func=AF.Sigmoid), reads=["oml"], writes=["oml"])
        add("pe", lambda e: e.matmul(PB[0][:, 0:24], lhsT=gT_in[0:24, :], rhs=id_f[0:24, 0:24], start=True, stop=True),
            reads=["gT_in", "id_f"], writes=[pbn(0)])
        add("dve", lambda e: e.tensor_copy(out=gfm[:], in_=PB[0][:, 0:24]), reads=[pbn(0)], writes=["gfm"])
        add("pe", lambda e: e.matmul(PB[1][:, 0:72], lhsT=bT_in[0:72, :], rhs=id_f[0:72, 0:72], start=True, stop=True),
            reads=["bT_in", "id_f"], writes=[pbn(1)])
        add("dve", lambda e: e.tensor_copy(out=bT[:], in_=PB[1][:, 0:72]), reads=[pbn(1)], writes=["bT"])
        add("act", lambda e: e.activation(out=sc6, in_=c6, func=AF.Silu), reads=R_c6, writes=R_sc6)
        for kc in range(8):
            add("pe", lambda e, kc=kc: e.matmul(PB[2][:, kc * NSEQ:(kc + 1) * NSEQ], lhsT=sc6[0:NSEQ, kc * 128:(kc + 1) * 128],
                                                 rhs=id_f[0:NSEQ, 0:NSEQ], start=True, stop=True),
                reads=R_sc6 + ["id_f"], writes=[pbn(2)])
        add("dve", lambda e: e.tensor_copy(out=scT[:].rearrange("p k s -> p (k s)"), in_=PB[2][:, 0:8 * NSEQ]),
            reads=[pbn(2)], writes=["scT"])

        tiles = []
        for s_ in range(NP):
            for t in range(SEQ // 512):
                tiles.append(dict(kind="p", seq=s_, t=t, ntc=4))
        tiles.append(dict(kind="s", ntc=NS // 2))
        stream = [("ada", blk) for blk in range(18)]
        for ti in range(len(tiles)):
            for pi in range(NPIECE):
                stream.append((ti, pi))
        issued = [0]

        def slot_res(sl):
            return ["wr%d_a" % sl, "wr%d_b" % sl, "wr%d_c" % sl]

        def issue_one():
            g = issued[0]
            if g >= len(stream):
                return
            issued[0] += 1
            sl = g % NSLOT
            key = "w%d" % sl
            res = slot_res(sl)
            a, b = stream[g]
            if a == "ada":
                src = w_ada.rearrange("(kc p) c -> p kc c", p=128)[:, :, b * 512:(b + 1) * 512]
                dst = WR[sl][:].rearrange("p (k c) -> p k c", k=8)
                add("pool", lambda e: e.dma_start(out=dst, in_=src), writes=res, dma=True, semkey=key)
                return
            ti, pi = a, b
            name, idx, c0, n = PIECES[pi]
            if ti > 0:
                add("sp", lambda e: e.dma_start(out=WR[sl][:], in_=scr[pi]), reads=["scr%d" % pi], writes=res, dma=True, semkey="ws%d" % sl)
                return
            if name.startswith("up"):
                W = w_up[int(name[2])].rearrange("(kc p) c -> p kc c", p=128)
                dst = WR[sl][:].rearrange("p (k c) -> p k c", k=8)
                add("pool", lambda e: e.dma_start(out=dst[:, :, 0:256], in_=W[:, :, idx * 256:(idx + 1) * 256]),
                    writes=[res[0]], dma=True, semkey=key)
                add("pool", lambda e: e.dma_start(out=dst[:, :, 256:512], in_=W[:, :, DFF + idx * 256:DFF + (idx + 1) * 256]),
                    writes=[res[1]], dma=True, semkey=key + "b")
            elif name.startswith("dn"):
                half = idx // 3
                k0, nk = c0, n
                W = w_down[int(name[2])].rearrange("(kc p) c -> p kc c", p=128)
                dst = WR[sl][:, 0:nk * 512].rearrange("p (k c) -> p k c", k=nk)
                add("pool", lambda e: e.dma_start(out=dst, in_=W[:, k0:k0 + nk, half * 512:(half + 1) * 512]),
                    writes=res, dma=True, semkey=key)
            elif name == "inK":
                W = w_in.rearrange("(kc p) c -> p kc c", p=128)
                dst = WR[sl][:, 0:8 * 256].rearrange("p (k c) -> p k c", k=8)
                add("pool", lambda e: e.dma_start(out=dst[:, :, 0:128], in_=W[:, :, 2560:2688]), writes=[res[0]], dma=True, semkey=key)
                add("pool", lambda e: e.dma_start(out=dst[:, :, 128:192], in_=W[:, :, 2624:2688]), writes=[res[1]], dma=True, semkey=key + "b")
                add("pool", lambda e: e.dma_start(out=dst[:, :, 192:256], in_=W[:, :, 2560:2624]), writes=[res[2]], dma=True, semkey=key + "c")
            else:
                W = (w_out if name == "out" else w_in).rearrange("(kc p) c -> p kc c", p=128)
                dst = WR[sl][:, 0:8 * n].rearrange("p (k c) -> p k c", k=8)
                add("pool", lambda e: e.dma_start(out=dst, in_=W[:, :, c0:c0 + n]), writes=res, dma=True, semkey=key)
            add("sp", lambda e: e.dma_start(out=scr[pi], in_=WR[sl][:]), reads=res, writes=["scr%d" % pi], dma=True, semkey="scw%d" % sl)

        consumed = [0]

        def next_piece(lookahead=NSLOT):
            g = consumed[0]
            consumed[0] += 1
            while issued[0] < min(len(stream), g + lookahead):
                issue_one()
            sl = g % NSLOT
            return WR[sl], slot_res(sl), stream[g]

        def x_src(tile, tc):
            if tile["kind"] == "p":
                r0 = tile["t"] * 512 + tc * 128
                return xp[tile["seq"], r0:r0 + 128, :]
            return xs[tc * 128:(tc + 1) * 128, :]

        def y_dst(tile, tc):
            if tile["kind"] == "p":
                r0 = tile["t"] * 512 + tc * 128
                return yp[tile["seq"], r0:r0 + 128, :]
            return ys[tc * 128:(tc + 1) * 128, :]

        def load_x(ti):
            tile = tiles[ti]
            slot = ti % 2
            for tc in range(tile["ntc"]):
                add("pool", lambda e, tc=tc: e.dma_start(out=XT[slot][:, tc, :], in_=x_src(tile, tc)),
                    writes=["xt%d_%d" % (slot, tc)], dma=True, semkey="x%d_%d" % (slot, tc))

        load_x(0)

        for blk in range(18):
            wt, wres, _ = next_piece()
            wv = wt[:].rearrange("p (k c) -> p k c", k=8)
            pbk = 4 + (blk % 2)
            for jj in range(4):
                for kc in range(8):
                    add("pe", lambda e, jj=jj, kc=kc, wv=wv, pbk=pbk: e.matmul(
                        PB[pbk][:, jj * NSEQ:(jj + 1) * NSEQ], lhsT=wv[:, kc, jj * 128:(jj + 1) * 128], rhs=scT[:, kc, :],
                        start=(kc == 0), stop=(kc == 7)), reads=wres + ["scT"], writes=[pbn(pbk)])
            for jj in range(4):
                jc = blk * 4 + jj
                add("dve", lambda e, jj=jj, jc=jc, pbk=pbk: e.tensor_scalar(
                    out=modFM[:, jc, :], in0=PB[pbk][:, jj * NSEQ:(jj + 1) * NSEQ], scalar1=bT[:, jc:jc + 1], scalar2=None, op0=ALU.add),
                    reads=[pbn(pbk), "bT"], writes=["modFM%d" % jc])
        for n in range(3):
            for kc in range(8):
                jc = (3 * n + 1) * 8 + kc
                add("dve", lambda e, n=n, kc=kc, jc=jc: e.tensor_scalar(
                    out=GS[:, n, kc, :], in0=modFM[:, jc, :], scalar1=1.0, scalar2=gfm[:, n * 8 + kc:n * 8 + kc + 1],
                    op0=ALU.add, op1=ALU.mult), reads=["modFM%d" % jc, "gfm"], writes=["GS%d_%d" % (n, kc)])
        for n in range(3):
            for kc in range(8):
                jc = (3 * n + 2) * 8 + kc
                pbk = 6 + (n % 2)
                add("pe", lambda e, n=n, kc=kc, jc=jc, pbk=pbk: e.matmul(
                    PB[pbk][0:NSEQ, kc * 128:(kc + 1) * 128] if kc < 4 else PB[pbk][0:NSEQ, (kc - 4) * 128:(kc - 3) * 128],
                    lhsT=modFM[:, jc, :], rhs=id_f[:], start=True, stop=True), reads=["modFM%d" % jc, "id_f"], writes=[pbn(pbk)])
                if kc == 3 or kc == 7:
                    half = kc // 4
                    add("dve", lambda e, n=n, half=half, pbk=pbk: e.tensor_copy(
                        out=gateTM[:, n, half * 512:(half + 1) * 512], in_=PB[pbk][0:NSEQ, :]), reads=[pbn(pbk)], writes=R_gateTM)
        add("pool", lambda e: e.dma_start(out=modD, in_=gateTM), reads=R_gateTM, writes=["modD"], dma=True, semkey="modD")

        def tc_segs(tile, tc):
            if tile["kind"] == "p":
                return [(0, 128, tile["seq"])]
            return [(0, 64, NP + 2 * tc), (64, 128, NP + 2 * tc + 1)]

        gt_rr = [0]

        def load_gate(tile, n, scale_half):
            outs = []
            if tile["kind"] == "p":
                k = gt_rr[0] % 4
                gt_rr[0] += 1
                res = "gt%d" % k
                sq = tile["seq"]
                add("act", lambda e: e.dma_start(out=GT[:, k, :], in_=modD[sq, n, :].partition_broadcast(128)),
                    reads=["modD"], writes=[res], dma=True, semkey="g%d" % k)
                return [(GT[:, k, :], [res])] * tile["ntc"]
            for tc in range(tile["ntc"]):
                k = gt_rr[0] % 4
                gt_rr[0] += 1
                res = "gt%d" % k
                for (p0, p1, sq) in tc_segs(tile, tc):
                    add("act", lambda e, p0=p0, p1=p1, sq=sq, k=k: e.dma_start(
                        out=GT[p0:p1, k, :], in_=modD[sq, n, :].partition_broadcast(64)),
                        reads=["modD"], writes=[res + "_%d" % p0], dma=True, semkey="g%d" % k)
                outs.append((GT[:, k, :], [res + "_0", res + "_64"]))
            return outs

        tr_rr = [0]

        def norm_to_hT(tile, slot, n):
            ntc = tile["ntc"]
            for tc in range(ntc):
                xr = "xt%d_%d" % (slot, tc)
                xa = XT[slot][:, tc, :]
                c0 = 3 * tc
                sr = "statN%d" % tc
                add("dve", lambda e: e.memset(statN[:, c0:c0 + 1], 0.0), writes=[sr])
                add("act", lambda e: e.activation(out=junk[:], in_=xa, func=AF.Square, accum_out=statN[:, c0:c0 + 1]),
                    reads=[xr, sr], writes=[sr] + (["sg0"] if tc == 0 else []))
                add("act", lambda e: e.activation(out=statN[:, c0 + 1:c0 + 2], in_=statN[:, c0:c0 + 1], func=AF.Ln, scale=1.0 / D, bias=epsb[:, 0:1]),
                    reads=[sr, "epsb"], writes=[sr])
                add("act", lambda e: e.activation(out=statN[:, c0 + 2:c0 + 3], in_=statN[:, c0 + 1:c0 + 2], func=AF.Exp, scale=-0.5), reads=[sr], writes=[sr])
            for tc in range(ntc):
                xr = "xt%d_%d" % (slot, tc)
                xa = XT[slot][:, tc, :]
                c0 = 3 * tc
                add("dve", lambda e: e.tensor_scalar(out=xn4[:, tc, :], in0=xa, scalar1=statN[:, c0 + 2:c0 + 3], scalar2=None, op0=ALU.mult),
                    reads=[xr, "statN%d" % tc], writes=["xn%d" % tc])
            for tc in range(ntc):
                xnr = "xn%d" % tc
                pbk = tr_rr[0] % 2
                tr_rr[0] += 1
                pv = PB[pbk][:].bitcast(BF16).rearrange("p (k t) -> p k t", k=8)
                for kc in range(8):
                    add("pe", lambda e: e.transpose(out=pv[:, kc, :], in_=xn4[:, tc, kc * 128:(kc + 1) * 128], identity=id_b[:]),
                        reads=[xnr, "id_b"], writes=[pbn(pbk)])
                for kc in range(8):
                    jc = (3 * n) * 8 + kc
                    for (p0, p1, sq) in tc_segs(tile, tc):
                        hr_ = "hT_%d_%d_%d" % (tc, kc, p0)
                        if kc % 4 != 3:
                            add("dve", lambda e: e.tensor_scalar(
                                out=hT[:, kc, tc * 128 + p0:tc * 128 + p1], in0=pv[:, kc, p0:p1],
                                scalar1=GS[:, n, kc, sq:sq + 1], scalar2=modFM[:, jc, sq:sq + 1], op0=ALU.mult, op1=ALU.add),
                                reads=[pbn(pbk), "GS%d_%d" % (n, kc), "modFM%d" % jc], writes=[hr_])
                        else:
                            add("act", lambda e: e.activation(
                                out=hT[:, kc, tc * 128 + p0:tc * 128 + p1], in_=pv[:, kc, p0:p1], func=AF.Identity,
                                scale=GS[:, n, kc, sq:sq + 1], bias=modFM[:, jc, sq:sq + 1]),
                                reads=[pbn(pbk), "GS%d_%d" % (n, kc), "modFM%d" % jc], writes=[hr_])

        def hT_tc(tile, tc):
            return ["hT_%d_%d_%d" % (tc, kc, p0) for kc in range(8) for (p0, p1, sq) in tc_segs(tile, tc)]

        def hT_res(tile):
            r = []
            for tc in range(tile["ntc"]):
                r += hT_tc(tile, tc)
            return r

        def ffn(tile, slot, f, gates):
            NT = tile["ntc"] * 128
            ntc = tile["ntc"]
            hres = hT_res(tile)
            for p in range(11):
                wt, wres, _ = next_piece()
                wv = wt[:].rearrange("p (k c) -> p k c", k=8)
                for q in range(2):
                    c = 2 * p + q
                    bg, bu = (0, 1) if c % 2 == 0 else (2, 3)
                    for (bk, col) in ((bg, q * 128), (bu, 256 + q * 128)):
                        for kc in range(8):
                            add("pe", lambda e, bk=bk, col=col, kc=kc, wv=wv: e.matmul(
                                PB[bk][:, 0:NT], lhsT=wv[:, kc, col:col + 128], rhs=hT[:, kc, 0:NT], start=(kc == 0), stop=(kc == 7)),
                                reads=wres + hres, writes=[pbn(bk)])
                    sgi = c % 2
                    add("act", lambda e, bg=bg, sgi=sgi: e.activation(out=sg[sgi][:, 0:NT], in_=PB[bg][:, 0:NT], func=AF.Silu),
                        reads=[pbn(bg)], writes=["sg%d" % sgi])
                    add("dve", lambda e, bu=bu, sgi=sgi, c=c: e.tensor_tensor(out=hid[:, c, 0:NT], in0=PB[bu][:, 0:NT], in1=sg[sgi][:, 0:NT], op=ALU.mult),
                        reads=[pbn(bu), "sg%d" % sgi], writes=["hid%d" % c])
            for half in range(2):
                for kg, (k0, nk) in enumerate([(0, 8), (8, 8), (16, 6)]):
                    wt, wres, _ = next_piece()
                    wv = wt[:, 0:nk * 512].rearrange("p (k c) -> p k c", k=nk)
                    for tc in range(ntc):
                        for k in range(nk):
                            kk = k0 + k
                            add("pe", lambda e, tc=tc, k=k, kk=kk, wv=wv: e.matmul(
                                PB[4 + tc][:, :], lhsT=hid[:, kk, tc * 128:(tc + 1) * 128], rhs=wv[:, k, :],
                                start=(kk == 0), stop=(kk == NHID - 1)), reads=wres + ["hid%d" % kk], writes=[pbn(4 + tc)])
                for tc in range(ntc):
                    ga, gres = gates[tc]
                    xr = "xt%d_%d" % (slot, tc)
                    ti_ = tc % 2
                    add("dve", lambda e, tc=tc, ga=ga, half=half, ti_=ti_: e.scalar_tensor_tensor(
                        out=tmpE[ti_][:], in0=PB[4 + tc][:], scalar=0.5, in1=ga[:, half * 512:(half + 1) * 512], op0=ALU.mult, op1=ALU.mult),
                        reads=[pbn(4 + tc)] + gres, writes=["tmpE%d" % ti_])
                    add("pool", lambda e, tc=tc, half=half, ti_=ti_: e.tensor_tensor(
                        out=XT[slot][:, tc, half * 512:(half + 1) * 512], in0=XT[slot][:, tc, half * 512:(half + 1) * 512],
                        in1=tmpE[ti_][:], op=ALU.add), reads=["tmpE%d" % ti_, xr], writes=[xr])

        def mixer(tile, slot, gates, ti):
            ntc = tile["ntc"]
            NT = ntc * 128
            hres = hT_res(tile)
            is_p = tile["kind"] == "p"
            S.fence(HIDN, MIXN)
            if is_p:
                Mq, Mc, Mk, IND = triM, triRf, maskAf4, ind2
                Mres = ["triM", "triRf", "maskAf4", "ind2"]
            else:
                Mq, Mc, Mk, IND = triU, triR, maskA4, chunkind
                Mres = ["triU", "triR", "maskA4", "chunkind"]
            wq_t, wq_res, _ = next_piece()
            wqv = wq_t[:].rearrange("p (k c) -> p k c", k=8)
            for c in range(4):
                bk = c % 2
                for kc in range(8):
                    add("pe", lambda e: e.matmul(PB[bk][:, 0:NT], lhsT=wqv[:, kc, c * 128:(c + 1) * 128], rhs=hT[:, kc, 0:NT],
                                                 start=(kc == 0), stop=(kc == 7)), reads=wq_res + hres, writes=[pbn(bk)])
                add("act", lambda e: e.copy(out=QT[:, c, 0:NT], in_=PB[bk][:, 0:NT]), reads=[pbn(bk)], writes=["QT"])
            wk_t, wk_res, _ = next_piece()
            wkv = wk_t[:, 0:2048].rearrange("p (k c) -> p k c", k=8)
            for v_ in range(2):
                bk = 2 + v_
                for kc in range(8):
                    add("pe", lambda e: e.matmul(PB[bk][:, 0:NT], lhsT=wkv[:, kc, v_ * 128:(v_ + 1) * 128], rhs=hT[:, kc, 0:NT],
                                                 start=(kc == 0), stop=(kc == 7)), reads=wk_res + hres, writes=[pbn(bk)])
                dstK = (KTn if v_ == 0 else KTs)[:, 1:1 + ntc, :]
                add("act", lambda e: e.copy(out=dstK, in_=PB[bk][:, 0:NT].rearrange("p (a b) -> p a b", a=ntc)),
                    reads=[pbn(bk)], writes=["KTn" if v_ == 0 else "KTs"])
            tmw = [next_piece(lookahead=5 - i) for i in range(5)]

            def tm_proj(tc, i5, bank):
                wt, wres, _ = tmw[i5]
                n = 512 if i5 < 4 else 256
                wv = wt[:, 0:8 * n].rearrange("p (k c) -> p k c", k=8)
                for kc in range(8):
                    add("pe", lambda e: e.matmul(PB[bank][:, 0:n], lhsT=hT[:, kc, tc * 128:(tc + 1) * 128], rhs=wv[:, kc, :],
                                                 start=(kc == 0), stop=(kc == 7)), reads=wres + hT_tc(tile, tc), writes=[pbn(bank)])

            def hgrn_chain(tc):
                tsl = slice(tc * 128, (tc + 1) * 128)
                tm_proj(tc, 0, 4)
                tm_proj(tc, 1, 5)
                tm_proj(tc, 2, 6)
                tm_proj(tc, 3, 7)
                add("act", lambda e: e.activation(out=hq[:], in_=PB[4][:], func=AF.Silu), reads=[pbn(4)], writes=["hq"])
                add("act", lambda e: e.activation(out=Gg[:], in_=PB[7][:], func=AF.Silu), reads=[pbn(7)], writes=["Gg"])
                add("act", lambda e: e.activation(out=hk[:], in_=PB[5][:], func=AF.Sigmoid, scale=-1.0), reads=[pbn(5)], writes=["hk"])
                add("dve", lambda e: e.tensor_copy(out=v_b[:], in_=PB[6][:]), reads=[pbn(6)], writes=["v_b"])
                add("dve", lambda e: e.tensor_tensor(out=hk[:], in0=hk[:], in1=oml[:], op=ALU.mult), reads=["hk", "oml"], writes=["hk"])
                add("act", lambda e: e.activation(out=hg[:], in_=hk[:], func=AF.Ln, scale=-1.0, bias=1.0), reads=["hk"], writes=["hg"])
                add("pool", lambda e: e.tensor_tensor(out=Gg[:], in0=Gg[:], in1=G4[:], op=ALU.mult), reads=["Gg"] + G4r, writes=["Gg"])
                tm_proj(tc, 4, 4)
                add("dve", lambda e: e.tensor_copy(out=kv_f[:], in_=PB[4][:, 0:256]), reads=[pbn(4)], writes=["kv_f"])
                vslot = 1 + tc
                add("pool", lambda e: e.tensor_copy(out=Vb[:, vslot, :], in_=kv_f[:, 128:256]), reads=["kv_f"], writes=["Vb%d" % vslot])
                if is_p:
                    if tile["t"] == SEQ // 512 - 1 and tc == 3:
                        add("pool", lambda e: e.dma_start(out=o_kp[tile["seq"]], in_=kv_f[:, 0:128]), reads=["kv_f"], dma=True, semkey="co")
                        add("pool", lambda e: e.dma_start(out=o_vp[tile["seq"]], in_=kv_f[:, 128:256]), reads=["kv_f"], dma=True, semkey="co")
                else:
                    for ch in range(2):
                        sq = 2 * tc + ch
                        add("pool", lambda e: e.dma_start(out=o_ks[sq, 64:128, :], in_=kv_f[ch * 64:(ch + 1) * 64, 0:128]), reads=["kv_f"], dma=True, semkey="co")
                        add("pool", lambda e: e.dma_start(out=o_vs[sq, 64:128, :], in_=kv_f[ch * 64:(ch + 1) * 64, 128:256]), reads=["kv_f"], dma=True, semkey="co")
                        add("pool", lambda e: e.dma_start(out=o_ks[sq, 0:64, :], in_=ck_in[sq, 64:128, :]), dma=True, semkey="co")
                        add("pool", lambda e: e.dma_start(out=o_vs[sq, 0:64, :], in_=cv_in[sq, 64:128, :]), dma=True, semkey="co")
                yield
                add("pe", lambda e: e.matmul(PB[5][:], lhsT=Mq[:], rhs=hg[:], start=True, stop=True), reads=[Mres[0], "hg"], writes=[pbn(5)])
                add("pe", lambda e: e.matmul(PB[6][:], lhsT=Mc[:], rhs=hg[:], start=True, stop=True), reads=[Mres[1], "hg"], writes=[pbn(6)])
                for h in range(4):
                    add("pe", lambda e: e.matmul(PB[7][:, 2 * h:2 * h + 2], lhsT=hg[:, h * 128:(h + 1) * 128], rhs=IND[:], start=True, stop=True),
                        reads=["hg", Mres[3]], writes=[pbn(7)])
                add("act", lambda e: e.activation(out=ex[0][:], in_=PB[5][:], func=AF.Exp), reads=[pbn(5)], writes=["ex0"])
                add("act", lambda e: e.activation(out=ex[1][:], in_=PB[5][:], func=AF.Exp, scale=-1.0), reads=[pbn(5)], writes=["ex1"])
                add("dve", lambda e: e.tensor_tensor(out=qt_b[:], in0=hq[:], in1=ex[0][:], op=ALU.mult), reads=["hq", "ex0"], writes=["qt_b"])
                add("dve", lambda e: e.tensor_tensor(out=kt_b[:], in0=hk[:], in1=ex[1][:], op=ALU.mult), reads=["hk", "ex1"], writes=["kt_b"])
                add("act", lambda e: e.activation(out=ex[0][:], in_=PB[6][:], func=AF.Exp), reads=[pbn(6)], writes=["ex0"])
                add("act", lambda e: e.activation(out=ebl[:], in_=PB[7][:, 0:8], func=AF.Exp), reads=[pbn(7)], writes=["ebl"])
                add("pool", lambda e: e.tensor_tensor(out=kh_b[:], in0=hk[:], in1=ex[0][:], op=ALU.mult), reads=["hk", "ex0"], writes=["kh_b"])
                yield
                pv = PB[4][:].bitcast(BF16).rearrange("p (k t) -> p k t", k=8)
                for h in range(4):
                    add("pe", lambda e: e.transpose(out=pv[:, h, :], in_=qt_b[:, h * 128:(h + 1) * 128], identity=id_b[:]),
                        reads=["qt_b", "id_b"], writes=[pbn(4)])
                for h in range(4):
                    add("pe", lambda e: e.transpose(out=pv[:, 4 + h, :], in_=kt_b[:, h * 128:(h + 1) * 128], identity=id_b[:]),
                        reads=["kt_b", "id_b"], writes=[pbn(4)])
                add("act", lambda e: e.copy(out=qTz[0][:, :, 0:64], in_=pv[:, 0:4, 0:64]), reads=[pbn(4)], writes=["qTz0"])
                add("act", lambda e: e.copy(out=qTz[1][:, :, 64:128], in_=pv[:, 0:4, 64:128]), reads=[pbn(4)], writes=["qTz1"])
                add("act", lambda e: e.copy(out=kT[:], in_=pv[:, 4:8, :]), reads=[pbn(4)], writes=["kT"])
                if is_p:
                    if tile["t"] == 0 and tc == 0:
                        add("dve", lambda e: e.memset(Sf[0][:], 0.0), writes=["Sf0_%d" % h_ for h_ in range(4)])
                    for h in range(4):
                        add("pool", lambda e: e.tensor_scalar(out=Sb[0][:, h, :], in0=Sf[0][:, h, :], scalar1=ebl[:, 2 * h:2 * h + 1], scalar2=None, op0=ALU.mult),
                            reads=["Sf0_%d" % h, "ebl"], writes=["Sb0_%d" % h])
                    sbs = [0, 0]
                else:
                    for ch in range(2):
                        sq = 2 * tc + ch
                        add("act", lambda e: e.dma_start(out=Sf[ch][:], in_=st_in[sq].rearrange("h k v -> k h v")),
                            writes=["Sf%d_%d" % (ch, h_) for h_ in range(4)], dma=True, semkey="sl%d" % ch)
                        add("pool", lambda e: e.tensor_copy(out=Sb[ch][:], in_=Sf[ch][:]), reads=["Sf%d_%d" % (ch, h_) for h_ in range(4)],
                            writes=["Sb%d_%d" % (ch, h_) for h_ in range(4)])
                    sbs = [0, 1]
                yield
                for h in range(4):
                    add("pe", lambda e: e.matmul(PB[5][:, h * 128:h * 128 + 64], lhsT=kT[:, h, :], rhs=qTz[0][:, h, 0:64], start=True, stop=True),
                        reads=["kT", "qTz0"], writes=[pbn(5)])
                    add("pe", lambda e: e.matmul(PB[5][:, h * 128 + 64:h * 128 + 128], lhsT=kT[:, h, :], rhs=qTz[1][:, h, 64:128], start=True, stop=True),
                        reads=["kT", "qTz1"], writes=[pbn(5)])
                add("dve", lambda e: e.tensor_tensor(out=AT[:].rearrange("p h t -> p (h t)"), in0=PB[5][:], in1=Mk[:], op=ALU.mult),
                    reads=[pbn(5), Mres[2]], writes=["AT"])
                yield
                for h in range(4):
                    osl = PB[6][:, h * 128:(h + 1) * 128]
                    add("pe", lambda e: e.matmul(osl, lhsT=AT[:, h, :], rhs=v_b[:, h * 128:(h + 1) * 128], start=True, stop=False),
                        reads=["AT", "v_b"], writes=[pbn(6)])
                    add("pe", lambda e: e.matmul(osl, lhsT=qTz[0][:, h, :], rhs=Sb[sbs[0]][:, h, :], start=False, stop=False),
                        reads=["qTz0", "Sb%d_%d" % (sbs[0], h)], writes=[pbn(6)])
                    add("pe", lambda e: e.matmul(osl, lhsT=qTz[1][:, h, :], rhs=Sb[sbs[1]][:, h, :], start=False, stop=True),
                        reads=["qTz1", "Sb%d_%d" % (sbs[1], h)], writes=[pbn(6)])
                if is_p:
                    for h in range(4):
                        add("pe", lambda e: e.matmul(PB[7][:, h * 128:(h + 1) * 128], lhsT=kh_b[:, h * 128:(h + 1) * 128],
                                                     rhs=v_b[:, h * 128:(h + 1) * 128], start=True, stop=True),
                            reads=["kh_b", "v_b"], writes=[pbn(7)])
                    for h in range(4):
                        add("dve", lambda e: e.scalar_tensor_tensor(
                            out=Sf[0][:, h, :], in0=Sf[0][:, h, :], scalar=ebl[:, 2 * h + 1:2 * h + 2], in1=PB[7][:, h * 128:(h + 1) * 128],
                            op0=ALU.mult, op1=ALU.add), reads=["Sf0_%d" % h, "ebl", pbn(7)], writes=["Sf0_%d" % h])
                    if tile["t"] == SEQ // 512 - 1 and tc == 3:
                        dsto = o_sp[tile["seq"]].rearrange("h k v -> k h v")
                        add("pool", lambda e: e.dma_start(out=dsto, in_=Sf[0][:]), reads=["Sf0_%d" % h_ for h_ in range(4)], dma=True, semkey="so0")
                else:
                    for ch in range(2):
                        ps_ = slice(ch * 64, (ch + 1) * 64)
                        for h in range(4):
                            add("pe", lambda e: e.matmul(PB[7][:, h * 128:(h + 1) * 128], lhsT=kh_b[ps_, h * 128:(h + 1) * 128],
                                                         rhs=v_b[ps_, h * 128:(h + 1) * 128], start=True, stop=True),
                                reads=["kh_b", "v_b"], writes=[pbn(7)])
                        for h in range(4):
                            add("dve", lambda e: e.scalar_tensor_tensor(
                                out=Sf[ch][:, h, :], in0=Sf[ch][:, h, :], scalar=ebl[:, 2 * h + ch:2 * h + ch + 1], in1=PB[7][:, h * 128:(h + 1) * 128],
                                op0=ALU.mult, op1=ALU.add), reads=["Sf%d_%d" % (ch, h), "ebl", pbn(7)], writes=["Sf%d_%d" % (ch, h)])
                        dsto = o_ss[2 * tc + ch].rearrange("h k v -> k h v")
                        add("pool", lambda e: e.dma_start(out=dsto, in_=Sf[ch][:]), reads=["Sf%d_%d" % (ch, h_) for h_ in range(4)], dma=True, semkey="so%d" % ch)
                yield
                add("dve", lambda e: e.memset(hst[:], 0.0), writes=["hst0", "hst1", "hst2", "hst3", "hstr"])
                for h in range(4):
                    add("act", lambda e: e.activation(out=junk[:, h * 128:(h + 1) * 128], in_=PB[6][:, h * 128:(h + 1) * 128], func=AF.Square,
                                                      accum_out=hst[:, h:h + 1]), reads=[pbn(6), "hst%d" % h], writes=["hst%d" % h] + (["sg0"] if h == 0 else []))
                add("act", lambda e: e.activation(out=hst[:, 4:8], in_=hst[:, 0:4], func=AF.Ln, scale=1.0 / 128, bias=epsb[:, 0:1]),
                    reads=["hst0", "hst1", "hst2", "hst3", "epsb"], writes=["hstr"])
                add("act", lambda e: e.activation(out=hst[:, 4:8], in_=hst[:, 4:8], func=AF.Exp, scale=-0.5), reads=["hstr"], writes=["hstr"])
                for h in range(4):
                    add("dve", lambda e: e.scalar_tensor_tensor(out=yhg[:, h * 128:(h + 1) * 128], in0=PB[6][:, h * 128:(h + 1) * 128],
                                                              scalar=hst[:, 4 + h:5 + h], in1=Gg[:, h * 128:(h + 1) * 128],
                                                              op0=ALU.mult, op1=ALU.mult), reads=[pbn(6), "hstr", "Gg"], writes=["yhg%d" % h])
                yield
                pv4 = PB[4][:].bitcast(BF16).rearrange("p (k t) -> p k t", k=8)
                for h in range(4):
                    add("pe", lambda e: e.transpose(out=pv4[:, h, :], in_=yhg[:, h * 128:(h + 1) * 128], identity=id_b[:]),
                        reads=["yhg%d" % h, "id_b"], writes=[pbn(4)])
                add("act", lambda e: e.copy(out=yT[:, 0:4, tsl], in_=pv4[:, 0:4, :]), reads=[pbn(4)], writes=["yTh_%d" % tc])
                yield

            def swa_chain(tc):
                tsl = slice(tc * 128, (tc + 1) * 128)
                if is_p:
                    first = tile["t"] == 0 and tc == 0
                    nk = 128 if first else 256
                    kb0 = (1 + tc) if first else tc
                    tcol0 = 128 if first else 0
                    Kn, Ks, kres = KTn, KTs, ["KTn", "KTs"]
                    vblocks = [(Vb[:, 1 + tc, :], "Vb%d" % (1 + tc))] if first else [(Vb[:, tc, :], "Vb%d" % tc), (Vb[:, 1 + tc, :], "Vb%d" % (1 + tc))]
                    GH = 8
                else:
                    nk = 384
                    kb0 = 0
                    tcol0 = 0
                    GH = 2
                    Kn, Ks, kres = KSn, KSs, ["KSn", "KSs"]
                    for ch in range(2):
                        sq = 2 * tc + ch
                        add("act", lambda e: e.dma_start(out=ck_f[:, ch, 0, :], in_=ck_in[sq]), writes=["ck_f%d" % ch], dma=True, semkey="ck%d_0" % ch)
                        add("act", lambda e: e.dma_start(out=ck_f[:, ch, 1, 0:64], in_=ck_in[sq][:, 64:128]), writes=["ck_fa%d" % ch], dma=True, semkey="ck%d_1" % ch)
                        add("act", lambda e: e.dma_start(out=ck_f[:, ch, 1, 64:128], in_=ck_in[sq][:, 0:64]), writes=["ck_fb%d" % ch], dma=True, semkey="ck%d_2" % ch)
                        add("act", lambda e: e.dma_start(out=cv_f[:, ch, :], in_=cv_in[sq]), writes=["cv_f%d" % ch], dma=True, semkey="ck%d_3" % ch)
                        for v_ in range(2):
                            add("pe", lambda e: e.matmul(PB[3][:, (2 * ch + v_) * 128:(2 * ch + v_ + 1) * 128], lhsT=ck_f[:, ch, v_, :], rhs=id_f[:],
                                                         start=True, stop=True),
                                reads=["ck_f%d" % ch, "ck_fa%d" % ch, "ck_fb%d" % ch, "id_f"], writes=[pbn(3)])
                        add("pool", lambda e: e.tensor_copy(out=Vc[:, ch, :], in_=cv_f[:, ch, :]), reads=["cv_f%d" % ch], writes=["Vc"])
                    add("act", lambda e: e.copy(out=KSn[:, 0:2, :], in_=PB[3][:].rearrange("p (c v t) -> p c v t", c=2, v=2)[:, :, 0, :]), reads=[pbn(3)], writes=["KSn"])
                    add("act", lambda e: e.copy(out=KSs[:, 0:2, :], in_=PB[3][:].rearrange("p (c v t) -> p c v t", c=2, v=2)[:, :, 1, :]), reads=[pbn(3)], writes=["KSs"])
                    add("pool", lambda e: e.tensor_copy(out=KSn[:, 2, :], in_=KTn[:, 1 + tc, :]), reads=["KTn"], writes=["KSn"])
                    add("pool", lambda e: e.tensor_copy(out=KSs[:, 2, :], in_=KTs[:, 1 + tc, :]), reads=["KTs"], writes=["KSs"])
                    vblocks = [(Vc[:, 0, :], "Vc"), (Vc[:, 1, :], "Vc"), (Vb[:, 1 + tc, :], "Vb%d" % (1 + tc))]
                    yield
                nkb = nk // 128
                scsv = scsF[:, 0:GH * nk].rearrange("p (h k) -> p h k", h=GH)
                pexv = pexF[:, 0:GH * nk].rearrange("p (h k) -> p h k", h=GH)
                PTv = PTF[:, 0:GH * nkb * 128].rearrange("p (h k t) -> p h k t", h=GH, k=nkb)
                A_M, A_NM, A_ES, A_RS = 0, 8, 16, 24
                for g in range(8 // GH):
                    for hh in range(GH):
                        h = GH * g + hh
                        base = 64 * (h % 2)
                        kvh = h // 4
                        nat = (kvh == 0 and base == 0) or (kvh == 1 and base == 64)
                        Kt = Kn if nat else Ks
                        if is_p:
                            sbk, sc0 = hh // 2, (hh % 2) * 256
                        else:
                            sbk, sc0 = hh, 0
                        add("pe", lambda e: e.matmul(
                            PB[sbk][:, sc0:sc0 + nk], lhsT=QT[base:base + 64, h // 2, tsl],
                            rhs=Kt[base:base + 64, kb0:kb0 + nkb, :].rearrange("p a b -> p (a b)"), start=True, stop=True),
                            reads=["QT"] + kres, writes=[pbn(sbk)])
                        add("dve", lambda e: e.scalar_tensor_tensor(
                            out=scsv[:, hh, :], in0=PB[sbk][:, sc0:sc0 + nk], scalar=0.125, in1=TAB[:, h, tcol0:tcol0 + nk],
                            op0=ALU.mult, op1=ALU.add), reads=[pbn(sbk), "tab"], writes=["scs%d" % hh])
                    yield
                    scn = ["scs%d" % q_ for q_ in range(GH)]
                    rsn = ["ast_rs%d" % q_ for q_ in range(GH)]
                    add("dve", lambda e: e.tensor_reduce(out=ast[:, A_M:A_M + GH], in_=scsv, axis=AX.X, op=ALU.max), reads=scn, writes=["ast_m"])
                    add("dve", lambda e: e.tensor_tensor(out=ast[:, A_M:A_M + GH], in0=ast[:, A_M:A_M + GH], in1=sinkb[:, GH * g:GH * g + GH], op=ALU.max),
                        reads=["ast_m", "sinkb"], writes=["ast_m"])
                    add("dve", lambda e: e.tensor_scalar(out=ast[:, A_NM:A_NM + GH], in0=ast[:, A_M:A_M + GH], scalar1=-1.0, scalar2=None, op0=ALU.mult),
                        reads=["ast_m"], writes=["ast_nm"])
                    add("dve", lambda e: e.tensor_tensor(out=ast[:, A_ES:A_ES + GH], in0=sinkb[:, GH * g:GH * g + GH], in1=ast[:, A_M:A_M + GH], op=ALU.subtract),
                        reads=["ast_m", "sinkb"], writes=["ast_es"])
                    add("dve", lambda e: e.memset(ast[:, A_RS:A_RS + 8], 0.0), writes=["ast_rs%d" % q_ for q_ in range(8)])
                    add("act", lambda e: e.activation(out=ast[:, A_ES:A_ES + GH], in_=ast[:, A_ES:A_ES + GH], func=AF.Exp), reads=["ast_es"], writes=["ast_es"])
                    for hh in range(GH):
                        add("act", lambda e: e.activation(out=pexv[:, hh, :], in_=scsv[:, hh, :], func=AF.Exp, bias=ast[:, A_NM + hh:A_NM + hh + 1],
                                                          accum_out=ast[:, A_RS + hh:A_RS + hh + 1]),
                            reads=["scs%d" % hh, "ast_nm", "ast_rs%d" % hh], writes=["pexp%d" % hh, "ast_rs%d" % hh])
                    add("dve", lambda e: e.tensor_tensor(out=ast[:, A_RS:A_RS + GH], in0=ast[:, A_RS:A_RS + GH], in1=ast[:, A_ES:A_ES + GH], op=ALU.add),
                        reads=["ast_es"] + rsn, writes=rsn)
                    add("dve", lambda e: e.reciprocal(out=ast[:, A_RS:A_RS + GH], in_=ast[:, A_RS:A_RS + GH]), reads=rsn, writes=rsn)
                    yield
                    for hh in range(GH):
                        for kb in range(nkb):
                            idx = hh * nkb + kb
                            tb = (idx // 8) if is_p else 2
                            pvt = PB[tb][:].bitcast(BF16).rearrange("p (k t) -> p k t", k=8)
                            add("pe", lambda e: e.transpose(out=pvt[:, idx % 8, :], in_=pexv[:, hh, kb * 128:(kb + 1) * 128], identity=id_b[:]),
                                reads=["pexp%d" % hh, "id_b"], writes=[pbn(tb)])
                    hpb = (8 // nkb) if is_p else GH
                    for h0 in range(0, GH, hpb):
                        hn = min(hpb, GH - h0)
                        tb = ((h0 * nkb) // 8) if is_p else 2
                        pvt = PB[tb][:].bitcast(BF16).rearrange("p (k t) -> p k t", k=8)
                        add("act", lambda e: e.copy(out=PTv[:, h0:h0 + hn, :, :],
                                                    in_=pvt[:, 0:hn * nkb, :].rearrange("p (h k) t -> p h k t", k=nkb)),
                            reads=[pbn(tb)], writes=["PT%d" % q_ for q_ in range(h0, h0 + hn)])
                    yield
                    ob = 2 if is_p else 3
                    for hh in range(GH):
                        h = GH * g + hh
                        kvh = h // 4
                        for kb in range(nkb):
                            va, vres = vblocks[kb]
                            add("pe", lambda e: e.matmul(PB[ob][:, hh * 64:(hh + 1) * 64], lhsT=PTv[:, hh, kb, :],
                                                         rhs=va[:, kvh * 64:(kvh + 1) * 64], start=(kb == 0), stop=(kb == nkb - 1)),
                                reads=["PT%d" % hh, vres], writes=[pbn(ob)])
                    h_lo = GH * g
                    add("dve", lambda e: e.tensor_tensor(
                        out=ysw[:, h_lo * 64:(h_lo + GH) * 64].rearrange("p (h d) -> p h d", h=GH),
                        in0=PB[ob][:, 0:GH * 64].rearrange("p (h d) -> p h d", h=GH),
                        in1=ast[:, A_RS:A_RS + GH].unsqueeze(2).to_broadcast([128, GH, 64]), op=ALU.mult),
                        reads=[pbn(ob)] + ["ast_rs%d" % q_ for q_ in range(GH)], writes=["ysw%d" % (h_lo + q_) for q_ in range(GH)])
                    yield
                pv3 = PB[3][:].bitcast(BF16).rearrange("p (k t) -> p k t", k=8)
                for c in range(4):
                    add("pe", lambda e: e.transpose(out=pv3[:, c, :], in_=ysw[:, c * 128:(c + 1) * 128], identity=id_b[:]),
                        reads=["ysw%d" % (2 * c), "ysw%d" % (2 * c + 1), "id_b"], writes=[pbn(3)])
                add("act", lambda e: e.copy(out=yT[:, 4:8, tsl], in_=pv3[:, 0:4, :]), reads=[pbn(3)], writes=["yTs_%d" % tc])
                yield

            def run_chains(chains):
                chains = [c for c in chains if c is not None]
                while chains:
                    for c in list(chains):
                        try:
                            next(c)
                        except StopIteration:
                            chains.remove(c)

            for tc in range(ntc + 1):
                run_chains([hgrn_chain(tc) if tc < ntc else None, swa_chain(tc - 1) if tc >= 1 else None])
            if is_p:
                add("pool", lambda e: e.tensor_copy(out=KTn[:, 0, :], in_=KTn[:, 4, :]), reads=["KTn"], writes=["KTn"])
                add("pool", lambda e: e.tensor_copy(out=KTs[:, 0, :], in_=KTs[:, 4, :]), reads=["KTs"], writes=["KTs"])
                add("pool", lambda e: e.tensor_copy(out=Vb[:, 0, :], in_=Vb[:, 4, :]), reads=["Vb4"], writes=["Vb0"])
            for half in range(2):
                wt, wres, _ = next_piece()
                wv = wt[:].rearrange("p (k c) -> p k c", k=8)
                for tc in range(ntc):
                    ga, gres = gates[tc]
                    xr = "xt%d_%d" % (slot, tc)
                    bk = tc % 2
                    for fc in range(8):
                        add("pe", lambda e, tc=tc, fc=fc, wv=wv, bk=bk: e.matmul(PB[bk][:], lhsT=yT[:, fc, tc * 128:(tc + 1) * 128], rhs=wv[:, fc, :],
                                                                          start=(fc == 0), stop=(fc == 7)), reads=wres + ["yTh_%d" % tc, "yTs_%d" % tc], writes=[pbn(bk)])
                    ti_ = tc % 2
                    add("dve", lambda e, ga=ga, half=half, bk=bk, ti_=ti_: e.tensor_tensor(out=tmpE[ti_][:], in0=PB[bk][:], in1=ga[:, half * 512:(half + 1) * 512], op=ALU.mult),
                        reads=[pbn(bk)] + gres, writes=["tmpE%d" % ti_])
                    add("pool", lambda e, tc=tc, half=half, ti_=ti_: e.tensor_tensor(
                        out=XT[slot][:, tc, half * 512:(half + 1) * 512], in0=XT[slot][:, tc, half * 512:(half + 1) * 512],
                        in1=tmpE[ti_][:], op=ALU.add), reads=["tmpE%d" % ti_, xr], writes=[xr])

        def final_norm_store(tile, slot):
            for tc in range(tile["ntc"]):
                xr = "xt%d_%d" % (slot, tc)
                xa = XT[slot][:, tc, :]
                if DBG:
                    add("pool", lambda e, tc=tc, xa=xa: e.dma_start(out=y_dst(tile, tc), in_=xa), reads=[xr], dma=True, semkey="y%d_%d" % (slot, tc))
                    continue
                add("dve", lambda e: e.memset(stat[:, 4:5], 0.0), writes=["stat2"])
                add("act", lambda e, xa=xa: e.activation(out=junk[:], in_=xa, func=AF.Square, accum_out=stat[:, 4:5]),
                    reads=[xr, "stat2"], writes=["stat2"] + (["sg0"] if tc == 0 else []))
                add("act", lambda e: e.activation(out=stat[:, 5:6], in_=stat[:, 4:5], func=AF.Ln, scale=1.0 / D, bias=epsb[:, 0:1]),
                    reads=["stat2", "epsb"], writes=["stat2"])
                add("act", lambda e: e.activation(out=stat[:, 6:7], in_=stat[:, 5:6], func=AF.Exp, scale=-0.5), reads=["stat2"], writes=["stat2"])
                add("dve", lambda e, xa=xa: e.scalar_tensor_tensor(out=xa, in0=xa, scalar=stat[:, 6:7], in1=gfin[:], op0=ALU.mult, op1=ALU.mult),
                    reads=[xr, "stat2", "gfin"], writes=[xr])
                add("pool", lambda e, tc=tc, xa=xa: e.dma_start(out=y_dst(tile, tc), in_=xa), reads=[xr], dma=True, semkey="y%d_%d" % (slot, tc))

        cur_tab = [None]
        g1_next = [None]
        for ti, tile in enumerate(tiles):
            slot = ti % 2
            if ti + 1 < len(tiles):
                load_x(ti + 1)
            want = "P" if tile["kind"] == "p" else "S"
            if cur_tab[0] != want:
                src = c_tabP if want == "P" else c_tabS
                add("act", lambda e, src=src: e.dma_start(out=TAB[:], in_=src), writes=["tab"], dma=True, semkey="tab")
                cur_tab[0] = want
            if ti == 0:
                g1 = load_gate(tile, 0, True)
                norm_to_hT(tile, slot, 0)
            else:
                g1 = g1_next[0]
            ffn(tile, slot, 0, g1)
            if DBG != 1:
                g2 = load_gate(tile, 1, False)
                norm_to_hT(tile, slot, 1)
                mixer(tile, slot, g2, ti)
            else:
                for _ in range(9):
                    next_piece()
            if DBG not in (1, 2):
                g3 = load_gate(tile, 2, True)
                S.fence(MIXN, HIDN)
                norm_to_hT(tile, slot, 2)
                ffn(tile, slot, 1, g3)
            else:
                for _ in range(17):
                    next_piece()
            if ti + 1 < len(tiles) and not DBG:
                g1_next[0] = load_gate(tiles[ti + 1], 0, True)
                norm_to_hT(tiles[ti + 1], (ti + 1) % 2, 0)
            elif ti + 1 < len(tiles):
                g1_next[0] = load_gate(tiles[ti + 1], 0, True)
                norm_to_hT(tiles[ti + 1], (ti + 1) % 2, 0)
            final_norm_store(tile, slot)

        S.emit(nc)
    return nc


_CONST = None


def kernel(x_prompt, x_sample, state_hgrn, cache_k, cache_v, c_prompt, c_sample,
           w_ada, b_ada, g_ffn1, w_up1, w_down1, g_mix, w_in, lb_logits, g_hgrn, sinks,
           w_out, g_ffn2, w_up2, w_down2, g_final, _cfg=None):
    n_cores = 8 if _cfg is None else _cfg
    f = lambda a: np.ascontiguousarray(np.asarray(a, dtype=np.float32))
    x_prompt, x_sample = f(x_prompt), f(x_sample)
    B, SEQ, _ = x_prompt.shape
    DB = x_sample.shape[0]
    NP = B // n_cores
    NS = DB // n_cores
    consts = host_constants()
    shared = dict(
        w_ada=f(w_ada)[0], b_ada=f(b_ada)[0].reshape(72, 128),
        g_norms=np.concatenate([f(g_ffn1)[0].reshape(8, 128), f(g_mix)[0].reshape(8, 128), f(g_ffn2)[0].reshape(8, 128)], axis=0),
        w_up1=f(w_up1)[0], w_up2=f(w_up2)[0], w_down1=f(w_down1)[0], w_down2=f(w_down2)[0],
        w_in=f(w_in)[0], lb_logits=f(lb_logits), g_hgrn=f(g_hgrn), sinks=f(sinks), w_out=f(w_out)[0],
        g_final=f(g_final).reshape(1, D), **consts)
    st0 = f(state_hgrn)[0]
    ck0 = f(cache_k)[0].reshape(DB, 128, 128)
    cv0 = f(cache_v)[0].reshape(DB, 128, 128)
    cp, cs = f(c_prompt), f(c_sample)
    in_maps = []
    for i in range(n_cores):
        m = dict(shared)
        m["xp"] = x_prompt[i * NP:(i + 1) * NP]
        m["xs"] = x_sample[i * NS:(i + 1) * NS].reshape(NS * 64, D)
        m["st_in"] = st0[i * NS:(i + 1) * NS]
        m["ck_in"] = ck0[i * NS:(i + 1) * NS]
        m["cv_in"] = cv0[i * NS:(i + 1) * NS]
        m["c_in"] = np.concatenate([cp[i * NP:(i + 1) * NP], cs[i * NS:(i + 1) * NS]], axis=0)
        in_maps.append(m)
    nc = build_program(NP, SEQ, NS)
    res = run_bass_kernel_spmd(nc, in_maps, core_ids=list(range(n_cores)))
    R = res.results
    cat = lambda k: np.concatenate([np.asarray(r[k]) for r in R], axis=0)
    y_prompt = cat("yp").reshape(B, SEQ, D)
    y_sample = cat("ys").reshape(DB, 64, D)
    sp = cat("o_sp").reshape(1, B, 4, 128, 128)
    kp = cat("o_kp").reshape(1, B, 128, 2, 64)
    vp = cat("o_vp").reshape(1, B, 128, 2, 64)
    ss = cat("o_ss").reshape(1, DB, 4, 128, 128)
    ks = cat("o_ks").reshape(1, DB, 128, 2, 64)
    vs = cat("o_vs").reshape(1, DB, 128, 2, 64)
    return tuple(np.ascontiguousarray(a, dtype=np.float32) for a in (y_prompt, y_sample, sp, kp, vp, ss, ks, vs))
```

```python
import contextlib
import numpy as np
import concourse.bass as bass
import concourse.mybir as mybir
from concourse.bass_utils import run_bass_kernel_spmd

F32 = mybir.dt.float32
BF16 = mybir.dt.bfloat16
AF = mybir.ActivationFunctionType
ALU = mybir.AluOpType
AX = mybir.AxisListType

D = 1024
KC = 8
DFF = 2816
NHID = 22
EPS = 1e-6
NSLOT = 5
SLOTW = 4096
NEG = -30000.0
DBG = 0
import os as _os
SKIP = set(_os.environ.get("KSKIP", "").split(","))
KSTOP = int(_os.environ.get("KSTOP", "99"))
COARSE = [p for p in _os.environ.get("KCOARSE", "hT_").split(",") if p]

ENGS = ("pe", "act", "dve", "pool", "sp")


class Op:
    __slots__ = ("eng", "fn", "deps", "signal", "is_dma", "semkey", "count", "pos")

    def __init__(self, eng, fn, is_dma, semkey):
        self.eng = eng
        self.fn = fn
        self.deps = []
        self.signal = False
        self.is_dma = is_dma
        self.semkey = semkey
        self.count = 0


class _Rec:
    def __init__(self):
        self.calls = []

    def __getattr__(self, name):
        def f(*a, **k):
            self.calls.append((name, a, k))
            return self
        return f


class Sched:
    def __init__(self, same_engine_sync=True):
        self.ops = {e: [] for e in ENGS}
        self.last_w = {}
        self.readers = {}
        self.same_engine_sync = same_engine_sync
        self.dma_counts = {}

    def _m(self, name):
        import re
        for pat in COARSE:
            if pat and name.startswith(pat):
                if pat == "hT_":
                    return "_".join(name.split("_")[:2])
                return pat
        return name

    def add(self, eng, fn, reads=(), writes=(), dma=False, semkey=None):
        if COARSE:
            reads = [self._m(r) for r in reads]
            writes = [self._m(w) for w in writes]
        rec = _Rec()
        fn(rec)
        assert len(rec.calls) == 1
        op = Op(eng, rec.calls[0], dma, semkey)
        deps = []
        for r in reads:
            w = self.last_w.get(r)
            if w is not None:
                deps.append(w)
        for w_ in writes:
            w = self.last_w.get(w_)
            if w is not None:
                deps.append(w)
            deps.extend(self.readers.get(w_, ()))
        seen = set()
        latest = {}
        for d in deps:
            if id(d) in seen:
                continue
            seen.add(id(d))
            if d.is_dma:
                op.deps.append(d)
                continue
            if d.eng == eng and (eng == "pe" or not self.same_engine_sync):
                continue
            cur = latest.get(d.eng)
            if cur is None or d.pos > cur.pos:
                latest[d.eng] = d
        op.deps.extend(latest.values())
        for r in reads:
            self.readers.setdefault(r, []).append(op)
        for w_ in writes:
            self.last_w[w_] = op
            self.readers[w_] = []
        op.pos = len(self.ops[eng])
        self.ops[eng].append(op)
        if dma:
            c = self.dma_counts.get(semkey, 0) + 16
            self.dma_counts[semkey] = c
            op.count = c
        return op

    def fence(self, old, new):
        ops = []
        seen = set()
        for o in old:
            w = self.last_w.get(o)
            cand = ([w] if w is not None else []) + list(self.readers.get(o, ()))
            for c in cand:
                if id(c) not in seen:
                    seen.add(id(c))
                    ops.append(c)
        for n in new:
            self.last_w.pop(n, None)
            self.readers[n] = list(ops)

    def emit(self, nc, final_wait_eng="pool"):
        for e in ENGS:
            for op in self.ops[e]:
                for d in op.deps:
                    d.signal = True
        last_dma = {}
        for e in ENGS:
            c = 0
            for op in self.ops[e]:
                if op.is_dma:
                    last_dma[op.semkey] = op
                elif op.signal:
                    c += 1
                    op.count = c
        with contextlib.ExitStack() as stack:
            esem = {e: stack.enter_context(nc.semaphore("s_" + e)) for e in ENGS}
            dsem = {}
            for i, k in enumerate(self.dma_counts):
                dsem[k] = stack.enter_context(nc.semaphore("d%d" % i))
            block = stack.enter_context(nc.Block())

            def run(e, engine):
                seen_e = {}
                seen_d = {}
                for op in self.ops[e]:
                    need_d, need_e = {}, {}
                    for d in op.deps:
                        if d.is_dma:
                            need_d[d.semkey] = max(need_d.get(d.semkey, 0), d.count)
                        else:
                            need_e[d.eng] = max(need_e.get(d.eng, 0), d.count)
                    for k, c in need_d.items():
                        if seen_d.get(k, 0) < c:
                            seen_d[k] = c
                            engine.wait_ge(dsem[k], c)
                    for k, c in need_e.items():
                        if seen_e.get(k, 0) < c:
                            seen_e[k] = c
                            engine.wait_ge(esem[k], c)
                    name_, a_, k_ = op.fn
                    ins = getattr(engine, name_)(*a_, **k_)
                    if op.is_dma:
                        ins.then_inc(dsem[op.semkey], 16)
                    elif op.signal:
                        ins.then_inc(esem[e], 1)
                if e == final_wait_eng:
                    for k, op in last_dma.items():
                        if seen_d.get(k, 0) < op.count:
                            engine.wait_ge(dsem[k], op.count)

            @block.tensor
            def _(eng):
                run("pe", eng)

            @block.scalar
            def _(eng):
                run("act", eng)

            @block.vector
            def _(eng):
                run("dve", eng)

            @block.gpsimd
            def _(eng):
                run("pool", eng)

            @block.sync
            def _(eng):
                run("sp", eng)


def host_constants():
    ident = np.eye(128, dtype=np.float32)
    s = np.arange(128)
    same = (s[:, None] // 64) == (s[None, :] // 64)
    triU = (same & (s[:, None] <= s[None, :])).astype(np.float32)
    triR = (same & (s[:, None] > s[None, :])).astype(np.float32)
    chunkind = np.zeros((128, 2), np.float32)
    chunkind[:64, 0] = 1.0
    chunkind[64:, 1] = 1.0
    maskA4 = np.tile(triU, (1, 4)).astype(np.float32)
    le = (s[:, None] <= s[None, :])
    triM = (le.astype(np.float32) - (s[:, None] <= 63).astype(np.float32) * np.ones((1, 128), np.float32)).astype(np.float32)
    triRf = (s[:, None] > s[None, :]).astype(np.float32)
    maskAf4 = np.tile(le.astype(np.float32), (1, 4)).astype(np.float32)
    ind2 = np.stack([(s < 64).astype(np.float32), np.ones(128, np.float32)], axis=1)
    slopes = np.exp2(-8.0 * np.arange(1, 9, dtype=np.float32) / 8.0).astype(np.float32)
    i = np.arange(128)
    j = np.arange(256)
    dist = np.abs(i[:, None] - (j[None, :] - 128)).astype(np.float32)
    qc = i[:, None] // 64
    kc = j[None, :] // 64 - 2
    vis = (kc >= qc - 2) & (kc <= qc)
    tabP = np.where(vis[:, None, :], -slopes[None, :, None] * dist[:, None, :], NEG).astype(np.float32)
    j = np.arange(384)
    qpos = 128 + (i % 64)
    kpos = np.where(j < 128, j, np.where(j < 256, j - 128, 128 + ((j - 256) % 64)))
    kseq = np.where(j < 128, 0, np.where(j < 256, 1, (j - 256) // 64))
    qseq = i // 64
    vis = qseq[:, None] == kseq[None, :]
    dist = np.abs(qpos[:, None] - kpos[None, :]).astype(np.float32)
    tabS = np.where(vis[:, None, :], -slopes[None, :, None] * dist[:, None, :], NEG).astype(np.float32)
    tabP3 = np.full((128, 8, 384), NEG, np.float32)
    tabP3[:, :, :256] = tabP
    return dict(c_triM=triM, c_triRf=triRf, c_maskAf4=maskAf4, c_ind2=ind2, c_ident=ident, c_triU=triU, c_triR=triR, c_chunkind=chunkind, c_maskA4=maskA4,
                c_tabP=np.ascontiguousarray(tabP3), c_tabS=np.ascontiguousarray(tabS))


def build_program(NP, SEQ, NS):
    assert SEQ % 512 == 0 and NS % 2 == 0
    NSEQ = NP + NS
    nc = bass.Bass("TRN2", target_bir_lowering=False)

    def din(name, shape):
        return nc.dram_tensor(name, list(shape), F32, kind="ExternalInput").ap()

    def dout(name, shape):
        return nc.dram_tensor(name, list(shape), F32, kind="ExternalOutput").ap()

    xp = din("xp", [NP, SEQ, D])
    xs = din("xs", [NS * 64, D])
    st_in = din("st_in", [NS, 4, 128, 128])
    ck_in = din("ck_in", [NS, 128, 128])
    cv_in = din("cv_in", [NS, 128, 128])
    c_in = din("c_in", [NSEQ, D])
    w_ada = din("w_ada", [D, 9 * D])
    b_ada = din("b_ada", [72, 128])
    g_norms = din("g_norms", [24, 128])
    w_up = [din("w_up1", [D, 2 * DFF]), din("w_up2", [D, 2 * DFF])]
    w_down = [din("w_down1", [DFF, D]), din("w_down2", [DFF, D])]
    w_in = din("w_in", [D, 2816])
    lb_logits = din("lb_logits", [2, 512])
    g_hgrn = din("g_hgrn", [1, 128])
    sinks = din("sinks", [1, 8])
    w_out = din("w_out", [D, D])
    g_final = din("g_final", [1, D])
    c_ident = din("c_ident", [128, 128])
    c_triU = din("c_triU", [128, 128])
    c_triR = din("c_triR", [128, 128])
    c_chunkind = din("c_chunkind", [128, 2])
    c_maskA4 = din("c_maskA4", [128, 512])
    c_triM = din("c_triM", [128, 128])
    c_triRf = din("c_triRf", [128, 128])
    c_maskAf4 = din("c_maskAf4", [128, 512])
    c_ind2 = din("c_ind2", [128, 2])
    c_tabP = din("c_tabP", [128, 8, 384])
    c_tabS = din("c_tabS", [128, 8, 384])

    yp = dout("yp", [NP, SEQ, D])
    ys = dout("ys", [NS * 64, D])
    o_sp = dout("o_sp", [NP, 4, 128, 128])
    o_kp = dout("o_kp", [NP, 128, 128])
    o_vp = dout("o_vp", [NP, 128, 128])
    o_ss = dout("o_ss", [NS, 4, 128, 128])
    o_ks = dout("o_ks", [NS, 128, 128])
    o_vs = dout("o_vs", [NS, 128, 128])

    PIECES = []
    for f in range(2):
        if f == 1:
            PIECES.append(("inQ", 0, 2048, 512))
            PIECES.append(("inK", 0, 2560, 256))
            for i, (c0, n) in enumerate([(0, 512), (512, 512), (1024, 512), (1536, 512), (2560, 256)]):
                PIECES.append(("inT", i, c0, n))
            PIECES.append(("out", 0, 0, 512))
            PIECES.append(("out", 1, 512, 512))
        for p in range(11):
            PIECES.append(("up%d" % f, p, 0, 512))
        for half in range(2):
            for kg, (k0, nk) in enumerate([(0, 8), (8, 8), (16, 6)]):
                PIECES.append(("dn%d" % f, half * 3 + kg, k0, nk))
    NPIECE = len(PIECES)
    scr = nc.dram_tensor("scr", [NPIECE, 128, SLOTW], BF16, kind="Internal").ap()
    modD = nc.dram_tensor("modD", [NSEQ, 3, D], F32, kind="Internal").ap()

    S = Sched()
    st = contextlib.ExitStack()
    with st:
        def sb(name, shape, dt=F32):
            return st.enter_context(nc.sbuf_tensor(name, list(shape), dt))

        PB = [st.enter_context(nc.psum_tensor("pb%d" % i, [128, 512], F32)) for i in range(8)]

        def pbn(i):
            return "pb%d" % i

        XT = [sb("xt%d" % i, [128, 4, D]) for i in range(2)]
        WR = [sb("wr%d" % i, [128, SLOTW], BF16) for i in range(NSLOT)]
        hT = sb("hT", [128, 8, 512], BF16)
        yT = sb("yT", [128, 8, 512], BF16)
        hid = sb("hid", [128, NHID, 512], BF16)
        GT = sb("gt", [128, 4, D])
        TAB = sb("tab", [128, 8, 384])
        id_f = sb("id_f", [128, 128])
        id_b = sb("id_b", [128, 128], BF16)
        triU = sb("triU", [128, 128])
        triR = sb("triR", [128, 128])
        chunkind = sb("chunkind", [128, 2])
        maskA4 = sb("maskA4", [128, 512])
        triM = sb("triM", [128, 128])
        triRf = sb("triRf", [128, 128])
        maskAf4 = sb("maskAf4", [128, 512])
        ind2 = sb("ind2", [128, 2])
        oml = sb("oml", [128, 512])
        G4 = sb("G4", [128, 512])
        gfin = sb("gfin", [128, D])
        sinkb = sb("sinkb", [128, 8])
        gT_in = sb("gT_in", [24, 128])
        gfm = sb("gfm", [128, 24])
        bT_in = sb("bT_in", [72, 128])
        bT = sb("bT", [128, 72])
        scT = sb("scT", [128, 8, NSEQ], BF16)
        modFM = sb("modFM", [128, 72, NSEQ])
        GS = sb("GS", [128, 3, 8, NSEQ])
        epsb = sb("epsb", [128, 1])
        stat = sb("stat", [128, 8])
        xn4 = sb("xn4", [128, 4, D], BF16)
        statN = sb("statN", [128, 16])
        PTF = xn4[:, 0:2, :].rearrange("p a b -> p (a b)")
        junk = None
        hidF = hid[:].rearrange("p c n -> p (c n)").bitcast(F32)
        gateTM = hidF[0:NSEQ, 0:3072].rearrange("p (a b) -> p a b", a=3)
        c6 = hidF[0:NSEQ, 3072:4096]
        sc6 = hidF[0:NSEQ, 4096:5120]
        R_gateTM = ["hid%d" % c for c in range(0, 12)]
        R_c6 = ["hid%d" % c for c in range(12, 16)]
        R_sc6 = ["hid%d" % c for c in range(16, 20)]
        lbt = XT[1][:, 0, :].rearrange("p (a b) -> p a b", a=2)
        scsF = hidF[:, 0:2048]
        QT = hidF[:, 2048:3072].bitcast(BF16).rearrange("p (a b) -> p a b", a=4)
        hq = hidF[:, 3072:3584]
        hk = hidF[:, 3584:4096]
        hg = hidF[:, 4096:4608]
        pexF = hidF[:, 4608:5632].bitcast(BF16)
        HIDN = ["hid%d" % c for c in range(NHID)] + ["xn0", "xn1"]
        MIXN = ["scs%d" % q_ for q_ in range(8)] + ["QT", "hq", "hk", "hg"] + ["pexp%d" % q_ for q_ in range(8)] + ["PT%d" % q_ for q_ in range(8)]
        sg = [sb("sg%d" % i, [128, 512]) for i in range(2)]
        junk = sg[0][:].bitcast(BF16)
        tmpE = [sb("tmpE%d" % i, [128, 512]) for i in range(2)]
        ex = [sb("ex%d" % i, [128, 512]) for i in range(2)]
        v_b = sb("v_b", [128, 512], BF16)
        Gg = sb("Gg", [128, 512])
        qt_b = sb("qt_b", [128, 512], BF16)
        kt_b = sb("kt_b", [128, 512], BF16)
        kh_b = sb("kh_b", [128, 512], BF16)
        qTz = [sb("qTz%d" % i, [128, 4, 128], BF16) for i in range(2)]
        kT = sb("kT", [128, 4, 128], BF16)
        AT = sb("AT", [128, 4, 128], BF16)
        ebl = sb("ebl", [128, 8])
        Sf = [sb("Sf%d" % i, [128, 4, 128]) for i in range(2)]
        Sb = [sb("Sb%d" % i, [128, 4, 128], BF16) for i in range(2)]
        hst = sb("hst", [128, 8])
        yhg = sb("yhg", [128, 512], BF16)
        KTn = sb("KTn", [128, 5, 128], BF16)
        KTs = sb("KTs", [128, 5, 128], BF16)
        Vb = sb("Vb", [128, 5, 128], BF16)
        KSn = sb("KSn", [128, 3, 128], BF16)
        KSs = sb("KSs", [128, 3, 128], BF16)
        Vc = sb("Vc", [128, 2, 128], BF16)
        ck_f = sb("ck_f", [128, 2, 2, 128])
        cv_f = sb("cv_f", [128, 2, 128])
        kv_f = sb("kv_f", [128, 256])
        ast = sb("ast", [128, 32])
        ysw = sb("ysw", [128, 512], BF16)

        add = S.add
        OB = [5, 0, 1, 2]

        def ld(eng, dst, src, res, key):
            add(eng, lambda e: e.dma_start(out=dst, in_=src), writes=[res], dma=True, semkey=key)

        ld("act", id_f[:], c_ident, "id_f", "c0")
        ld("act", triU[:], c_triU, "triU", "c1")
        ld("act", triR[:], c_triR, "triR", "c2")
        ld("act", chunkind[:], c_chunkind, "chunkind", "c3")
        ld("act", maskA4[:], c_maskA4, "maskA4", "c4")
        ld("act", triM[:], c_triM, "triM", "c12")
        ld("act", triRf[:], c_triRf, "triRf", "c13")
        ld("act", maskAf4[:], c_maskAf4, "maskAf4", "c14")
        ld("act", ind2[:], c_ind2, "ind2", "c15")
        ld("act", lbt, lb_logits.partition_broadcast(128), "xt1_0", "c5")
        for h in range(4):
            ld("act", G4[:, h * 128:(h + 1) * 128], g_hgrn[0].partition_broadcast(128), "G4_%d" % h, "c6")
        ld("act", gfin[:], g_final[0].partition_broadcast(128), "gfin", "c7")
        ld("act", sinkb[:], sinks[0].partition_broadcast(128), "sinkb", "c8")
        ld("act", gT_in[:], g_norms, "gT_in", "c9")
        ld("act", bT_in[:], b_ada, "bT_in", "c10")
        add("act", lambda e: e.dma_start(out=c6, in_=c_in), writes=R_c6, dma=True, semkey="c11")
        G4r = ["G4_%d" % h for h in range(4)]

        add("dve", lambda e: e.tensor_copy(out=id_b[:], in_=id_f[:]), reads=["id_f"], writes=["id_b"])
        add("dve", lambda e: e.memset(epsb[:], EPS), writes=["epsb"])
        for i in range(2):
            add("pool", lambda e, i=i: e.memset(qTz[i][:], 0.0), writes=["qTz%d" % i])
        add("dve", lambda e: e.tensor_sub(out=oml[:], in0=lbt[:, 1, :], in1=lbt[:, 0, :]), reads=["xt1_0"], writes=["oml"])
        add("act", lambda e: e.activation(out=oml[:], in_=oml[:], func=AF.Sigmoid), reads=["oml"], writes=["oml"])
        add("pe", lambda e: e.matmul(PB[0][:, 0:24], lhsT=gT_in[0:24, :], rhs=id_f[0:24, 0:24], start=True, stop=True),
            reads=["gT_in", "id_f"], writes=[pbn(0)])
        add("dve", lambda e: e.tensor_copy(out=gfm[:], in_=PB[0][:, 0:24]), reads=[pbn(0)], writes=["gfm"])
        add("pe", lambda e: e.matmul(PB[1][:, 0:72], lhsT=bT_in[0:72, :], rhs=id_f[0:72, 0:72], start=True, stop=True),
            reads=["bT_in", "id_f"], writes=[pbn(1)])
        add("dve", lambda e: e.tensor_copy(out=bT[:], in_=PB[1][:, 0:72]), reads=[pbn(1)], writes=["bT"])
        add("act", lambda e: e.activation(out=sc6, in_=c6, func=AF.Silu), reads=R_c6, writes=R_sc6)
        for kc in range(8):
            add("pe", lambda e, kc=kc: e.matmul(PB[2][:, kc * NSEQ:(kc + 1) * NSEQ], lhsT=sc6[0:NSEQ, kc * 128:(kc + 1) * 128],
                                                 rhs=id_f[0:NSEQ, 0:NSEQ], start=True, stop=True),
                reads=R_sc6 + ["id_f"], writes=[pbn(2)])
        add("dve", lambda e: e.tensor_copy(out=scT[:].rearrange("p k s -> p (k s)"), in_=PB[2][:, 0:8 * NSEQ]),
            reads=[pbn(2)], writes=["scT"])

        tiles = []
        for s_ in range(NP):
            for t in range(SEQ // 512):
                tiles.append(dict(kind="p", seq=s_, t=t, ntc=4))
        tiles.append(dict(kind="s", ntc=NS // 2))
        stream = [("ada", blk) for blk in range(18)]
        for ti in range(len(tiles)):
            for pi in range(NPIECE):
                stream.append((ti, pi))
        issued = [0]

        def slot_res(sl):
            return ["wr%d_a" % sl, "wr%d_b" % sl, "wr%d_c" % sl]

        def issue_one():
            g = issued[0]
            if g >= len(stream):
                return
            issued[0] += 1
            sl = g % NSLOT
            key = "w%d" % sl
            res = slot_res(sl)
            a, b = stream[g]
            if a == "ada":
                src = w_ada.rearrange("(kc p) c -> p kc c", p=128)[:, :, b * 512:(b + 1) * 512]
                dst = WR[sl][:].rearrange("p (k c) -> p k c", k=8)
                add("pool", lambda e: e.dma_start(out=dst, in_=src), writes=res, dma=True, semkey=key)
                return
            ti, pi = a, b
            name, idx, c0, n = PIECES[pi]
            if ti > 0:
                add("sp", lambda e: e.dma_start(out=WR[sl][:], in_=scr[pi]), reads=["scr%d" % pi], writes=res, dma=True, semkey="ws%d" % sl)
                return
            if name.startswith("up"):
                W = w_up[int(name[2])].rearrange("(kc p) c -> p kc c", p=128)
                dst = WR[sl][:].rearrange("p (k c) -> p k c", k=8)
                add("pool", lambda e: e.dma_start(out=dst[:, :, 0:256], in_=W[:, :, idx * 256:(idx + 1) * 256]),
                    writes=[res[0]], dma=True, semkey=key)
                add("pool", lambda e: e.dma_start(out=dst[:, :, 256:512], in_=W[:, :, DFF + idx * 256:DFF + (idx + 1) * 256]),
                    writes=[res[1]], dma=True, semkey=key + "b")
            elif name.startswith("dn"):
                half = idx // 3
                k0, nk = c0, n
                W = w_down[int(name[2])].rearrange("(kc p) c -> p kc c", p=128)
                dst = WR[sl][:, 0:nk * 512].rearrange("p (k c) -> p k c", k=nk)
                add("pool", lambda e: e.dma_start(out=dst, in_=W[:, k0:k0 + nk, half * 512:(half + 1) * 512]),
                    writes=res, dma=True, semkey=key)
            elif name == "inK":
                W = w_in.rearrange("(kc p) c -> p kc c", p=128)
                dst = WR[sl][:, 0:8 * 256].rearrange("p (k c) -> p k c", k=8)
                add("pool", lambda e: e.dma_start(out=dst[:, :, 0:128], in_=W[:, :, 2560:2688]), writes=[res[0]], dma=True, semkey=key)
                add("pool", lambda e: e.dma_start(out=dst[:, :, 128:192], in_=W[:, :, 2624:2688]), writes=[res[1]], dma=True, semkey=key + "b")
                add("pool", lambda e: e.dma_start(out=dst[:, :, 192:256], in_=W[:, :, 2560:2624]), writes=[res[2]], dma=True, semkey=key + "c")
            else:
                W = (w_out if name == "out" else w_in).rearrange("(kc p) c -> p kc c", p=128)
                dst = WR[sl][:, 0:8 * n].rearrange("p (k c) -> p k c", k=8)
                add("pool", lambda e: e.dma_start(out=dst, in_=W[:, :, c0:c0 + n]), writes=res, dma=True, semkey=key)
            add("sp", lambda e: e.dma_start(out=scr[pi], in_=WR[sl][:]), reads=res, writes=["scr%d" % pi], dma=True, semkey="scw%d" % sl)

        consumed = [0]

        def next_piece(lookahead=NSLOT):
            g = consumed[0]
            consumed[0] += 1
            while issued[0] < min(len(stream), g + lookahead):
                issue_one()
            sl = g % NSLOT
            return WR[sl], slot_res(sl), stream[g]

        def x_src(tile, tc):
            if tile["kind"] == "p":
                r0 = tile["t"] * 512 + tc * 128
                return xp[tile["seq"], r0:r0 + 128, :]
            return xs[tc * 128:(tc + 1) * 128, :]

        def y_dst(tile, tc):
            if tile["kind"] == "p":
                r0 = tile["t"] * 512 + tc * 128
                return yp[tile["seq"], r0:r0 + 128, :]
            return ys[tc * 128:(tc + 1) * 128, :]

        def load_x(ti):
            tile = tiles[ti]
            slot = ti % 2
            for tc in range(tile["ntc"]):
                add("pool", lambda e, tc=tc: e.dma_start(out=XT[slot][:, tc, :], in_=x_src(tile, tc)),
                    writes=["xt%d_%d" % (slot, tc)], dma=True, semkey="x%d_%d" % (slot, tc))

        load_x(0)

        for blk in range(18):
            wt, wres, _ = next_piece()
            wv = wt[:].rearrange("p (k c) -> p k c", k=8)
            pbk = 4 + (blk % 2)
            for jj in range(4):
                for kc in range(8):
                    add("pe", lambda e, jj=jj, kc=kc, wv=wv, pbk=pbk: e.matmul(
                        PB[pbk][:, jj * NSEQ:(jj + 1) * NSEQ], lhsT=wv[:, kc, jj * 128:(jj + 1) * 128], rhs=scT[:, kc, :],
                        start=(kc == 0), stop=(kc == 7)), reads=wres + ["scT"], writes=[pbn(pbk)])
            for jj in range(4):
                jc = blk * 4 + jj
                add("dve", lambda e, jj=jj, jc=jc, pbk=pbk: e.tensor_scalar(
                    out=modFM[:, jc, :], in0=PB[pbk][:, jj * NSEQ:(jj + 1) * NSEQ], scalar1=bT[:, jc:jc + 1], scalar2=None, op0=ALU.add),
                    reads=[pbn(pbk), "bT"], writes=["modFM%d" % jc])
        for n in range(3):
            for kc in range(8):
                jc = (3 * n + 1) * 8 + kc
                add("dve", lambda e, n=n, kc=kc, jc=jc: e.tensor_scalar(
                    out=GS[:, n, kc, :], in0=modFM[:, jc, :], scalar1=1.0, scalar2=gfm[:, n * 8 + kc:n * 8 + kc + 1],
                    op0=ALU.add, op1=ALU.mult), reads=["modFM%d" % jc, "gfm"], writes=["GS%d_%d" % (n, kc)])
        for n in range(3):
            for kc in range(8):
                jc = (3 * n + 2) * 8 + kc
                pbk = 6 + (n % 2)
                add("pe", lambda e, n=n, kc=kc, jc=jc, pbk=pbk: e.matmul(
                    PB[pbk][0:NSEQ, kc * 128:(kc + 1) * 128] if kc < 4 else PB[pbk][0:NSEQ, (kc - 4) * 128:(kc - 3) * 128],
                    lhsT=modFM[:, jc, :], rhs=id_f[:], start=True, stop=True), reads=["modFM%d" % jc, "id_f"], writes=[pbn(pbk)])
                if kc == 3 or kc == 7:
                    half = kc // 4
                    add("dve", lambda e, n=n, half=half, pbk=pbk: e.tensor_copy(
                        out=gateTM[:, n, half * 512:(half + 1) * 512], in_=PB[pbk][0:NSEQ, :]), reads=[pbn(pbk)], writes=R_gateTM)
        add("pool", lambda e: e.dma_start(out=modD, in_=gateTM), reads=R_gateTM, writes=["modD"], dma=True, semkey="modD")

        def tc_segs(tile, tc):
            if tile["kind"] == "p":
                return [(0, 128, tile["seq"])]
            return [(0, 64, NP + 2 * tc), (64, 128, NP + 2 * tc + 1)]

        gt_rr = [0]

        def load_gate(tile, n, scale_half):
            outs = []
            if tile["kind"] == "p":
                k = gt_rr[0] % 4
                gt_rr[0] += 1
                res = "gt%d" % k
                sq = tile["seq"]
                add("act", lambda e: e.dma_start(out=GT[:, k, :], in_=modD[sq, n, :].partition_broadcast(128)),
                    reads=["modD"], writes=[res], dma=True, semkey="g%d" % k)
                return [(GT[:, k, :], [res])] * tile["ntc"]
            for tc in range(tile["ntc"]):
                k = gt_rr[0] % 4
                gt_rr[0] += 1
                res = "gt%d" % k
                for (p0, p1, sq) in tc_segs(tile, tc):
                    add("act", lambda e, p0=p0, p1=p1, sq=sq, k=k: e.dma_start(
                        out=GT[p0:p1, k, :], in_=modD[sq, n, :].partition_broadcast(64)),
                        reads=["modD"], writes=[res + "_%d" % p0], dma=True, semkey="g%d" % k)
                outs.append((GT[:, k, :], [res + "_0", res + "_64"]))
            return outs

        tr_rr = [0]

        def norm_to_hT(tile, slot, n):
            ntc = tile["ntc"]
            for tc in range(ntc):
                xr = "xt%d_%d" % (slot, tc)
                xa = XT[slot][:, tc, :]
                c0 = 3 * tc
                sr = "statN%d" % tc
                add("dve", lambda e: e.memset(statN[:, c0:c0 + 1], 0.0), writes=[sr])
                add("act", lambda e: e.activation(out=junk[:], in_=xa, func=AF.Square, accum_out=statN[:, c0:c0 + 1]),
                    reads=[xr, sr], writes=[sr] + (["sg0"] if tc == 0 else []))
                add("act", lambda e: e.activation(out=statN[:, c0 + 1:c0 + 2], in_=statN[:, c0:c0 + 1], func=AF.Ln, scale=1.0 / D, bias=epsb[:, 0:1]),
                    reads=[sr, "epsb"], writes=[sr])
                add("act", lambda e: e.activation(out=statN[:, c0 + 2:c0 + 3], in_=statN[:, c0 + 1:c0 + 2], func=AF.Exp, scale=-0.5), reads=[sr], writes=[sr])
            for tc in range(ntc):
                xr = "xt%d_%d" % (slot, tc)
                xa = XT[slot][:, tc, :]
                c0 = 3 * tc
                add("dve", lambda e: e.tensor_scalar(out=xn4[:, tc, :], in0=xa, scalar1=statN[:, c0 + 2:c0 + 3], scalar2=None, op0=ALU.mult),
                    reads=[xr, "statN%d" % tc], writes=["xn%d" % tc])
            for tc in range(ntc):
                xnr = "xn%d" % tc
                pbk = tr_rr[0] % 2
                tr_rr[0] += 1
                pv = PB[pbk][:].bitcast(BF16).rearrange("p (k t) -> p k t", k=8)
                for kc in range(8):
                    add("pe", lambda e: e.transpose(out=pv[:, kc, :], in_=xn4[:, tc, kc * 128:(kc + 1) * 128], identity=id_b[:]),
                        reads=[xnr, "id_b"], writes=[pbn(pbk)])
                for kc in range(8):
                    jc = (3 * n) * 8 + kc
                    for (p0, p1, sq) in tc_segs(tile, tc):
                        hr_ = "hT_%d_%d_%d" % (tc, kc, p0)
                        if kc % 4 != 3:
                            add("dve", lambda e: e.tensor_scalar(
                                out=hT[:, kc, tc * 128 + p0:tc * 128 + p1], in0=pv[:, kc, p0:p1],
                                scalar1=GS[:, n, kc, sq:sq + 1], scalar2=modFM[:, jc, sq:sq + 1], op0=ALU.mult, op1=ALU.add),
                                reads=[pbn(pbk), "GS%d_%d" % (n, kc), "modFM%d" % jc], writes=[hr_])
                        else:
                            add("act", lambda e: e.activation(
                                out=hT[:, kc, tc * 128 + p0:tc * 128 + p1], in_=pv[:, kc, p0:p1], func=AF.Identity,
                                scale=GS[:, n, kc, sq:sq + 1], bias=modFM[:, jc, sq:sq + 1]),
                                reads=[pbn(pbk), "GS%d_%d" % (n, kc), "modFM%d" % jc], writes=[hr_])

        def hT_tc(tile, tc):
            return ["hT_%d_%d_%d" % (tc, kc, p0) for kc in range(8) for (p0, p1, sq) in tc_segs(tile, tc)]

        def hT_res(tile):
            r = []
            for tc in range(tile["ntc"]):
                r += hT_tc(tile, tc)
            return r

        def ffn(tile, slot, f, gates):
            NT = tile["ntc"] * 128
            ntc = tile["ntc"]
            hres = hT_res(tile)
            for p in range(11):
                wt, wres, _ = next_piece()
                wv = wt[:].rearrange("p (k c) -> p k c", k=8)
                for q in range(2):
                    c = 2 * p + q
                    bg, bu = (0, 1) if c % 2 == 0 else (2, 3)
                    for (bk, col) in ((bg, q * 128), (bu, 256 + q * 128)):
                        for kc in range(8):
                            add("pe", lambda e, bk=bk, col=col, kc=kc, wv=wv: e.matmul(
                                PB[bk][:, 0:NT], lhsT=wv[:, kc, col:col + 128], rhs=hT[:, kc, 0:NT], start=(kc == 0), stop=(kc == 7)),
                                reads=wres + hres, writes=[pbn(bk)])
                    sgi = c % 2
                    add("act", lambda e, bg=bg, sgi=sgi: e.activation(out=sg[sgi][:, 0:NT], in_=PB[bg][:, 0:NT], func=AF.Silu),
                        reads=[pbn(bg)], writes=["sg%d" % sgi])
                    add("dve", lambda e, bu=bu, sgi=sgi, c=c: e.tensor_tensor(out=hid[:, c, 0:NT], in0=PB[bu][:, 0:NT], in1=sg[sgi][:, 0:NT], op=ALU.mult),
                        reads=[pbn(bu), "sg%d" % sgi], writes=["hid%d" % c])
            for half in range(2):
                for kg, (k0, nk) in enumerate([(0, 8), (8, 8), (16, 6)]):
                    wt, wres, _ = next_piece()
                    wv = wt[:, 0:nk * 512].rearrange("p (k c) -> p k c", k=nk)
                    for tc in range(ntc):
                        for k in range(nk):
                            kk = k0 + k
                            add("pe", lambda e, tc=tc, k=k, kk=kk, wv=wv: e.matmul(
                                PB[4 + tc][:, :], lhsT=hid[:, kk, tc * 128:(tc + 1) * 128], rhs=wv[:, k, :],
                                start=(kk == 0), stop=(kk == NHID - 1)), reads=wres + ["hid%d" % kk], writes=[pbn(4 + tc)])
                for tc in range(ntc):
                    ga, gres = gates[tc]
                    xr = "xt%d_%d" % (slot, tc)
                    ti_ = tc % 2
                    add("dve", lambda e, tc=tc, ga=ga, half=half, ti_=ti_: e.scalar_tensor_tensor(
                        out=tmpE[ti_][:], in0=PB[4 + tc][:], scalar=0.5, in1=ga[:, half * 512:(half + 1) * 512], op0=ALU.mult, op1=ALU.mult),
                        reads=[pbn(4 + tc)] + gres, writes=["tmpE%d" % ti_])
                    add("pool", lambda e, tc=tc, half=half, ti_=ti_: e.tensor_tensor(
                        out=XT[slot][:, tc, half * 512:(half + 1) * 512], in0=XT[slot][:, tc, half * 512:(half + 1) * 512],
                        in1=tmpE[ti_][:], op=ALU.add), reads=["tmpE%d" % ti_, xr], writes=[xr])

        def mixer(tile, slot, gates, ti):
            ntc = tile["ntc"]
            NT = ntc * 128
            hres = hT_res(tile)
            is_p = tile["kind"] == "p"
            S.fence(HIDN, MIXN)
            if is_p:
                Mq, Mc, Mk, IND = triM, triRf, maskAf4, ind2
                Mres = ["triM", "triRf", "maskAf4", "ind2"]
            else:
                Mq, Mc, Mk, IND = triU, triR, maskA4, chunkind
                Mres = ["triU", "triR", "maskA4", "chunkind"]
            wq_t, wq_res, _ = next_piece()
            wqv = wq_t[:].rearrange("p (k c) -> p k c", k=8)
            for c in range(4):
                bk = c % 2
                for kc in range(8):
                    add("pe", lambda e: e.matmul(PB[bk][:, 0:NT], lhsT=wqv[:, kc, c * 128:(c + 1) * 128], rhs=hT[:, kc, 0:NT],
                                                 start=(kc == 0), stop=(kc == 7)), reads=wq_res + hres, writes=[pbn(bk)])
                add("act", lambda e: e.copy(out=QT[:, c, 0:NT], in_=PB[bk][:, 0:NT]), reads=[pbn(bk)], writes=["QT"])
            wk_t, wk_res, _ = next_piece()
            wkv = wk_t[:, 0:2048].rearrange("p (k c) -> p k c", k=8)
            for v_ in range(2):
                bk = 2 + v_
                for kc in range(8):
                    add("pe", lambda e: e.matmul(PB[bk][:, 0:NT], lhsT=wkv[:, kc, v_ * 128:(v_ + 1) * 128], rhs=hT[:, kc, 0:NT],
                                                 start=(kc == 0), stop=(kc == 7)), reads=wk_res + hres, writes=[pbn(bk)])
                dstK = (KTn if v_ == 0 else KTs)[:, 1:1 + ntc, :]
                add("act", lambda e: e.copy(out=dstK, in_=PB[bk][:, 0:NT].rearrange("p (a b) -> p a b", a=ntc)),
                    reads=[pbn(bk)], writes=["KTn" if v_ == 0 else "KTs"])
            tmw = [next_piece(lookahead=5 - i) for i in range(5)]

            def tm_proj(tc, i5, bank):
                wt, wres, _ = tmw[i5]
                n = 512 if i5 < 4 else 256
                wv = wt[:, 0:8 * n].rearrange("p (k c) -> p k c", k=8)
                for kc in range(8):
                    add("pe", lambda e: e.matmul(PB[bank][:, 0:n], lhsT=hT[:, kc, tc * 128:(tc + 1) * 128], rhs=wv[:, kc, :],
                                                 start=(kc == 0), stop=(kc == 7)), reads=wres + hT_tc(tile, tc), writes=[pbn(bank)])

            def hgrn_chain(tc):
                tsl = slice(tc * 128, (tc + 1) * 128)
                tm_proj(tc, 0, 4)
                tm_proj(tc, 1, 5)
                tm_proj(tc, 2, 6)
                tm_proj(tc, 3, 7)
                add("act", lambda e: e.activation(out=hq[:], in_=PB[4][:], func=AF.Silu), reads=[pbn(4)], writes=["hq"])
                add("act", lambda e: e.activation(out=Gg[:], in_=PB[7][:], func=AF.Silu), reads=[pbn(7)], writes=["Gg"])
                add("act", lambda e: e.activation(out=hk[:], in_=PB[5][:], func=AF.Sigmoid, scale=-1.0), reads=[pbn(5)], writes=["hk"])
                add("dve", lambda e: e.tensor_copy(out=v_b[:], in_=PB[6][:]), reads=[pbn(6)], writes=["v_b"])
                add("dve", lambda e: e.tensor_tensor(out=hk[:], in0=hk[:], in1=oml[:], op=ALU.mult), reads=["hk", "oml"], writes=["hk"])
                add("act", lambda e: e.activation(out=hg[:], in_=hk[:], func=AF.Ln, scale=-1.0, bias=1.0), reads=["hk"], writes=["hg"])
                add("pool", lambda e: e.tensor_tensor(out=Gg[:], in0=Gg[:], in1=G4[:], op=ALU.mult), reads=["Gg"] + G4r, writes=["Gg"])
                tm_proj(tc, 4, 4)
                add("dve", lambda e: e.tensor_copy(out=kv_f[:], in_=PB[4][:, 0:256]), reads=[pbn(4)], writes=["kv_f"])
                vslot = 1 + tc
                add("pool", lambda e: e.tensor_copy(out=Vb[:, vslot, :], in_=kv_f[:, 128:256]), reads=["kv_f"], writes=["Vb%d" % vslot])
                if is_p:
                    if tile["t"] == SEQ // 512 - 1 and tc == 3:
                        add("pool", lambda e: e.dma_start(out=o_kp[tile["seq"]], in_=kv_f[:, 0:128]), reads=["kv_f"], dma=True, semkey="co")
                        add("pool", lambda e: e.dma_start(out=o_vp[tile["seq"]], in_=kv_f[:, 128:256]), reads=["kv_f"], dma=True, semkey="co")
                else:
                    for ch in range(2):
                        sq = 2 * tc + ch
                        add("pool", lambda e: e.dma_start(out=o_ks[sq, 64:128, :], in_=kv_f[ch * 64:(ch + 1) * 64, 0:128]), reads=["kv_f"], dma=True, semkey="co")
                        add("pool", lambda e: e.dma_start(out=o_vs[sq, 64:128, :], in_=kv_f[ch * 64:(ch + 1) * 64, 128:256]), reads=["kv_f"], dma=True, semkey="co")
                        add("pool", lambda e: e.dma_start(out=o_ks[sq, 0:64, :], in_=ck_in[sq, 64:128, :]), dma=True, semkey="co")
                        add("pool", lambda e: e.dma_start(out=o_vs[sq, 0:64, :], in_=cv_in[sq, 64:128, :]), dma=True, semkey="co")
                yield
                add("pe", lambda e: e.matmul(PB[5][:], lhsT=Mq[:], rhs=hg[:], start=True, stop=True), reads=[Mres[0], "hg"], writes=[pbn(5)])
                add("pe", lambda e: e.matmul(PB[6][:], lhsT=Mc[:], rhs=hg[:], start=True, stop=True), reads=[Mres[1], "hg"], writes=[pbn(6)])
                for h in range(4):
                    add("pe", lambda e: e.matmul(PB[7][:, 2 * h:2 * h + 2], lhsT=hg[:, h * 128:(h + 1) * 128], rhs=IND[:], start=True, stop=True),
                        reads=["hg", Mres[3]], writes=[pbn(7)])
                add("act", lambda e: e.activation(out=ex[0][:], in_=PB[5][:], func=AF.Exp), reads=[pbn(5)], writes=["ex0"])
                add("act", lambda e: e.activation(out=ex[1][:], in_=PB[5][:], func=AF.Exp, scale=-1.0), reads=[pbn(5)], writes=["ex1"])
                add("dve", lambda e: e.tensor_tensor(out=qt_b[:], in0=hq[:], in1=ex[0][:], op=ALU.mult), reads=["hq", "ex0"], writes=["qt_b"])
                add("dve", lambda e: e.tensor_tensor(out=kt_b[:], in0=hk[:], in1=ex[1][:], op=ALU.mult), reads=["hk", "ex1"], writes=["kt_b"])
                add("act", lambda e: e.activation(out=ex[0][:], in_=PB[6][:], func=AF.Exp), reads=[pbn(6)], writes=["ex0"])
                add("act", lambda e: e.activation(out=ebl[:], in_=PB[7][:, 0:8], func=AF.Exp), reads=[pbn(7)], writes=["ebl"])
                add("pool", lambda e: e.tensor_tensor(out=kh_b[:], in0=hk[:], in1=ex[0][:], op=ALU.mult), reads=["hk", "ex0"], writes=["kh_b"])
                yield
                pv = PB[4][:].bitcast(BF16).rearrange("p (k t) -> p k t", k=8)
                for h in range(4):
                    add("pe", lambda e: e.transpose(out=pv[:, h, :], in_=qt_b[:, h * 128:(h + 1) * 128], identity=id_b[:]),
                        reads=["qt_b", "id_b"], writes=[pbn(4)])
                for h in range(4):
                    add("pe", lambda e: e.transpose(out=pv[:, 4 + h, :], in_=kt_b[:, h * 128:(h + 1) * 128], identity=id_b[:]),
                        reads=["kt_b", "id_b"], writes=[pbn(4)])
                add("act", lambda e: e.copy(out=qTz[0][:, :, 0:64], in_=pv[:, 0:4, 0:64]), reads=[pbn(4)], writes=["qTz0"])
                add("act", lambda e: e.copy(out=qTz[1][:, :, 64:128], in_=pv[:, 0:4, 64:128]), reads=[pbn(4)], writes=["qTz1"])
                add("act", lambda e: e.copy(out=kT[:], in_=pv[:, 4:8, :]), reads=[pbn(4)], writes=["kT"])
                if is_p:
                    if tile["t"] == 0 and tc == 0:
                        add("dve", lambda e: e.memset(Sf[0][:], 0.0), writes=["Sf0_%d" % h_ for h_ in range(4)])
                    for h in range(4):
                        add("act", lambda e: e.activation(out=Sb[0][:, h, :], in_=Sf[0][:, h, :], func=AF.Copy, scale=ebl[:, 2 * h:2 * h + 1]),
                            reads=["Sf0_%d" % h, "ebl"], writes=["Sb0_%d" % h])
                    sbs = [0, 0]
                else:
                    for ch in range(2):
                        sq = 2 * tc + ch
                        add("act", lambda e: e.dma_start(out=Sf[ch][:], in_=st_in[sq].rearrange("h k v -> k h v")),
                            writes=["Sf%d_%d" % (ch, h_) for h_ in range(4)], dma=True, semkey="sl%d" % ch)
                        add("pool", lambda e: e.tensor_copy(out=Sb[ch][:], in_=Sf[ch][:]), reads=["Sf%d_%d" % (ch, h_) for h_ in range(4)],
                            writes=["Sb%d_%d" % (ch, h_) for h_ in range(4)])
                    sbs = [0, 1]
                yield
                for h in range(4):
                    add("pe", lambda e: e.matmul(PB[5][:, h * 128:h * 128 + 64], lhsT=kT[:, h, :], rhs=qTz[0][:, h, 0:64], start=True, stop=True),
                        reads=["kT", "qTz0"], writes=[pbn(5)])
                    add("pe", lambda e: e.matmul(PB[5][:, h * 128 + 64:h * 128 + 128], lhsT=kT[:, h, :], rhs=qTz[1][:, h, 64:128], start=True, stop=True),
                        reads=["kT", "qTz1"], writes=[pbn(5)])
                add("dve", lambda e: e.tensor_tensor(out=AT[:].rearrange("p h t -> p (h t)"), in0=PB[5][:], in1=Mk[:], op=ALU.mult),
                    reads=[pbn(5), Mres[2]], writes=["AT"])
                yield
                for h in range(4):
                    osl = PB[6][:, h * 128:(h + 1) * 128]
                    add("pe", lambda e: e.matmul(osl, lhsT=AT[:, h, :], rhs=v_b[:, h * 128:(h + 1) * 128], start=True, stop=False),
                        reads=["AT", "v_b"], writes=[pbn(6)])
                    add("pe", lambda e: e.matmul(osl, lhsT=qTz[0][:, h, :], rhs=Sb[sbs[0]][:, h, :], start=False, stop=False),
                        reads=["qTz0", "Sb%d_%d" % (sbs[0], h)], writes=[pbn(6)])
                    add("pe", lambda e: e.matmul(osl, lhsT=qTz[1][:, h, :], rhs=Sb[sbs[1]][:, h, :], start=False, stop=True),
                        reads=["qTz1", "Sb%d_%d" % (sbs[1], h)], writes=[pbn(6)])
                if is_p:
                    for h in range(4):
                        add("pe", lambda e: e.matmul(PB[7][:, h * 128:(h + 1) * 128], lhsT=kh_b[:, h * 128:(h + 1) * 128],
                                                     rhs=v_b[:, h * 128:(h + 1) * 128], start=True, stop=True),
                            reads=["kh_b", "v_b"], writes=[pbn(7)])
                    for h in range(4):
                        add("dve", lambda e: e.scalar_tensor_tensor(
                            out=Sf[0][:, h, :], in0=Sf[0][:, h, :], scalar=ebl[:, 2 * h + 1:2 * h + 2], in1=PB[7][:, h * 128:(h + 1) * 128],
                            op0=ALU.mult, op1=ALU.add), reads=["Sf0_%d" % h, "ebl", pbn(7)], writes=["Sf0_%d" % h])
                    if tile["t"] == SEQ // 512 - 1 and tc == 3:
                        dsto = o_sp[tile["seq"]].rearrange("h k v -> k h v")
                        add("pool", lambda e: e.dma_start(out=dsto, in_=Sf[0][:]), reads=["Sf0_%d" % h_ for h_ in range(4)], dma=True, semkey="so0")
                else:
                    for ch in range(2):
                        ps_ = slice(ch * 64, (ch + 1) * 64)
                        for h in range(4):
                            add("pe", lambda e: e.matmul(PB[7][:, h * 128:(h + 1) * 128], lhsT=kh_b[ps_, h * 128:(h + 1) * 128],
                                                         rhs=v_b[ps_, h * 128:(h + 1) * 128], start=True, stop=True),
                                reads=["kh_b", "v_b"], writes=[pbn(7)])
                        for h in range(4):
                            add("dve", lambda e: e.scalar_tensor_tensor(
                                out=Sf[ch][:, h, :], in0=Sf[ch][:, h, :], scalar=ebl[:, 2 * h + ch:2 * h + ch + 1], in1=PB[7][:, h * 128:(h + 1) * 128],
                                op0=ALU.mult, op1=ALU.add), reads=["Sf%d_%d" % (ch, h), "ebl", pbn(7)], writes=["Sf%d_%d" % (ch, h)])
                        dsto = o_ss[2 * tc + ch].rearrange("h k v -> k h v")
                        add("pool", lambda e: e.dma_start(out=dsto, in_=Sf[ch][:]), reads=["Sf%d_%d" % (ch, h_) for h_ in range(4)], dma=True, semkey="so%d" % ch)
                yield
                add("dve", lambda e: e.memset(hst[:], 0.0), writes=["hst0", "hst1", "hst2", "hst3", "hstr"])
                for h in range(4):
                    add("act", lambda e: e.activation(out=junk[:, h * 128:(h + 1) * 128], in_=PB[6][:, h * 128:(h + 1) * 128], func=AF.Square,
                                                      accum_out=hst[:, h:h + 1]), reads=[pbn(6), "hst%d" % h], writes=["hst%d" % h] + (["sg0"] if h == 0 else []))
                add("act", lambda e: e.activation(out=hst[:, 4:8], in_=hst[:, 0:4], func=AF.Ln, scale=1.0 / 128, bias=epsb[:, 0:1]),
                    reads=["hst0", "hst1", "hst2", "hst3", "epsb"], writes=["hstr"])
                add("act", lambda e: e.activation(out=hst[:, 4:8], in_=hst[:, 4:8], func=AF.Exp, scale=-0.5), reads=["hstr"], writes=["hstr"])
                for h in range(4):
                    add("dve", lambda e: e.scalar_tensor_tensor(out=yhg[:, h * 128:(h + 1) * 128], in0=PB[6][:, h * 128:(h + 1) * 128],
                                                              scalar=hst[:, 4 + h:5 + h], in1=Gg[:, h * 128:(h + 1) * 128],
                                                              op0=ALU.mult, op1=ALU.mult), reads=[pbn(6), "hstr", "Gg"], writes=["yhg%d" % h])
                yield
                pv4 = PB[4][:].bitcast(BF16).rearrange("p (k t) -> p k t", k=8)
                for h in range(4):
                    add("pe", lambda e: e.transpose(out=pv4[:, h, :], in_=yhg[:, h * 128:(h + 1) * 128], identity=id_b[:]),
                        reads=["yhg%d" % h, "id_b"], writes=[pbn(4)])
                add("act", lambda e: e.copy(out=yT[:, 0:4, tsl], in_=pv4[:, 0:4, :]), reads=[pbn(4)], writes=["yTh_%d" % tc])
                yield

            def swa_chain(tc):
                tsl = slice(tc * 128, (tc + 1) * 128)
                if is_p:
                    first = tile["t"] == 0 and tc == 0
                    nk = 128 if first else 256
                    kb0 = (1 + tc) if first else tc
                    tcol0 = 128 if first else 0
                    Kn, Ks, kres = KTn, KTs, ["KTn", "KTs"]
                    vblocks = [(Vb[:, 1 + tc, :], "Vb%d" % (1 + tc))] if first else [(Vb[:, tc, :], "Vb%d" % tc), (Vb[:, 1 + tc, :], "Vb%d" % (1 + tc))]
                    GH = 8
                else:
                    nk = 384
                    kb0 = 0
                    tcol0 = 0
                    GH = 2
                    Kn, Ks, kres = KSn, KSs, ["KSn", "KSs"]
                    for ch in range(2):
                        sq = 2 * tc + ch
                        add("act", lambda e: e.dma_start(out=ck_f[:, ch, 0, :], in_=ck_in[sq]), writes=["ck_f%d" % ch], dma=True, semkey="ck%d_0" % ch)
                        add("act", lambda e: e.dma_start(out=ck_f[:, ch, 1, 0:64], in_=ck_in[sq][:, 64:128]), writes=["ck_fa%d" % ch], dma=True, semkey="ck%d_1" % ch)
                        add("act", lambda e: e.dma_start(out=ck_f[:, ch, 1, 64:128], in_=ck_in[sq][:, 0:64]), writes=["ck_fb%d" % ch], dma=True, semkey="ck%d_2" % ch)
                        add("act", lambda e: e.dma_start(out=cv_f[:, ch, :], in_=cv_in[sq]), writes=["cv_f%d" % ch], dma=True, semkey="ck%d_3" % ch)
                        for v_ in range(2):
                            add("pe", lambda e: e.matmul(PB[3][:, (2 * ch + v_) * 128:(2 * ch + v_ + 1) * 128], lhsT=ck_f[:, ch, v_, :], rhs=id_f[:],
                                                         start=True, stop=True),
                                reads=["ck_f%d" % ch, "ck_fa%d" % ch, "ck_fb%d" % ch, "id_f"], writes=[pbn(3)])
                        add("pool", lambda e: e.tensor_copy(out=Vc[:, ch, :], in_=cv_f[:, ch, :]), reads=["cv_f%d" % ch], writes=["Vc"])
                    add("act", lambda e: e.copy(out=KSn[:, 0:2, :], in_=PB[3][:].rearrange("p (c v t) -> p c v t", c=2, v=2)[:, :, 0, :]), reads=[pbn(3)], writes=["KSn"])
                    add("act", lambda e: e.copy(out=KSs[:, 0:2, :], in_=PB[3][:].rearrange("p (c v t) -> p c v t", c=2, v=2)[:, :, 1, :]), reads=[pbn(3)], writes=["KSs"])
                    add("pool", lambda e: e.tensor_copy(out=KSn[:, 2, :], in_=KTn[:, 1 + tc, :]), reads=["KTn"], writes=["KSn"])
                    add("pool", lambda e: e.tensor_copy(out=KSs[:, 2, :], in_=KTs[:, 1 + tc, :]), reads=["KTs"], writes=["KSs"])
                    vblocks = [(Vc[:, 0, :], "Vc"), (Vc[:, 1, :], "Vc"), (Vb[:, 1 + tc, :], "Vb%d" % (1 + tc))]
                    yield
                nkb = nk // 128
                scsv = scsF[:, 0:GH * nk].rearrange("p (h k) -> p h k", h=GH)
                pexv = pexF[:, 0:GH * nk].rearrange("p (h k) -> p h k", h=GH)
                PTv = PTF[:, 0:GH * nkb * 128].rearrange("p (h k t) -> p h k t", h=GH, k=nkb)
                A_M, A_NM, A_ES, A_RS = 0, 8, 16, 24
                for g in range(8 // GH):
                    for hh in range(GH):
                        h = GH * g + hh
                        base = 64 * (h % 2)
                        kvh = h // 4
                        nat = (kvh == 0 and base == 0) or (kvh == 1 and base == 64)
                        Kt = Kn if nat else Ks
                        if is_p:
                            sbk, sc0 = hh // 2, (hh % 2) * 256
                        else:
                            sbk, sc0 = hh, 0
                        add("pe", lambda e: e.matmul(
                            PB[sbk][:, sc0:sc0 + nk], lhsT=QT[base:base + 64, h // 2, tsl],
                            rhs=Kt[base:base + 64, kb0:kb0 + nkb, :].rearrange("p a b -> p (a b)"), start=True, stop=True),
                            reads=["QT"] + kres, writes=[pbn(sbk)])
                        add("dve", lambda e: e.scalar_tensor_tensor(
                            out=scsv[:, hh, :], in0=PB[sbk][:, sc0:sc0 + nk], scalar=0.125, in1=TAB[:, h, tcol0:tcol0 + nk],
                            op0=ALU.mult, op1=ALU.add), reads=[pbn(sbk), "tab"], writes=["scs%d" % hh])
                    yield
                    scn = ["scs%d" % q_ for q_ in range(GH)]
                    rsn = ["ast_rs%d" % q_ for q_ in range(GH)]
                    add("dve", lambda e: e.tensor_reduce(out=ast[:, A_M:A_M + GH], in_=scsv, axis=AX.X, op=ALU.max), reads=scn, writes=["ast_m"])
                    add("dve", lambda e: e.tensor_tensor(out=ast[:, A_M:A_M + GH], in0=ast[:, A_M:A_M + GH], in1=sinkb[:, GH * g:GH * g + GH], op=ALU.max),
                        reads=["ast_m", "sinkb"], writes=["ast_m"])
                    add("dve", lambda e: e.tensor_scalar(out=ast[:, A_NM:A_NM + GH], in0=ast[:, A_M:A_M + GH], scalar1=-1.0, scalar2=None, op0=ALU.mult),
                        reads=["ast_m"], writes=["ast_nm"])
                    add("dve", lambda e: e.tensor_tensor(out=ast[:, A_ES:A_ES + GH], in0=sinkb[:, GH * g:GH * g + GH], in1=ast[:, A_M:A_M + GH], op=ALU.subtract),
                        reads=["ast_m", "sinkb"], writes=["ast_es"])
                    add("dve", lambda e: e.memset(ast[:, A_RS:A_RS + 8], 0.0), writes=["ast_rs%d" % q_ for q_ in range(8)])
                    add("act", lambda e: e.activation(out=ast[:, A_ES:A_ES + GH], in_=ast[:, A_ES:A_ES + GH], func=AF.Exp), reads=["ast_es"], writes=["ast_es"])
                    for hh in range(GH):
                        add("act", lambda e: e.activation(out=pexv[:, hh, :], in_=scsv[:, hh, :], func=AF.Exp, bias=ast[:, A_NM + hh:A_NM + hh + 1],
                                                          accum_out=ast[:, A_RS + hh:A_RS + hh + 1]),
                            reads=["scs%d" % hh, "ast_nm", "ast_rs%d" % hh], writes=["pexp%d" % hh, "ast_rs%d" % hh])
                    add("dve", lambda e: e.tensor_tensor(out=ast[:, A_RS:A_RS + GH], in0=ast[:, A_RS:A_RS + GH], in1=ast[:, A_ES:A_ES + GH], op=ALU.add),
                        reads=["ast_es"] + rsn, writes=rsn)
                    add("dve", lambda e: e.reciprocal(out=ast[:, A_RS:A_RS + GH], in_=ast[:, A_RS:A_RS + GH]), reads=rsn, writes=rsn)
                    yield
                    for hh in range(GH):
                        for kb in range(nkb):
                            idx = hh * nkb + kb
                            tb = (idx // 8) if is_p else 2
                            pvt = PB[tb][:].bitcast(BF16).rearrange("p (k t) -> p k t", k=8)
                            add("pe", lambda e: e.transpose(out=pvt[:, idx % 8, :], in_=pexv[:, hh, kb * 128:(kb + 1) * 128], identity=id_b[:]),
                                reads=["pexp%d" % hh, "id_b"], writes=[pbn(tb)])
                    hpb = (8 // nkb) if is_p else GH
                    for h0 in range(0, GH, hpb):
                        hn = min(hpb, GH - h0)
                        tb = ((h0 * nkb) // 8) if is_p else 2
                        pvt = PB[tb][:].bitcast(BF16).rearrange("p (k t) -> p k t", k=8)
                        add("act", lambda e: e.copy(out=PTv[:, h0:h0 + hn, :, :],
                                                    in_=pvt[:, 0:hn * nkb, :].rearrange("p (h k) t -> p h k t", k=nkb)),
                            reads=[pbn(tb)], writes=["PT%d" % q_ for q_ in range(h0, h0 + hn)])
                    yield
                    ob = 2 if is_p else 3
                    for hh in range(GH):
                        h = GH * g + hh
                        kvh = h // 4
                        for kb in range(nkb):
                            va, vres = vblocks[kb]
                            add("pe", lambda e: e.matmul(PB[ob][:, hh * 64:(hh + 1) * 64], lhsT=PTv[:, hh, kb, :],
                                                         rhs=va[:, kvh * 64:(kvh + 1) * 64], start=(kb == 0), stop=(kb == nkb - 1)),
                                reads=["PT%d" % hh, vres], writes=[pbn(ob)])
                    h_lo = GH * g
                    add("dve", lambda e: e.tensor_tensor(
                        out=ysw[:, h_lo * 64:(h_lo + GH) * 64].rearrange("p (h d) -> p h d", h=GH),
                        in0=PB[ob][:, 0:GH * 64].rearrange("p (h d) -> p h d", h=GH),
                        in1=ast[:, A_RS:A_RS + GH].unsqueeze(2).to_broadcast([128, GH, 64]), op=ALU.mult),
                        reads=[pbn(ob)] + ["ast_rs%d" % q_ for q_ in range(GH)], writes=["ysw%d" % (h_lo + q_) for q_ in range(GH)])
                    yield
                pv3 = PB[3][:].bitcast(BF16).rearrange("p (k t) -> p k t", k=8)
                for c in range(4):
                    add("pe", lambda e: e.transpose(out=pv3[:, c, :], in_=ysw[:, c * 128:(c + 1) * 128], identity=id_b[:]),
                        reads=["ysw%d" % (2 * c), "ysw%d" % (2 * c + 1), "id_b"], writes=[pbn(3)])
                add("act", lambda e: e.copy(out=yT[:, 4:8, tsl], in_=pv3[:, 0:4, :]), reads=[pbn(3)], writes=["yTs_%d" % tc])
                yield

            def run_chains(chains):
                chains = [c for c in chains if c is not None]
                while chains:
                    for c in list(chains):
                        try:
                            next(c)
                        except StopIteration:
                            chains.remove(c)

            for tc in range(ntc + 1):
                run_chains([hgrn_chain(tc) if tc < ntc else None, swa_chain(tc - 1) if tc >= 1 else None])
            if is_p:
                add("pool", lambda e: e.tensor_copy(out=KTn[:, 0, :], in_=KTn[:, 4, :]), reads=["KTn"], writes=["KTn"])
                add("pool", lambda e: e.tensor_copy(out=KTs[:, 0, :], in_=KTs[:, 4, :]), reads=["KTs"], writes=["KTs"])
                add("pool", lambda e: e.tensor_copy(out=Vb[:, 0, :], in_=Vb[:, 4, :]), reads=["Vb4"], writes=["Vb0"])
            for half in range(2):
                wt, wres, _ = next_piece()
                wv = wt[:].rearrange("p (k c) -> p k c", k=8)
                for tc in range(ntc):
                    ga, gres = gates[tc]
                    xr = "xt%d_%d" % (slot, tc)
                    bk = tc % 2
                    for fc in range(8):
                        add("pe", lambda e, tc=tc, fc=fc, wv=wv, bk=bk: e.matmul(PB[bk][:], lhsT=yT[:, fc, tc * 128:(tc + 1) * 128], rhs=wv[:, fc, :],
                                                                          start=(fc == 0), stop=(fc == 7)), reads=wres + ["yTh_%d" % tc, "yTs_%d" % tc], writes=[pbn(bk)])
                    ti_ = tc % 2
                    add("dve", lambda e, ga=ga, half=half, bk=bk, ti_=ti_: e.tensor_tensor(out=tmpE[ti_][:], in0=PB[bk][:], in1=ga[:, half * 512:(half + 1) * 512], op=ALU.mult),
                        reads=[pbn(bk)] + gres, writes=["tmpE%d" % ti_])
                    add("pool", lambda e, tc=tc, half=half, ti_=ti_: e.tensor_tensor(
                        out=XT[slot][:, tc, half * 512:(half + 1) * 512], in0=XT[slot][:, tc, half * 512:(half + 1) * 512],
                        in1=tmpE[ti_][:], op=ALU.add), reads=["tmpE%d" % ti_, xr], writes=[xr])

        def final_norm_store(tile, slot):
            for tc in range(tile["ntc"]):
                xr = "xt%d_%d" % (slot, tc)
                xa = XT[slot][:, tc, :]
                if DBG:
                    add("pool", lambda e, tc=tc, xa=xa: e.dma_start(out=y_dst(tile, tc), in_=xa), reads=[xr], dma=True, semkey="y%d_%d" % (slot, tc))
                    continue
                add("dve", lambda e: e.memset(stat[:, 4:5], 0.0), writes=["stat2"])
                add("act", lambda e, xa=xa: e.activation(out=junk[:], in_=xa, func=AF.Square, accum_out=stat[:, 4:5]),
                    reads=[xr, "stat2"], writes=["stat2"] + (["sg0"] if tc == 0 else []))
                add("act", lambda e: e.activation(out=stat[:, 5:6], in_=stat[:, 4:5], func=AF.Ln, scale=1.0 / D, bias=epsb[:, 0:1]),
                    reads=["stat2", "epsb"], writes=["stat2"])
                add("act", lambda e: e.activation(out=stat[:, 6:7], in_=stat[:, 5:6], func=AF.Exp, scale=-0.5), reads=["stat2"], writes=["stat2"])
                add("dve", lambda e, xa=xa: e.scalar_tensor_tensor(out=xa, in0=xa, scalar=stat[:, 6:7], in1=gfin[:], op0=ALU.mult, op1=ALU.mult),
                    reads=[xr, "stat2", "gfin"], writes=[xr])
                add("pool", lambda e, tc=tc, xa=xa: e.dma_start(out=y_dst(tile, tc), in_=xa), reads=[xr], dma=True, semkey="y%d_%d" % (slot, tc))

        cur_tab = [None]
        g1_next = [None]
        for ti, tile in enumerate(tiles):
            slot = ti % 2
            if ti + 1 < len(tiles):
                load_x(ti + 1)
            want = "P" if tile["kind"] == "p" else "S"
            if cur_tab[0] != want:
                src = c_tabP if want == "P" else c_tabS
                add("act", lambda e, src=src: e.dma_start(out=TAB[:], in_=src), writes=["tab"], dma=True, semkey="tab")
                cur_tab[0] = want
            if ti == 0:
                g1 = load_gate(tile, 0, True)
                norm_to_hT(tile, slot, 0)
            else:
                g1 = g1_next[0]
            ffn(tile, slot, 0, g1)
            if DBG != 1:
                g2 = load_gate(tile, 1, False)
                norm_to_hT(tile, slot, 1)
                mixer(tile, slot, g2, ti)
            else:
                for _ in range(9):
                    next_piece()
            if DBG not in (1, 2):
                g3 = load_gate(tile, 2, True)
                S.fence(MIXN, HIDN)
                norm_to_hT(tile, slot, 2)
                ffn(tile, slot, 1, g3)
            else:
                for _ in range(17):
                    next_piece()
            if ti + 1 < len(tiles) and not DBG:
                g1_next[0] = load_gate(tiles[ti + 1], 0, True)
                norm_to_hT(tiles[ti + 1], (ti + 1) % 2, 0)
            elif ti + 1 < len(tiles):
                g1_next[0] = load_gate(tiles[ti + 1], 0, True)
                norm_to_hT(tiles[ti + 1], (ti + 1) % 2, 0)
            final_norm_store(tile, slot)

        S.emit(nc)
    return nc


_CONST = None


def kernel(x_prompt, x_sample, state_hgrn, cache_k, cache_v, c_prompt, c_sample,
           w_ada, b_ada, g_ffn1, w_up1, w_down1, g_mix, w_in, lb_logits, g_hgrn, sinks,
           w_out, g_ffn2, w_up2, w_down2, g_final, _cfg=None):
    n_cores = 8 if _cfg is None else _cfg
    f = lambda a: np.ascontiguousarray(np.asarray(a, dtype=np.float32))
    x_prompt, x_sample = f(x_prompt), f(x_sample)
    B, SEQ, _ = x_prompt.shape
    DB = x_sample.shape[0]
    NP = B // n_cores
    NS = DB // n_cores
    consts = host_constants()
    shared = dict(
        w_ada=f(w_ada)[0], b_ada=f(b_ada)[0].reshape(72, 128),
        g_norms=np.concatenate([f(g_ffn1)[0].reshape(8, 128), f(g_mix)[0].reshape(8, 128), f(g_ffn2)[0].reshape(8, 128)], axis=0),
        w_up1=f(w_up1)[0], w_up2=f(w_up2)[0], w_down1=f(w_down1)[0], w_down2=f(w_down2)[0],
        w_in=f(w_in)[0], lb_logits=f(lb_logits), g_hgrn=f(g_hgrn), sinks=f(sinks), w_out=f(w_out)[0],
        g_final=f(g_final).reshape(1, D), **consts)
    st0 = f(state_hgrn)[0]
    ck0 = f(cache_k)[0].reshape(DB, 128, 128)
    cv0 = f(cache_v)[0].reshape(DB, 128, 128)
    cp, cs = f(c_prompt), f(c_sample)
    in_maps = []
    for i in range(n_cores):
        m = dict(shared)
        m["xp"] = x_prompt[i * NP:(i + 1) * NP]
        m["xs"] = x_sample[i * NS:(i + 1) * NS].reshape(NS * 64, D)
        m["st_in"] = st0[i * NS:(i + 1) * NS]
        m["ck_in"] = ck0[i * NS:(i + 1) * NS]
        m["cv_in"] = cv0[i * NS:(i + 1) * NS]
        m["c_in"] = np.concatenate([cp[i * NP:(i + 1) * NP], cs[i * NS:(i + 1) * NS]], axis=0)
        in_maps.append(m)
    nc = build_program(NP, SEQ, NS)
    res = run_bass_kernel_spmd(nc, in_maps, core_ids=list(range(n_cores)))
    R = res.results
    cat = lambda k: np.concatenate([np.asarray(r[k]) for r in R], axis=0)
    y_prompt = cat("yp").reshape(B, SEQ, D)
    y_sample = cat("ys").reshape(DB, 64, D)
    sp = cat("o_sp").reshape(1, B, 4, 128, 128)
    kp = cat("o_kp").reshape(1, B, 128, 2, 64)
    vp = cat("o_vp").reshape(1, B, 128, 2, 64)
    ss = cat("o_ss").reshape(1, DB, 4, 128, 128)
    ks = cat("o_ks").reshape(1, DB, 128, 2, 64)
    vs = cat("o_vs").reshape(1, DB, 128, 2, 64)
    return tuple(np.ascontiguousarray(a, dtype=np.float32) for a in (y_prompt, y_sample, sp, kp, vp, ss, ks, vs))
```

```python
import contextlib
import numpy as np
import concourse.bass as bass
import concourse.mybir as mybir
from concourse.bass_utils import run_bass_kernel_spmd

F32 = mybir.dt.float32
BF16 = mybir.dt.bfloat16
AF = mybir.ActivationFunctionType
ALU = mybir.AluOpType
AX = mybir.AxisListType

D = 1024
KC = 8
DFF = 2816
NHID = 22
EPS = 1e-6
NSLOT = 5
SLOTW = 4096
NEG = -30000.0
DBG = 0
import os as _os
SKIP = set(_os.environ.get("KSKIP", "").split(","))
KSTOP = int(_os.environ.get("KSTOP", "99"))
COARSE = [p for p in _os.environ.get("KCOARSE", "hT_").split(",") if p]

ENGS = ("pe", "act", "dve", "pool", "sp")


class Op:
    __slots__ = ("eng", "fn", "deps", "signal", "is_dma", "semkey", "count", "pos")

    def __init__(self, eng, fn, is_dma, semkey):
        self.eng = eng
        self.fn = fn
        self.deps = []
        self.signal = False
        self.is_dma = is_dma
        self.semkey = semkey
        self.count = 0


class _Rec:
    def __init__(self):
        self.calls = []

    def __getattr__(self, name):
        def f(*a, **k):
            self.calls.append((name, a, k))
            return self
        return f


class Sched:
    def __init__(self, same_engine_sync=True):
        self.ops = {e: [] for e in ENGS}
        self.last_w = {}
        self.readers = {}
        self.same_engine_sync = same_engine_sync
        self.dma_counts = {}

    def _m(self, name):
        import re
        for pat in COARSE:
            if pat and name.startswith(pat):
                if pat == "hT_":
                    return "_".join(name.split("_")[:2])
                return pat
        return name

    def add(self, eng, fn, reads=(), writes=(), dma=False, semkey=None):
        if COARSE:
            reads = [self._m(r) for r in reads]
            writes = [self._m(w) for w in writes]
        rec = _Rec()
        fn(rec)
        assert len(rec.calls) == 1
        op = Op(eng, rec.calls[0], dma, semkey)
        deps = []
        for r in reads:
            w = self.last_w.get(r)
            if w is not None:
                deps.append(w)
        for w_ in writes:
            w = self.last_w.get(w_)
            if w is not None:
                deps.append(w)
            deps.extend(self.readers.get(w_, ()))
        seen = set()
        latest = {}
        for d in deps:
            if id(d) in seen:
                continue
            seen.add(id(d))
            if d.is_dma:
                op.deps.append(d)
                continue
            if d.eng == eng and (eng == "pe" or not self.same_engine_sync):
                continue
            cur = latest.get(d.eng)
            if cur is None or d.pos > cur.pos:
                latest[d.eng] = d
        op.deps.extend(latest.values())
        for r in reads:
            self.readers.setdefault(r, []).append(op)
        for w_ in writes:
            self.last_w[w_] = op
            self.readers[w_] = []
        op.pos = len(self.ops[eng])
        self.ops[eng].append(op)
        if dma:
            c = self.dma_counts.get(semkey, 0) + 16
            self.dma_counts[semkey] = c
            op.count = c
        return op

    def fence(self, old, new):
        ops = []
        seen = set()
        for o in old:
            w = self.last_w.get(o)
            cand = ([w] if w is not None else []) + list(self.readers.get(o, ()))
            for c in cand:
                if id(c) not in seen:
                    seen.add(id(c))
                    ops.append(c)
        for n in new:
            self.last_w.pop(n, None)
            self.readers[n] = list(ops)

    def emit(self, nc, final_wait_eng="pool"):
        for e in ENGS:
            for op in self.ops[e]:
                for d in op.deps:
                    d.signal = True
        last_dma = {}
        for e in ENGS:
            c = 0
            for op in self.ops[e]:
                if op.is_dma:
                    last_dma[op.semkey] = op
                elif op.signal:
                    c += 1
                    op.count = c
        with contextlib.ExitStack() as stack:
            esem = {e: stack.enter_context(nc.semaphore("s_" + e)) for e in ENGS}
            dsem = {}
            for i, k in enumerate(self.dma_counts):
                dsem[k] = stack.enter_context(nc.semaphore("d%d" % i))
            block = stack.enter_context(nc.Block())

            def run(e, engine):
                seen_e = {}
                seen_d = {}
                for op in self.ops[e]:
                    need_d, need_e = {}, {}
                    for d in op.deps:
                        if d.is_dma:
                            need_d[d.semkey] = max(need_d.get(d.semkey, 0), d.count)
                        else:
                            need_e[d.eng] = max(need_e.get(d.eng, 0), d.count)
                    for k, c in need_d.items():
                        if seen_d.get(k, 0) < c:
                            seen_d[k] = c
                            engine.wait_ge(dsem[k], c)
                    for k, c in need_e.items():
                        if seen_e.get(k, 0) < c:
                            seen_e[k] = c
                            engine.wait_ge(esem[k], c)
                    name_, a_, k_ = op.fn
                    ins = getattr(engine, name_)(*a_, **k_)
                    if op.is_dma:
                        ins.then_inc(dsem[op.semkey], 16)
                    elif op.signal:
                        ins.then_inc(esem[e], 1)
                if e == final_wait_eng:
                    for k, op in last_dma.items():
                        if seen_d.get(k, 0) < op.count:
                            engine.wait_ge(dsem[k], op.count)

            @block.tensor
            def _(eng):
                run("pe", eng)

            @block.scalar
            def _(eng):
                run("act", eng)

            @block.vector
            def _(eng):
                run("dve", eng)

            @block.gpsimd
            def _(eng):
                run("pool", eng)

            @block.sync
            def _(eng):
                run("sp", eng)


def host_constants():
    ident = np.eye(128, dtype=np.float32)
    s = np.arange(128)
    same = (s[:, None] // 64) == (s[None, :] // 64)
    triU = (same & (s[:, None] <= s[None, :])).astype(np.float32)
    triR = (same & (s[:, None] > s[None, :])).astype(np.float32)
    chunkind = np.zeros((128, 2), np.float32)
    chunkind[:64, 0] = 1.0
    chunkind[64:, 1] = 1.0
    maskA4 = np.tile(triU, (1, 4)).astype(np.float32)
    le = (s[:, None] <= s[None, :])
    triM = (le.astype(np.float32) - (s[:, None] <= 63).astype(np.float32) * np.ones((1, 128), np.float32)).astype(np.float32)
    triRf = (s[:, None] > s[None, :]).astype(np.float32)
    maskAf4 = np.tile(le.astype(np.float32), (1, 4)).astype(np.float32)
    ind2 = np.stack([(s < 64).astype(np.float32), np.ones(128, np.float32)], axis=1)
    slopes = np.exp2(-8.0 * np.arange(1, 9, dtype=np.float32) / 8.0).astype(np.float32)
    i = np.arange(128)
    j = np.arange(256)
    dist = np.abs(i[:, None] - (j[None, :] - 128)).astype(np.float32)
    qc = i[:, None] // 64
    kc = j[None, :] // 64 - 2
    vis = (kc >= qc - 2) & (kc <= qc)
    tabP = np.where(vis[:, None, :], -slopes[None, :, None] * dist[:, None, :], NEG).astype(np.float32)
    j = np.arange(384)
    qpos = 128 + (i % 64)
    kpos = np.where(j < 128, j, np.where(j < 256, j - 128, 128 + ((j - 256) % 64)))
    kseq = np.where(j < 128, 0, np.where(j < 256, 1, (j - 256) // 64))
    qseq = i // 64
    vis = qseq[:, None] == kseq[None, :]
    dist = np.abs(qpos[:, None] - kpos[None, :]).astype(np.float32)
    tabS = np.where(vis[:, None, :], -slopes[None, :, None] * dist[:, None, :], NEG).astype(np.float32)
    tabP3 = np.full((128, 8, 384), NEG, np.float32)
    tabP3[:, :, :256] = tabP
    return dict(c_triM=triM, c_triRf=triRf, c_maskAf4=maskAf4, c_ind2=ind2, c_ident=ident, c_triU=triU, c_triR=triR, c_chunkind=chunkind, c_maskA4=maskA4,
                c_tabP=np.ascontiguousarray(tabP3), c_tabS=np.ascontiguousarray(tabS))


def build_program(NP, SEQ, NS):
    assert SEQ % 512 == 0 and NS % 2 == 0
    NSEQ = NP + NS
    nc = bass.Bass("TRN2", target_bir_lowering=False)

    def din(name, shape):
        return nc.dram_tensor(name, list(shape), F32, kind="ExternalInput").ap()

    def dout(name, shape):
        return nc.dram_tensor(name, list(shape), F32, kind="ExternalOutput").ap()

    xp = din("xp", [NP, SEQ, D])
    xs = din("xs", [NS * 64, D])
    st_in = din("st_in", [NS, 4, 128, 128])
    ck_in = din("ck_in", [NS, 128, 128])
    cv_in = din("cv_in", [NS, 128, 128])
    c_in = din("c_in", [NSEQ, D])
    w_ada = din("w_ada", [D, 9 * D])
    b_ada = din("b_ada", [72, 128])
    g_norms = din("g_norms", [24, 128])
    w_up = [din("w_up1", [D, 2 * DFF]), din("w_up2", [D, 2 * DFF])]
    w_down = [din("w_down1", [DFF, D]), din("w_down2", [DFF, D])]
    w_in = din("w_in", [D, 2816])
    lb_logits = din("lb_logits", [2, 512])
    g_hgrn = din("g_hgrn", [1, 128])
    sinks = din("sinks", [1, 8])
    w_out = din("w_out", [D, D])
    g_final = din("g_final", [1, D])
    c_ident = din("c_ident", [128, 128])
    c_triU = din("c_triU", [128, 128])
    c_triR = din("c_triR", [128, 128])
    c_chunkind = din("c_chunkind", [128, 2])
    c_maskA4 = din("c_maskA4", [128, 512])
    c_triM = din("c_triM", [128, 128])
    c_triRf = din("c_triRf", [128, 128])
    c_maskAf4 = din("c_maskAf4", [128, 512])
    c_ind2 = din("c_ind2", [128, 2])
    c_tabP = din("c_tabP", [128, 8, 384])
    c_tabS = din("c_tabS", [128, 8, 384])

    yp = dout("yp", [NP, SEQ, D])
    ys = dout("ys", [NS * 64, D])
    o_sp = dout("o_sp", [NP, 4, 128, 128])
    o_kp = dout("o_kp", [NP, 128, 128])
    o_vp = dout("o_vp", [NP, 128, 128])
    o_ss = dout("o_ss", [NS, 4, 128, 128])
    o_ks = dout("o_ks", [NS, 128, 128])
    o_vs = dout("o_vs", [NS, 128, 128])

    PIECES = []
    for f in range(2):
        if f == 1:
            PIECES.append(("inQ", 0, 2048, 512))
            PIECES.append(("inK", 0, 2560, 256))
            for i, (c0, n) in enumerate([(0, 512), (512, 512), (1024, 512), (1536, 512), (2560, 256)]):
                PIECES.append(("inT", i, c0, n))
            PIECES.append(("out", 0, 0, 512))
            PIECES.append(("out", 1, 512, 512))
        for p in range(11):
            PIECES.append(("up%d" % f, p, 0, 512))
        for half in range(2):
            for kg, (k0, nk) in enumerate([(0, 8), (8, 8), (16, 6)]):
                PIECES.append(("dn%d" % f, half * 3 + kg, k0, nk))
    NPIECE = len(PIECES)
    scr = nc.dram_tensor("scr", [NPIECE, 128, SLOTW], BF16, kind="Internal").ap()
    modD = nc.dram_tensor("modD", [NSEQ, 3, D], F32, kind="Internal").ap()

    S = Sched()
    st = contextlib.ExitStack()
    with st:
        def sb(name, shape, dt=F32):
            return st.enter_context(nc.sbuf_tensor(name, list(shape), dt))

        PB = [st.enter_context(nc.psum_tensor("pb%d" % i, [128, 512], F32)) for i in range(8)]

        def pbn(i):
            return "pb%d" % i

        XT = [sb("xt%d" % i, [128, 4, D]) for i in range(2)]
        WR = [sb("wr%d" % i, [128, SLOTW], BF16) for i in range(NSLOT)]
        hT = sb("hT", [128, 8, 512], BF16)
        yT = sb("yT", [128, 8, 512], BF16)
        hid = sb("hid", [128, NHID, 512], BF16)
        GT = sb("gt", [128, 4, D])
        TAB = sb("tab", [128, 8, 384])
        id_f = sb("id_f", [128, 128])
        id_b = sb("id_b", [128, 128], BF16)
        triU = sb("triU", [128, 128])
        triR = sb("triR", [128, 128])
        chunkind = sb("chunkind", [128, 2])
        maskA4 = sb("maskA4", [128, 512])
        triM = sb("triM", [128, 128])
        triRf = sb("triRf", [128, 128])
        maskAf4 = sb("maskAf4", [128, 512])
        ind2 = sb("ind2", [128, 2])
        oml = sb("oml", [128, 512])
        G4 = sb("G4", [128, 512])
        gfin = sb("gfin", [128, D])
        sinkb = sb("sinkb", [128, 8])
        gT_in = sb("gT_in", [24, 128])
        gfm = sb("gfm", [128, 24])
        bT_in = sb("bT_in", [72, 128])
        bT = sb("bT", [128, 72])
        scT = sb("scT", [128, 8, NSEQ], BF16)
        modFM = sb("modFM", [128, 72, NSEQ])
        GS = sb("GS", [128, 3, 8, NSEQ])
        epsb = sb("epsb", [128, 1])
        stat = sb("stat", [128, 8])
        xn4 = sb("xn4", [128, 4, D], BF16)
        statN = sb("statN", [128, 16])
        PTF = xn4[:, 0:2, :].rearrange("p a b -> p (a b)")
        junk = None
        hidF = hid[:].rearrange("p c n -> p (c n)").bitcast(F32)
        gateTM = hidF[0:NSEQ, 0:3072].rearrange("p (a b) -> p a b", a=3)
        c6 = hidF[0:NSEQ, 3072:4096]
        sc6 = hidF[0:NSEQ, 4096:5120]
        R_gateTM = ["hid%d" % c for c in range(0, 12)]
        R_c6 = ["hid%d" % c for c in range(12, 16)]
        R_sc6 = ["hid%d" % c for c in range(16, 20)]
        lbt = XT[1][:, 0, :].rearrange("p (a b) -> p a b", a=2)
        scsF = hidF[:, 0:2048]
        QT = hidF[:, 2048:3072].bitcast(BF16).rearrange("p (a b) -> p a b", a=4)
        hq = hidF[:, 3072:3584]
        hk = hidF[:, 3584:4096]
        hg = hidF[:, 4096:4608]
        pexF = hidF[:, 4608:5632].bitcast(BF16)
        HIDN = ["hid%d" % c for c in range(NHID)] + ["xn0", "xn1"]
        MIXN = ["scs%d" % q_ for q_ in range(8)] + ["QT", "hq", "hk", "hg"] + ["pexp%d" % q_ for q_ in range(8)] + ["PT%d" % q_ for q_ in range(8)]
        sg = [sb("sg%d" % i, [128, 512]) for i in range(2)]
        junk = sg[0][:].bitcast(BF16)
        tmpE = [sb("tmpE%d" % i, [128, 512]) for i in range(2)]
        ex = [sb("ex%d" % i, [128, 512]) for i in range(2)]
        v_b = sb("v_b", [128, 512], BF16)
        Gg = sb("Gg", [128, 512])
        qt_b = sb("qt_b", [128, 512], BF16)
        kt_b = sb("kt_b", [128, 512], BF16)
        kh_b = sb("kh_b", [128, 512], BF16)
        qTz = [sb("qTz%d" % i, [128, 4, 128], BF16) for i in range(2)]
        kT = sb("kT", [128, 4, 128], BF16)
        AT = sb("AT", [128, 4, 128], BF16)
        ebl = sb("ebl", [128, 8])
        Sf = [sb("Sf%d" % i, [128, 4, 128]) for i in range(2)]
        Sb = [sb("Sb%d" % i, [128, 4, 128], BF16) for i in range(2)]
        hst = sb("hst", [128, 8])
        yhg = sb("yhg", [128, 512], BF16)
        KTn = sb("KTn", [128, 5, 128], BF16)
        KTs = sb("KTs", [128, 5, 128], BF16)
        Vb = sb("Vb", [128, 5, 128], BF16)
        KSn = sb("KSn", [128, 3, 128], BF16)
        KSs = sb("KSs", [128, 3, 128], BF16)
        Vc = sb("Vc", [128, 2, 128], BF16)
        ck_f = sb("ck_f", [128, 2, 2, 128])
        cv_f = sb("cv_f", [128, 2, 128])
        kv_f = sb("kv_f", [128, 256])
        ast = sb("ast", [128, 32])
        ysw = sb("ysw", [128, 512], BF16)

        add = S.add
        OB = [5, 0, 1, 2]

        def ld(eng, dst, src, res, key):
            add(eng, lambda e: e.dma_start(out=dst, in_=src), writes=[res], dma=True, semkey=key)

        ld("act", id_f[:], c_ident, "id_f", "c0")
        ld("act", triU[:], c_triU, "triU", "c1")
        ld("act", triR[:], c_triR, "triR", "c2")
        ld("act", chunkind[:], c_chunkind, "chunkind", "c3")
        ld("act", maskA4[:], c_maskA4, "maskA4", "c4")
        ld("act", triM[:], c_triM, "triM", "c12")
        ld("act", triRf[:], c_triRf, "triRf", "c13")
        ld("act", maskAf4[:], c_maskAf4, "maskAf4", "c14")
        ld("act", ind2[:], c_ind2, "ind2", "c15")
        ld("act", lbt, lb_logits.partition_broadcast(128), "xt1_0", "c5")
        for h in range(4):
            ld("act", G4[:, h * 128:(h + 1) * 128], g_hgrn[0].partition_broadcast(128), "G4_%d" % h, "c6")
        ld("act", gfin[:], g_final[0].partition_broadcast(128), "gfin", "c7")
        ld("act", sinkb[:], sinks[0].partition_broadcast(128), "sinkb", "c8")
        ld("act", gT_in[:], g_norms, "gT_in", "c9")
        ld("act", bT_in[:], b_ada, "bT_in", "c10")
        add("act", lambda e: e.dma_start(out=c6, in_=c_in), writes=R_c6, dma=True, semkey="c11")
        G4r = ["G4_%d" % h for h in range(4)]

        add("dve", lambda e: e.tensor_copy(out=id_b[:], in_=id_f[:]), reads=["id_f"], writes=["id_b"])
        add("dve", lambda e: e.memset(epsb[:], EPS), writes=["epsb"])
        for i in range(2):
            add("pool", lambda e, i=i: e.memset(qTz[i][:], 0.0), writes=["qTz%d" % i])
        add("dve", lambda e: e.tensor_sub(out=oml[:], in0=lbt[:, 1, :], in1=lbt[:, 0, :]), reads=["xt1_0"], writes=["oml"])
        add("act", lambda e: e.activation(out=oml[:], in_=oml[:], func=AF.Sigmoid), reads=["oml"], writes=["oml"])
        add("pe", lambda e: e.matmul(PB[0][:, 0:24], lhsT=gT_in[0:24, :], rhs=id_f[0:24, 0:24], start=True, stop=True),
            reads=["gT_in", "id_f"], writes=[pbn(0)])
        add("dve", lambda e: e.tensor_copy(out=gfm[:], in_=PB[0][:, 0:24]), reads=[pbn(0)], writes=["gfm"])
        add("pe", lambda e: e.matmul(PB[1][:, 0:72], lhsT=bT_in[0:72, :], rhs=id_f[0:72, 0:72], start=True, stop=True),
            reads=["bT_in", "id_f"], writes=[pbn(1)])
        add("dve", lambda e: e.tensor_copy(out=bT[:], in_=PB[1][:, 0:72]), reads=[pbn(1)], writes=["bT"])
        add("act", lambda e: e.activation(out=sc6, in_=c6, func=AF.Silu), reads=R_c6, writes=R_sc6)
        for kc in range(8):
            add("pe", lambda e, kc=kc: e.matmul(PB[2][:, kc * NSEQ:(kc + 1) * NSEQ], lhsT=sc6[0:NSEQ, kc * 128:(kc + 1) * 128],
                                                 rhs=id_f[0:NSEQ, 0:NSEQ], start=True, stop=True),
                reads=R_sc6 + ["id_f"], writes=[pbn(2)])
        add("dve", lambda e: e.tensor_copy(out=scT[:].rearrange("p k s -> p (k s)"), in_=PB[2][:, 0:8 * NSEQ]),
            reads=[pbn(2)], writes=["scT"])

        tiles = []
        for s_ in range(NP):
            for t in range(SEQ // 512):
                tiles.append(dict(kind="p", seq=s_, t=t, ntc=4))
        tiles.append(dict(kind="s", ntc=NS // 2))
        stream = [("ada", blk) for blk in range(18)]
        for ti in range(len(tiles)):
            for pi in range(NPIECE):
                stream.append((ti, pi))
        issued = [0]

        def slot_res(sl):
            return ["wr%d_a" % sl, "wr%d_b" % sl, "wr%d_c" % sl]

        def issue_one():
            g = issued[0]
            if g >= len(stream):
                return
            issued[0] += 1
            sl = g % NSLOT
            key = "w%d" % sl
            res = slot_res(sl)
            a, b = stream[g]
            if a == "ada":
                src = w_ada.rearrange("(kc p) c -> p kc c", p=128)[:, :, b * 512:(b + 1) * 512]
                dst = WR[sl][:].rearrange("p (k c) -> p k c", k=8)
                add("pool", lambda e: e.dma_start(out=dst, in_=src), writes=res, dma=True, semkey=key)
                return
            ti, pi = a, b
            name, idx, c0, n = PIECES[pi]
            if ti > 0:
                add("sp", lambda e: e.dma_start(out=WR[sl][:], in_=scr[pi]), reads=["scr%d" % pi], writes=res, dma=True, semkey="ws%d" % sl)
                return
            if name.startswith("up"):
                W = w_up[int(name[2])].rearrange("(kc p) c -> p kc c", p=128)
                dst = WR[sl][:].rearrange("p (k c) -> p k c", k=8)
                add("pool", lambda e: e.dma_start(out=dst[:, :, 0:256], in_=W[:, :, idx * 256:(idx + 1) * 256]),
                    writes=[res[0]], dma=True, semkey=key)
                add("pool", lambda e: e.dma_start(out=dst[:, :, 256:512], in_=W[:, :, DFF + idx * 256:DFF + (idx + 1) * 256]),
                    writes=[res[1]], dma=True, semkey=key + "b")
            elif name.startswith("dn"):
                half = idx // 3
                k0, nk = c0, n
                W = w_down[int(name[2])].rearrange("(kc p) c -> p kc c", p=128)
                dst = WR[sl][:, 0:nk * 512].rearrange("p (k c) -> p k c", k=nk)
                add("pool", lambda e: e.dma_start(out=dst, in_=W[:, k0:k0 + nk, half * 512:(half + 1) * 512]),
                    writes=res, dma=True, semkey=key)
            elif name == "inK":
                W = w_in.rearrange("(kc p) c -> p kc c", p=128)
                dst = WR[sl][:, 0:8 * 256].rearrange("p (k c) -> p k c", k=8)
                add("pool", lambda e: e.dma_start(out=dst[:, :, 0:128], in_=W[:, :, 2560:2688]), writes=[res[0]], dma=True, semkey=key)
                add("pool", lambda e: e.dma_start(out=dst[:, :, 128:192], in_=W[:, :, 2624:2688]), writes=[res[1]], dma=True, semkey=key + "b")
                add("pool", lambda e: e.dma_start(out=dst[:, :, 192:256], in_=W[:, :, 2560:2624]), writes=[res[2]], dma=True, semkey=key + "c")
            else:
                W = (w_out if name == "out" else w_in).rearrange("(kc p) c -> p kc c", p=128)
                dst = WR[sl][:, 0:8 * n].rearrange("p (k c) -> p k c", k=8)
                add("pool", lambda e: e.dma_start(out=dst, in_=W[:, :, c0:c0 + n]), writes=res, dma=True, semkey=key)
            add("sp", lambda e: e.dma_start(out=scr[pi], in_=WR[sl][:]), reads=res, writes=["scr%d" % pi], dma=True, semkey="scw%d" % sl)

        consumed = [0]

        def next_piece(lookahead=NSLOT):
            g = consumed[0]
            consumed[0] += 1
            while issued[0] < min(len(stream), g + lookahead):
                issue_one()
            sl = g % NSLOT
            return WR[sl], slot_res(sl), stream[g]

        def x_src(tile, tc):
            if tile["kind"] == "p":
                r0 = tile["t"] * 512 + tc * 128
                return xp[tile["seq"], r0:r0 + 128, :]
            return xs[tc * 128:(tc + 1) * 128, :]

        def y_dst(tile, tc):
            if tile["kind"] == "p":
                r0 = tile["t"] * 512 + tc * 128
                return yp[tile["seq"], r0:r0 + 128, :]
            return ys[tc * 128:(tc + 1) * 128, :]

        def load_x(ti):
            tile = tiles[ti]
            slot = ti % 2
            for tc in range(tile["ntc"]):
                add("pool", lambda e, tc=tc: e.dma_start(out=XT[slot][:, tc, :], in_=x_src(tile, tc)),
                    writes=["xt%d_%d" % (slot, tc)], dma=True, semkey="x%d_%d" % (slot, tc))

        load_x(0)

        for blk in range(18):
            wt, wres, _ = next_piece()
            wv = wt[:].rearrange("p (k c) -> p k c", k=8)
            pbk = 4 + (blk % 2)
            for jj in range(4):
                for kc in range(8):
                    add("pe", lambda e, jj=jj, kc=kc, wv=wv, pbk=pbk: e.matmul(
                        PB[pbk][:, jj * NSEQ:(jj + 1) * NSEQ], lhsT=wv[:, kc, jj * 128:(jj + 1) * 128], rhs=scT[:, kc, :],
                        start=(kc == 0), stop=(kc == 7)), reads=wres + ["scT"], writes=[pbn(pbk)])
            for jj in range(4):
                jc = blk * 4 + jj
                add("dve", lambda e, jj=jj, jc=jc, pbk=pbk: e.tensor_scalar(
                    out=modFM[:, jc, :], in0=PB[pbk][:, jj * NSEQ:(jj + 1) * NSEQ], scalar1=bT[:, jc:jc + 1], scalar2=None, op0=ALU.add),
                    reads=[pbn(pbk), "bT"], writes=["modFM%d" % jc])
        for n in range(3):
            for kc in range(8):
                jc = (3 * n + 1) * 8 + kc
                add("dve", lambda e, n=n, kc=kc, jc=jc: e.tensor_scalar(
                    out=GS[:, n, kc, :], in0=modFM[:, jc, :], scalar1=1.0, scalar2=gfm[:, n * 8 + kc:n * 8 + kc + 1],
                    op0=ALU.add, op1=ALU.mult), reads=["modFM%d" % jc, "gfm"], writes=["GS%d_%d" % (n, kc)])
        for n in range(3):
            for kc in range(8):
                jc = (3 * n + 2) * 8 + kc
                pbk = 6 + (n % 2)
                add("pe", lambda e, n=n, kc=kc, jc=jc, pbk=pbk: e.matmul(
                    PB[pbk][0:NSEQ, kc * 128:(kc + 1) * 128] if kc < 4 else PB[pbk][0:NSEQ, (kc - 4) * 128:(kc - 3) * 128],
                    lhsT=modFM[:, jc, :], rhs=id_f[:], start=True, stop=True), reads=["modFM%d" % jc, "id_f"], writes=[pbn(pbk)])
                if kc == 3 or kc == 7:
                    half = kc // 4
                    add("dve", lambda e, n=n, half=half, pbk=pbk: e.tensor_copy(
                        out=gateTM[:, n, half * 512:(half + 1) * 512], in_=PB[pbk][0:NSEQ, :]), reads=[pbn(pbk)], writes=R_gateTM)
        add("pool", lambda e: e.dma_start(out=modD, in_=gateTM), reads=R_gateTM, writes=["modD"], dma=True, semkey="modD")

        def tc_segs(tile, tc):
            if tile["kind"] == "p":
                return [(0, 128, tile["seq"])]
            return [(0, 64, NP + 2 * tc), (64, 128, NP + 2 * tc + 1)]

        gt_rr = [0]

        def load_gate(tile, n, scale_half):
            outs = []
            if tile["kind"] == "p":
                k = gt_rr[0] % 4
                gt_rr[0] += 1
                res = "gt%d" % k
                sq = tile["seq"]
                add("act", lambda e: e.dma_start(out=GT[:, k, :], in_=modD[sq, n, :].partition_broadcast(128)),
                    reads=["modD"], writes=[res], dma=True, semkey="g%d" % k)
                return [(GT[:, k, :], [res])] * tile["ntc"]
            for tc in range(tile["ntc"]):
                k = gt_rr[0] % 4
                gt_rr[0] += 1
                res = "gt%d" % k
                for (p0, p1, sq) in tc_segs(tile, tc):
                    add("act", lambda e, p0=p0, p1=p1, sq=sq, k=k: e.dma_start(
                        out=GT[p0:p1, k, :], in_=modD[sq, n, :].partition_broadcast(64)),
                        reads=["modD"], writes=[res + "_%d" % p0], dma=True, semkey="g%d" % k)
                outs.append((GT[:, k, :], [res + "_0", res + "_64"]))
            return outs

        tr_rr = [0]

        def norm_to_hT(tile, slot, n):
            ntc = tile["ntc"]
            for tc in range(ntc):
                xr = "xt%d_%d" % (slot, tc)
                xa = XT[slot][:, tc, :]
                c0 = 3 * tc
                sr = "statN%d" % tc
                add("dve", lambda e: e.memset(statN[:, c0:c0 + 1], 0.0), writes=[sr])
                add("act", lambda e: e.activation(out=junk[:], in_=xa, func=AF.Square, accum_out=statN[:, c0:c0 + 1]),
                    reads=[xr, sr], writes=[sr] + (["sg0"] if tc == 0 else []))
                add("act", lambda e: e.activation(out=statN[:, c0 + 1:c0 + 2], in_=statN[:, c0:c0 + 1], func=AF.Ln, scale=1.0 / D, bias=epsb[:, 0:1]),
                    reads=[sr, "epsb"], writes=[sr])
                add("act", lambda e: e.activation(out=statN[:, c0 + 2:c0 + 3], in_=statN[:, c0 + 1:c0 + 2], func=AF.Exp, scale=-0.5), reads=[sr], writes=[sr])
            for tc in range(ntc):
                xr = "xt%d_%d" % (slot, tc)
                xa = XT[slot][:, tc, :]
                c0 = 3 * tc
                add("dve", lambda e: e.tensor_scalar(out=xn4[:, tc, :], in0=xa, scalar1=statN[:, c0 + 2:c0 + 3], scalar2=None, op0=ALU.mult),
                    reads=[xr, "statN%d" % tc], writes=["xn%d" % tc])
            for tc in range(ntc):
                xnr = "xn%d" % tc
                pbk = tr_rr[0] % 2
                tr_rr[0] += 1
                pv = PB[pbk][:].bitcast(BF16).rearrange("p (k t) -> p k t", k=8)
                for kc in range(8):
                    add("pe", lambda e: e.transpose(out=pv[:, kc, :], in_=xn4[:, tc, kc * 128:(kc + 1) * 128], identity=id_b[:]),
                        reads=[xnr, "id_b"], writes=[pbn(pbk)])
                for kc in range(8):
                    jc = (3 * n) * 8 + kc
                    for (p0, p1, sq) in tc_segs(tile, tc):
                        hr_ = "hT_%d_%d_%d" % (tc, kc, p0)
                        if kc % 4 != 3:
                            add("dve", lambda e: e.tensor_scalar(
                                out=hT[:, kc, tc * 128 + p0:tc * 128 + p1], in0=pv[:, kc, p0:p1],
                                scalar1=GS[:, n, kc, sq:sq + 1], scalar2=modFM[:, jc, sq:sq + 1], op0=ALU.mult, op1=ALU.add),
                                reads=[pbn(pbk), "GS%d_%d" % (n, kc), "modFM%d" % jc], writes=[hr_])
                        else:
                            add("act", lambda e: e.activation(
                                out=hT[:, kc, tc * 128 + p0:tc * 128 + p1], in_=pv[:, kc, p0:p1], func=AF.Identity,
                                scale=GS[:, n, kc, sq:sq + 1], bias=modFM[:, jc, sq:sq + 1]),
                                reads=[pbn(pbk), "GS%d_%d" % (n, kc), "modFM%d" % jc], writes=[hr_])

        def hT_tc(tile, tc):
            return ["hT_%d_%d_%d" % (tc, kc, p0) for kc in range(8) for (p0, p1, sq) in tc_segs(tile, tc)]

        def hT_res(tile):
            r = []
            for tc in range(tile["ntc"]):
                r += hT_tc(tile, tc)
            return r

        def ffn(tile, slot, f, gates):
            NT = tile["ntc"] * 128
            ntc = tile["ntc"]
            hres = hT_res(tile)
            for p in range(11):
                wt, wres, _ = next_piece()
                wv = wt[:].rearrange("p (k c) -> p k c", k=8)
                for q in range(2):
                    c = 2 * p + q
                    bg, bu = (0, 1) if c % 2 == 0 else (2, 3)
                    for (bk, col) in ((bg, q * 128), (bu, 256 + q * 128)):
                        for kc in range(8):
                            add("pe", lambda e, bk=bk, col=col, kc=kc, wv=wv: e.matmul(
                                PB[bk][:, 0:NT], lhsT=wv[:, kc, col:col + 128], rhs=hT[:, kc, 0:NT], start=(kc == 0), stop=(kc == 7)),
                                reads=wres + hres, writes=[pbn(bk)])
                    sgi = c % 2
                    add("act", lambda e, bg=bg, sgi=sgi: e.activation(out=sg[sgi][:, 0:NT], in_=PB[bg][:, 0:NT], func=AF.Silu),
                        reads=[pbn(bg)], writes=["sg%d" % sgi])
                    add("dve", lambda e, bu=bu, sgi=sgi, c=c: e.tensor_tensor(out=hid[:, c, 0:NT], in0=PB[bu][:, 0:NT], in1=sg[sgi][:, 0:NT], op=ALU.mult),
                        reads=[pbn(bu), "sg%d" % sgi], writes=["hid%d" % c])
            for half in range(2):
                for kg, (k0, nk) in enumerate([(0, 8), (8, 8), (16, 6)]):
                    wt, wres, _ = next_piece()
                    wv = wt[:, 0:nk * 512].rearrange("p (k c) -> p k c", k=nk)
                    for tc in range(ntc):
                        for k in range(nk):
                            kk = k0 + k
                            add("pe", lambda e, tc=tc, k=k, kk=kk, wv=wv: e.matmul(
                                PB[4 + tc][:, :], lhsT=hid[:, kk, tc * 128:(tc + 1) * 128], rhs=wv[:, k, :],
                                start=(kk == 0), stop=(kk == NHID - 1)), reads=wres + ["hid%d" % kk], writes=[pbn(4 + tc)])
                for tc in range(ntc):
                    ga, gres = gates[tc]
                    xr = "xt%d_%d" % (slot, tc)
                    ti_ = tc % 2
                    add("dve", lambda e, tc=tc, ga=ga, half=half, ti_=ti_: e.scalar_tensor_tensor(
                        out=tmpE[ti_][:], in0=PB[4 + tc][:], scalar=0.5, in1=ga[:, half * 512:(half + 1) * 512], op0=ALU.mult, op1=ALU.mult),
                        reads=[pbn(4 + tc)] + gres, writes=["tmpE%d" % ti_])
                    add("pool", lambda e, tc=tc, half=half, ti_=ti_: e.tensor_tensor(
                        out=XT[slot][:, tc, half * 512:(half + 1) * 512], in0=XT[slot][:, tc, half * 512:(half + 1) * 512],
                        in1=tmpE[ti_][:], op=ALU.add), reads=["tmpE%d" % ti_, xr], writes=[xr])

        def mixer(tile, slot, gates, ti):
            ntc = tile["ntc"]
            NT = ntc * 128
            hres = hT_res(tile)
            is_p = tile["kind"] == "p"
            S.fence(HIDN, MIXN)
            if is_p:
                Mq, Mc, Mk, IND = triM, triRf, maskAf4, ind2
                Mres = ["triM", "triRf", "maskAf4", "ind2"]
            else:
                Mq, Mc, Mk, IND = triU, triR, maskA4, chunkind
                Mres = ["triU", "triR", "maskA4", "chunkind"]
            wq_t, wq_res, _ = next_piece()
            wqv = wq_t[:].rearrange("p (k c) -> p k c", k=8)
            for c in range(4):
                bk = c % 2
                for kc in range(8):
                    add("pe", lambda e: e.matmul(PB[bk][:, 0:NT], lhsT=wqv[:, kc, c * 128:(c + 1) * 128], rhs=hT[:, kc, 0:NT],
                                                 start=(kc == 0), stop=(kc == 7)), reads=wq_res + hres, writes=[pbn(bk)])
                add("act", lambda e: e.copy(out=QT[:, c, 0:NT], in_=PB[bk][:, 0:NT]), reads=[pbn(bk)], writes=["QT"])
            wk_t, wk_res, _ = next_piece()
            wkv = wk_t[:, 0:2048].rearrange("p (k c) -> p k c", k=8)
            for v_ in range(2):
                bk = 2 + v_
                for kc in range(8):
                    add("pe", lambda e: e.matmul(PB[bk][:, 0:NT], lhsT=wkv[:, kc, v_ * 128:(v_ + 1) * 128], rhs=hT[:, kc, 0:NT],
                                                 start=(kc == 0), stop=(kc == 7)), reads=wk_res + hres, writes=[pbn(bk)])
                dstK = (KTn if v_ == 0 else KTs)[:, 1:1 + ntc, :]
                add("act", lambda e: e.copy(out=dstK, in_=PB[bk][:, 0:NT].rearrange("p (a b) -> p a b", a=ntc)),
                    reads=[pbn(bk)], writes=["KTn" if v_ == 0 else "KTs"])
            tmw = [next_piece(lookahead=5 - i) for i in range(5)]

            def tm_proj(tc, i5, bank):
                wt, wres, _ = tmw[i5]
                n = 512 if i5 < 4 else 256
                wv = wt[:, 0:8 * n].rearrange("p (k c) -> p k c", k=8)
                for kc in range(8):
                    add("pe", lambda e: e.matmul(PB[bank][:, 0:n], lhsT=hT[:, kc, tc * 128:(tc + 1) * 128], rhs=wv[:, kc, :],
                                                 start=(kc == 0), stop=(kc == 7)), reads=wres + hT_tc(tile, tc), writes=[pbn(bank)])

            def hgrn_chain(tc):
                tsl = slice(tc * 128, (tc + 1) * 128)
                tm_proj(tc, 1, 5)
                add("act", lambda e: e.activation(out=hk[:], in_=PB[5][:], func=AF.Sigmoid, scale=-1.0), reads=[pbn(5)], writes=["hk"])
                add("dve", lambda e: e.tensor_tensor(out=hk[:], in0=hk[:], in1=oml[:], op=ALU.mult), reads=["hk", "oml"], writes=["hk"])
                add("act", lambda e: e.activation(out=hg[:], in_=hk[:], func=AF.Ln, scale=-1.0, bias=1.0), reads=["hk"], writes=["hg"])
                tm_proj(tc, 0, 4)
                tm_proj(tc, 2, 6)
                tm_proj(tc, 3, 7)
                add("act", lambda e: e.activation(out=hq[:], in_=PB[4][:], func=AF.Silu), reads=[pbn(4)], writes=["hq"])
                add("act", lambda e: e.activation(out=Gg[:], in_=PB[7][:], func=AF.Silu), reads=[pbn(7)], writes=["Gg"])
                add("dve", lambda e: e.tensor_copy(out=v_b[:], in_=PB[6][:]), reads=[pbn(6)], writes=["v_b"])
                add("pool", lambda e: e.tensor_tensor(out=Gg[:], in0=Gg[:], in1=G4[:], op=ALU.mult), reads=["Gg"] + G4r, writes=["Gg"])
                tm_proj(tc, 4, 4)
                add("dve", lambda e: e.tensor_copy(out=kv_f[:], in_=PB[4][:, 0:256]), reads=[pbn(4)], writes=["kv_f"])
                vslot = 1 + tc
                add("pool", lambda e: e.tensor_copy(out=Vb[:, vslot, :], in_=kv_f[:, 128:256]), reads=["kv_f"], writes=["Vb%d" % vslot])
                if is_p:
                    if tile["t"] == SEQ // 512 - 1 and tc == 3:
                        add("pool", lambda e: e.dma_start(out=o_kp[tile["seq"]], in_=kv_f[:, 0:128]), reads=["kv_f"], dma=True, semkey="co")
                        add("pool", lambda e: e.dma_start(out=o_vp[tile["seq"]], in_=kv_f[:, 128:256]), reads=["kv_f"], dma=True, semkey="co")
                else:
                    for ch in range(2):
                        sq = 2 * tc + ch
                        add("pool", lambda e: e.dma_start(out=o_ks[sq, 64:128, :], in_=kv_f[ch * 64:(ch + 1) * 64, 0:128]), reads=["kv_f"], dma=True, semkey="co")
                        add("pool", lambda e: e.dma_start(out=o_vs[sq, 64:128, :], in_=kv_f[ch * 64:(ch + 1) * 64, 128:256]), reads=["kv_f"], dma=True, semkey="co")
                        add("pool", lambda e: e.dma_start(out=o_ks[sq, 0:64, :], in_=ck_in[sq, 64:128, :]), dma=True, semkey="co")
                        add("pool", lambda e: e.dma_start(out=o_vs[sq, 0:64, :], in_=cv_in[sq, 64:128, :]), dma=True, semkey="co")
                yield
                add("pe", lambda e: e.matmul(PB[5][:], lhsT=Mq[:], rhs=hg[:], start=True, stop=True), reads=[Mres[0], "hg"], writes=[pbn(5)])
                add("pe", lambda e: e.matmul(PB[6][:], lhsT=Mc[:], rhs=hg[:], start=True, stop=True), reads=[Mres[1], "hg"], writes=[pbn(6)])
                for h in range(4):
                    add("pe", lambda e: e.matmul(PB[7][:, 2 * h:2 * h + 2], lhsT=hg[:, h * 128:(h + 1) * 128], rhs=IND[:], start=True, stop=True),
                        reads=["hg", Mres[3]], writes=[pbn(7)])
                add("act", lambda e: e.activation(out=ex[0][:], in_=PB[5][:], func=AF.Exp), reads=[pbn(5)], writes=["ex0"])
                add("act", lambda e: e.activation(out=ex[1][:], in_=PB[5][:], func=AF.Exp, scale=-1.0), reads=[pbn(5)], writes=["ex1"])
                add("dve", lambda e: e.tensor_tensor(out=qt_b[:], in0=hq[:], in1=ex[0][:], op=ALU.mult), reads=["hq", "ex0"], writes=["qt_b"])
                add("dve", lambda e: e.tensor_tensor(out=kt_b[:], in0=hk[:], in1=ex[1][:], op=ALU.mult), reads=["hk", "ex1"], writes=["kt_b"])
                add("act", lambda e: e.activation(out=ex[0][:], in_=PB[6][:], func=AF.Exp), reads=[pbn(6)], writes=["ex0"])
                add("act", lambda e: e.activation(out=ebl[:], in_=PB[7][:, 0:8], func=AF.Exp), reads=[pbn(7)], writes=["ebl"])
                add("pool", lambda e: e.tensor_tensor(out=kh_b[:], in0=hk[:], in1=ex[0][:], op=ALU.mult), reads=["hk", "ex0"], writes=["kh_b"])
                yield
                pv = PB[4][:].bitcast(BF16).rearrange("p (k t) -> p k t", k=8)
                for h in range(4):
                    add("pe", lambda e: e.transpose(out=pv[:, h, :], in_=qt_b[:, h * 128:(h + 1) * 128], identity=id_b[:]),
                        reads=["qt_b", "id_b"], writes=[pbn(4)])
                for h in range(4):
                    add("pe", lambda e: e.transpose(out=pv[:, 4 + h, :], in_=kt_b[:, h * 128:(h + 1) * 128], identity=id_b[:]),
                        reads=["kt_b", "id_b"], writes=[pbn(4)])
                add("act", lambda e: e.copy(out=qTz[0][:, :, 0:64], in_=pv[:, 0:4, 0:64]), reads=[pbn(4)], writes=["qTz0"])
                add("act", lambda e: e.copy(out=qTz[1][:, :, 64:128], in_=pv[:, 0:4, 64:128]), reads=[pbn(4)], writes=["qTz1"])
                add("act", lambda e: e.copy(out=kT[:], in_=pv[:, 4:8, :]), reads=[pbn(4)], writes=["kT"])
                if is_p:
                    if tile["t"] == 0 and tc == 0:
                        add("dve", lambda e: e.memset(Sf[0][:], 0.0), writes=["Sf0_%d" % h_ for h_ in range(4)])
                    for h in range(4):
                        add("act", lambda e: e.activation(out=Sb[0][:, h, :], in_=Sf[0][:, h, :], func=AF.Copy, scale=ebl[:, 2 * h:2 * h + 1]),
                            reads=["Sf0_%d" % h, "ebl"], writes=["Sb0_%d" % h])
                    sbs = [0, 0]
                else:
                    for ch in range(2):
                        sq = 2 * tc + ch
                        add("act", lambda e: e.dma_start(out=Sf[ch][:], in_=st_in[sq].rearrange("h k v -> k h v")),
                            writes=["Sf%d_%d" % (ch, h_) for h_ in range(4)], dma=True, semkey="sl%d" % ch)
                        add("pool", lambda e: e.tensor_copy(out=Sb[ch][:], in_=Sf[ch][:]), reads=["Sf%d_%d" % (ch, h_) for h_ in range(4)],
                            writes=["Sb%d_%d" % (ch, h_) for h_ in range(4)])
                    sbs = [0, 1]
                yield
                for h in range(4):
                    add("pe", lambda e: e.matmul(PB[5][:, h * 128:h * 128 + 64], lhsT=kT[:, h, :], rhs=qTz[0][:, h, 0:64], start=True, stop=True),
                        reads=["kT", "qTz0"], writes=[pbn(5)])
                    add("pe", lambda e: e.matmul(PB[5][:, h * 128 + 64:h * 128 + 128], lhsT=kT[:, h, :], rhs=qTz[1][:, h, 64:128], start=True, stop=True),
                        reads=["kT", "qTz1"], writes=[pbn(5)])
                add("dve", lambda e: e.tensor_tensor(out=AT[:].rearrange("p h t -> p (h t)"), in0=PB[5][:], in1=Mk[:], op=ALU.mult),
                    reads=[pbn(5), Mres[2]], writes=["AT"])
                yield
                for h in range(4):
                    osl = PB[6][:, h * 128:(h + 1) * 128]
                    add("pe", lambda e: e.matmul(osl, lhsT=AT[:, h, :], rhs=v_b[:, h * 128:(h + 1) * 128], start=True, stop=False),
                        reads=["AT", "v_b"], writes=[pbn(6)])
                    add("pe", lambda e: e.matmul(osl, lhsT=qTz[0][:, h, :], rhs=Sb[sbs[0]][:, h, :], start=False, stop=False),
                        reads=["qTz0", "Sb%d_%d" % (sbs[0], h)], writes=[pbn(6)])
                    add("pe", lambda e: e.matmul(osl, lhsT=qTz[1][:, h, :], rhs=Sb[sbs[1]][:, h, :], start=False, stop=True),
                        reads=["qTz1", "Sb%d_%d" % (sbs[1], h)], writes=[pbn(6)])
                if is_p:
                    for h in range(4):
                        add("pe", lambda e: e.matmul(PB[7][:, h * 128:(h + 1) * 128], lhsT=kh_b[:, h * 128:(h + 1) * 128],
                                                     rhs=v_b[:, h * 128:(h + 1) * 128], start=True, stop=True),
                            reads=["kh_b", "v_b"], writes=[pbn(7)])
                    for h in range(4):
                        add("dve", lambda e: e.scalar_tensor_tensor(
                            out=Sf[0][:, h, :], in0=Sf[0][:, h, :], scalar=ebl[:, 2 * h + 1:2 * h + 2], in1=PB[7][:, h * 128:(h + 1) * 128],
                            op0=ALU.mult, op1=ALU.add), reads=["Sf0_%d" % h, "ebl", pbn(7)], writes=["Sf0_%d" % h])
                    if tile["t"] == SEQ // 512 - 1 and tc == 3:
                        dsto = o_sp[tile["seq"]].rearrange("h k v -> k h v")
                        add("pool", lambda e: e.dma_start(out=dsto, in_=Sf[0][:]), reads=["Sf0_%d" % h_ for h_ in range(4)], dma=True, semkey="so0")
                else:
                    for ch in range(2):
                        ps_ = slice(ch * 64, (ch + 1) * 64)
                        for h in range(4):
                            add("pe", lambda e: e.matmul(PB[7][:, h * 128:(h + 1) * 128], lhsT=kh_b[ps_, h * 128:(h + 1) * 128],
                                                         rhs=v_b[ps_, h * 128:(h + 1) * 128], start=True, stop=True),
                                reads=["kh_b", "v_b"], writes=[pbn(7)])
                        for h in range(4):
                            add("dve", lambda e: e.scalar_tensor_tensor(
                                out=Sf[ch][:, h, :], in0=Sf[ch][:, h, :], scalar=ebl[:, 2 * h + ch:2 * h + ch + 1], in1=PB[7][:, h * 128:(h + 1) * 128],
                                op0=ALU.mult, op1=ALU.add), reads=["Sf%d_%d" % (ch, h), "ebl", pbn(7)], writes=["Sf%d_%d" % (ch, h)])
                        dsto = o_ss[2 * tc + ch].rearrange("h k v -> k h v")
                        add("pool", lambda e: e.dma_start(out=dsto, in_=Sf[ch][:]), reads=["Sf%d_%d" % (ch, h_) for h_ in range(4)], dma=True, semkey="so%d" % ch)
                yield
                add("dve", lambda e: e.memset(hst[:], 0.0), writes=["hst0", "hst1", "hst2", "hst3", "hstr"])
                for h in range(4):
                    add("act", lambda e: e.activation(out=junk[:, h * 128:(h + 1) * 128], in_=PB[6][:, h * 128:(h + 1) * 128], func=AF.Square,
                                                      accum_out=hst[:, h:h + 1]), reads=[pbn(6), "hst%d" % h], writes=["hst%d" % h] + (["sg0"] if h == 0 else []))
                add("act", lambda e: e.activation(out=hst[:, 4:8], in_=hst[:, 0:4], func=AF.Ln, scale=1.0 / 128, bias=epsb[:, 0:1]),
                    reads=["hst0", "hst1", "hst2", "hst3", "epsb"], writes=["hstr"])
                add("act", lambda e: e.activation(out=hst[:, 4:8], in_=hst[:, 4:8], func=AF.Exp, scale=-0.5), reads=["hstr"], writes=["hstr"])
                for h in range(4):
                    add("dve", lambda e: e.scalar_tensor_tensor(out=yhg[:, h * 128:(h + 1) * 128], in0=PB[6][:, h * 128:(h + 1) * 128],
                                                              scalar=hst[:, 4 + h:5 + h], in1=Gg[:, h * 128:(h + 1) * 128],
                                                              op0=ALU.mult, op1=ALU.mult), reads=[pbn(6), "hstr", "Gg"], writes=["yhg%d" % h])
                yield
                pv4 = PB[4][:].bitcast(BF16).rearrange("p (k t) -> p k t", k=8)
                for h in range(4):
                    add("pe", lambda e: e.transpose(out=pv4[:, h, :], in_=yhg[:, h * 128:(h + 1) * 128], identity=id_b[:]),
                        reads=["yhg%d" % h, "id_b"], writes=[pbn(4)])
                add("act", lambda e: e.copy(out=yT[:, 0:4, tsl], in_=pv4[:, 0:4, :]), reads=[pbn(4)], writes=["yTh_%d" % tc])
                yield

            def swa_chain(tc):
                tsl = slice(tc * 128, (tc + 1) * 128)
                if is_p:
                    first = tile["t"] == 0 and tc == 0
                    nk = 128 if first else 256
                    kb0 = (1 + tc) if first else tc
                    tcol0 = 128 if first else 0
                    Kn, Ks, kres = KTn, KTs, ["KTn", "KTs"]
                    vblocks = [(Vb[:, 1 + tc, :], "Vb%d" % (1 + tc))] if first else [(Vb[:, tc, :], "Vb%d" % tc), (Vb[:, 1 + tc, :], "Vb%d" % (1 + tc))]
                    GH = 8
                else:
                    nk = 384
                    kb0 = 0
                    tcol0 = 0
                    GH = 2
                    Kn, Ks, kres = KSn, KSs, ["KSn", "KSs"]
                    for ch in range(2):
                        sq = 2 * tc + ch
                        add("act", lambda e: e.dma_start(out=ck_f[:, ch, 0, :], in_=ck_in[sq]), writes=["ck_f%d" % ch], dma=True, semkey="ck%d_0" % ch)
                        add("act", lambda e: e.dma_start(out=ck_f[:, ch, 1, 0:64], in_=ck_in[sq][:, 64:128]), writes=["ck_fa%d" % ch], dma=True, semkey="ck%d_1" % ch)
                        add("act", lambda e: e.dma_start(out=ck_f[:, ch, 1, 64:128], in_=ck_in[sq][:, 0:64]), writes=["ck_fb%d" % ch], dma=True, semkey="ck%d_2" % ch)
                        add("act", lambda e: e.dma_start(out=cv_f[:, ch, :], in_=cv_in[sq]), writes=["cv_f%d" % ch], dma=True, semkey="ck%d_3" % ch)
                        for v_ in range(2):
                            add("pe", lambda e: e.matmul(PB[3][:, (2 * ch + v_) * 128:(2 * ch + v_ + 1) * 128], lhsT=ck_f[:, ch, v_, :], rhs=id_f[:],
                                                         start=True, stop=True),
                                reads=["ck_f%d" % ch, "ck_fa%d" % ch, "ck_fb%d" % ch, "id_f"], writes=[pbn(3)])
                        add("pool", lambda e: e.tensor_copy(out=Vc[:, ch, :], in_=cv_f[:, ch, :]), reads=["cv_f%d" % ch], writes=["Vc"])
                    add("act", lambda e: e.copy(out=KSn[:, 0:2, :], in_=PB[3][:].rearrange("p (c v t) -> p c v t", c=2, v=2)[:, :, 0, :]), reads=[pbn(3)], writes=["KSn"])
                    add("act", lambda e: e.copy(out=KSs[:, 0:2, :], in_=PB[3][:].rearrange("p (c v t) -> p c v t", c=2, v=2)[:, :, 1, :]), reads=[pbn(3)], writes=["KSs"])
                    add("pool", lambda e: e.tensor_copy(out=KSn[:, 2, :], in_=KTn[:, 1 + tc, :]), reads=["KTn"], writes=["KSn"])
                    add("pool", lambda e: e.tensor_copy(out=KSs[:, 2, :], in_=KTs[:, 1 + tc, :]), reads=["KTs"], writes=["KSs"])
                    vblocks = [(Vc[:, 0, :], "Vc"), (Vc[:, 1, :], "Vc"), (Vb[:, 1 + tc, :], "Vb%d" % (1 + tc))]
                    yield
                nkb = nk // 128
                scsv = scsF[:, 0:GH * nk].rearrange("p (h k) -> p h k", h=GH)
                pexv = pexF[:, 0:GH * nk].rearrange("p (h k) -> p h k", h=GH)
                PTv = PTF[:, 0:GH * nkb * 128].rearrange("p (h k t) -> p h k t", h=GH, k=nkb)
                A_M, A_NM, A_ES, A_RS = 0, 8, 16, 24
                for g in range(8 // GH):
                    for hh in range(GH):
                        h = GH * g + hh
                        base = 64 * (h % 2)
                        kvh = h // 4
                        nat = (kvh == 0 and base == 0) or (kvh == 1 and base == 64)
                        Kt = Kn if nat else Ks
                        if is_p:
                            sbk, sc0 = hh // 2, (hh % 2) * 256
                        else:
                            sbk, sc0 = hh, 0
                        add("pe", lambda e: e.matmul(
                            PB[sbk][:, sc0:sc0 + nk], lhsT=QT[base:base + 64, h // 2, tsl],
                            rhs=Kt[base:base + 64, kb0:kb0 + nkb, :].rearrange("p a b -> p (a b)"), start=True, stop=True),
                            reads=["QT"] + kres, writes=[pbn(sbk)])
                        add("dve", lambda e: e.scalar_tensor_tensor(
                            out=scsv[:, hh, :], in0=PB[sbk][:, sc0:sc0 + nk], scalar=0.125, in1=TAB[:, h, tcol0:tcol0 + nk],
                            op0=ALU.mult, op1=ALU.add), reads=[pbn(sbk), "tab"], writes=["scs%d" % hh])
                    yield
                    scn = ["scs%d" % q_ for q_ in range(GH)]
                    rsn = ["ast_rs%d" % q_ for q_ in range(GH)]
                    add("dve", lambda e: e.tensor_reduce(out=ast[:, A_M:A_M + GH], in_=scsv, axis=AX.X, op=ALU.max), reads=scn, writes=["ast_m"])
                    add("dve", lambda e: e.tensor_tensor(out=ast[:, A_M:A_M + GH], in0=ast[:, A_M:A_M + GH], in1=sinkb[:, GH * g:GH * g + GH], op=ALU.max),
                        reads=["ast_m", "sinkb"], writes=["ast_m"])
                    add("dve", lambda e: e.tensor_scalar(out=ast[:, A_NM:A_NM + GH], in0=ast[:, A_M:A_M + GH], scalar1=-1.0, scalar2=None, op0=ALU.mult),
                        reads=["ast_m"], writes=["ast_nm"])
                    add("dve", lambda e: e.tensor_tensor(out=ast[:, A_ES:A_ES + GH], in0=sinkb[:, GH * g:GH * g + GH], in1=ast[:, A_M:A_M + GH], op=ALU.subtract),
                        reads=["ast_m", "sinkb"], writes=["ast_es"])
                    add("dve", lambda e: e.memset(ast[:, A_RS:A_RS + 8], 0.0), writes=["ast_rs%d" % q_ for q_ in range(8)])
                    add("act", lambda e: e.activation(out=ast[:, A_ES:A_ES + GH], in_=ast[:, A_ES:A_ES + GH], func=AF.Exp), reads=["ast_es"], writes=["ast_es"])
                    for hh in range(GH):
                        add("act", lambda e: e.activation(out=pexv[:, hh, :], in_=scsv[:, hh, :], func=AF.Exp, bias=ast[:, A_NM + hh:A_NM + hh + 1],
                                                          accum_out=ast[:, A_RS + hh:A_RS + hh + 1]),
                            reads=["scs%d" % hh, "ast_nm", "ast_rs%d" % hh], writes=["pexp%d" % hh, "ast_rs%d" % hh])
                    add("dve", lambda e: e.tensor_tensor(out=ast[:, A_RS:A_RS + GH], in0=ast[:, A_RS:A_RS + GH], in1=ast[:, A_ES:A_ES + GH], op=ALU.add),
                        reads=["ast_es"] + rsn, writes=rsn)
                    add("dve", lambda e: e.reciprocal(out=ast[:, A_RS:A_RS + GH], in_=ast[:, A_RS:A_RS + GH]), reads=rsn, writes=rsn)
                    yield
                    for hh in range(GH):
                        for kb in range(nkb):
                            idx = hh * nkb + kb
                            tb = (idx // 8) if is_p else 2
                            pvt = PB[tb][:].bitcast(BF16).rearrange("p (k t) -> p k t", k=8)
                            add("pe", lambda e: e.transpose(out=pvt[:, idx % 8, :], in_=pexv[:, hh, kb * 128:(kb + 1) * 128], identity=id_b[:]),
                                reads=["pexp%d" % hh, "id_b"], writes=[pbn(tb)])
                    hpb = (8 // nkb) if is_p else GH
                    for h0 in range(0, GH, hpb):
                        hn = min(hpb, GH - h0)
                        tb = ((h0 * nkb) // 8) if is_p else 2
                        pvt = PB[tb][:].bitcast(BF16).rearrange("p (k t) -> p k t", k=8)
                        add("act", lambda e: e.copy(out=PTv[:, h0:h0 + hn, :, :],
                                                    in_=pvt[:, 0:hn * nkb, :].rearrange("p (h k) t -> p h k t", k=nkb)),
                            reads=[pbn(tb)], writes=["PT%d" % q_ for q_ in range(h0, h0 + hn)])
                    yield
                    ob = 2 if is_p else 3
                    for hh in range(GH):
                        h = GH * g + hh
                        kvh = h // 4
                        for kb in range(nkb):
                            va, vres = vblocks[kb]
                            add("pe", lambda e: e.matmul(PB[ob][:, hh * 64:(hh + 1) * 64], lhsT=PTv[:, hh, kb, :],
                                                         rhs=va[:, kvh * 64:(kvh + 1) * 64], start=(kb == 0), stop=(kb == nkb - 1)),
                                reads=["PT%d" % hh, vres], writes=[pbn(ob)])
                    h_lo = GH * g
                    add("dve", lambda e: e.tensor_tensor(
                        out=ysw[:, h_lo * 64:(h_lo + GH) * 64].rearrange("p (h d) -> p h d", h=GH),
                        in0=PB[ob][:, 0:GH * 64].rearrange("p (h d) -> p h d", h=GH),
                        in1=ast[:, A_RS:A_RS + GH].unsqueeze(2).to_broadcast([128, GH, 64]), op=ALU.mult),
                        reads=[pbn(ob)] + ["ast_rs%d" % q_ for q_ in range(GH)], writes=["ysw%d" % (h_lo + q_) for q_ in range(GH)])
                    yield
                pv3 = PB[3][:].bitcast(BF16).rearrange("p (k t) -> p k t", k=8)
                for c in range(4):
                    add("pe", lambda e: e.transpose(out=pv3[:, c, :], in_=ysw[:, c * 128:(c + 1) * 128], identity=id_b[:]),
                        reads=["ysw%d" % (2 * c), "ysw%d" % (2 * c + 1), "id_b"], writes=[pbn(3)])
                add("act", lambda e: e.copy(out=yT[:, 4:8, tsl], in_=pv3[:, 0:4, :]), reads=[pbn(3)], writes=["yTs_%d" % tc])
                yield

            def run_chains(chains):
                chains = [c for c in chains if c is not None]
                while chains:
                    for c in list(chains):
                        try:
                            next(c)
                        except StopIteration:
                            chains.remove(c)

            for tc in range(ntc + 1):
                run_chains([hgrn_chain(tc) if tc < ntc else None, swa_chain(tc - 1) if tc >= 1 else None])
            if is_p:
                add("pool", lambda e: e.tensor_copy(out=KTn[:, 0, :], in_=KTn[:, 4, :]), reads=["KTn"], writes=["KTn"])
                add("pool", lambda e: e.tensor_copy(out=KTs[:, 0, :], in_=KTs[:, 4, :]), reads=["KTs"], writes=["KTs"])
                add("pool", lambda e: e.tensor_copy(out=Vb[:, 0, :], in_=Vb[:, 4, :]), reads=["Vb4"], writes=["Vb0"])
            for half in range(2):
                wt, wres, _ = next_piece()
                wv = wt[:].rearrange("p (k c) -> p k c", k=8)
                for tc in range(ntc):
                    ga, gres = gates[tc]
                    xr = "xt%d_%d" % (slot, tc)
                    bk = tc % 2
                    for fc in range(8):
                        add("pe", lambda e, tc=tc, fc=fc, wv=wv, bk=bk: e.matmul(PB[bk][:], lhsT=yT[:, fc, tc * 128:(tc + 1) * 128], rhs=wv[:, fc, :],
                                                                          start=(fc == 0), stop=(fc == 7)), reads=wres + ["yTh_%d" % tc, "yTs_%d" % tc], writes=[pbn(bk)])
                    ti_ = tc % 2
                    add("dve", lambda e, ga=ga, half=half, bk=bk, ti_=ti_: e.tensor_tensor(out=tmpE[ti_][:], in0=PB[bk][:], in1=ga[:, half * 512:(half + 1) * 512], op=ALU.mult),
                        reads=[pbn(bk)] + gres, writes=["tmpE%d" % ti_])
                    add("pool", lambda e, tc=tc, half=half, ti_=ti_: e.tensor_tensor(
                        out=XT[slot][:, tc, half * 512:(half + 1) * 512], in0=XT[slot][:, tc, half * 512:(half + 1) * 512],
                        in1=tmpE[ti_][:], op=ALU.add), reads=["tmpE%d" % ti_, xr], writes=[xr])

        def final_norm_store(tile, slot):
            for tc in range(tile["ntc"]):
                xr = "xt%d_%d" % (slot, tc)
                xa = XT[slot][:, tc, :]
                if DBG:
                    add("pool", lambda e, tc=tc, xa=xa: e.dma_start(out=y_dst(tile, tc), in_=xa), reads=[xr], dma=True, semkey="y%d_%d" % (slot, tc))
                    continue
                add("dve", lambda e: e.memset(stat[:, 4:5], 0.0), writes=["stat2"])
                add("act", lambda e, xa=xa: e.activation(out=junk[:], in_=xa, func=AF.Square, accum_out=stat[:, 4:5]),
                    reads=[xr, "stat2"], writes=["stat2"] + (["sg0"] if tc == 0 else []))
                add("act", lambda e: e.activation(out=stat[:, 5:6], in_=stat[:, 4:5], func=AF.Ln, scale=1.0 / D, bias=epsb[:, 0:1]),
                    reads=["stat2", "epsb"], writes=["stat2"])
                add("act", lambda e: e.activation(out=stat[:, 6:7], in_=stat[:, 5:6], func=AF.Exp, scale=-0.5), reads=["stat2"], writes=["stat2"])
                add("dve", lambda e, xa=xa: e.scalar_tensor_tensor(out=xa, in0=xa, scalar=stat[:, 6:7], in1=gfin[:], op0=ALU.mult, op1=ALU.mult),
                    reads=[xr, "stat2", "gfin"], writes=[xr])
                add("pool", lambda e, tc=tc, xa=xa: e.dma_start(out=y_dst(tile, tc), in_=xa), reads=[xr], dma=True, semkey="y%d_%d" % (slot, tc))

        cur_tab = [None]
        g1_next = [None]
        for ti, tile in enumerate(tiles):
            slot = ti % 2
            if ti + 1 < len(tiles):
                load_x(ti + 1)
            want = "P" if tile["kind"] == "p" else "S"
            if cur_tab[0] != want:
                src = c_tabP if want == "P" else c_tabS
                add("act", lambda e, src=src: e.dma_start(out=TAB[:], in_=src), writes=["tab"], dma=True, semkey="tab")
                cur_tab[0] = want
            if ti == 0:
                g1 = load_gate(tile, 0, True)
                norm_to_hT(tile, slot, 0)
            else:
                g1 = g1_next[0]
            ffn(tile, slot, 0, g1)
            if DBG != 1:
                g2 = load_gate(tile, 1, False)
                norm_to_hT(tile, slot, 1)
                mixer(tile, slot, g2, ti)
            else:
                for _ in range(9):
                    next_piece()
            if DBG not in (1, 2):
                g3 = load_gate(tile, 2, True)
                S.fence(MIXN, HIDN)
                norm_to_hT(tile, slot, 2)
                ffn(tile, slot, 1, g3)
            else:
                for _ in range(17):
                    next_piece()
            if ti + 1 < len(tiles) and not DBG:
                g1_next[0] = load_gate(tiles[ti + 1], 0, True)
                norm_to_hT(tiles[ti + 1], (ti + 1) % 2, 0)
            elif ti + 1 < len(tiles):
                g1_next[0] = load_gate(tiles[ti + 1], 0, True)
                norm_to_hT(tiles[ti + 1], (ti + 1) % 2, 0)
            final_norm_store(tile, slot)

        S.emit(nc)
    return nc


_CONST = None


def kernel(x_prompt, x_sample, state_hgrn, cache_k, cache_v, c_prompt, c_sample,
           w_ada, b_ada, g_ffn1, w_up1, w_down1, g_mix, w_in, lb_logits, g_hgrn, sinks,
           w_out, g_ffn2, w_up2, w_down2, g_final, _cfg=None):
    n_cores = 8 if _cfg is None else _cfg
    f = lambda a: np.ascontiguousarray(np.asarray(a, dtype=np.float32))
    x_prompt, x_sample = f(x_prompt), f(x_sample)
    B, SEQ, _ = x_prompt.shape
    DB = x_sample.shape[0]
    NP = B // n_cores
    NS = DB // n_cores
    consts = host_constants()
    shared = dict(
        w_ada=f(w_ada)[0], b_ada=f(b_ada)[0].reshape(72, 128),
        g_norms=np.concatenate([f(g_ffn1)[0].reshape(8, 128), f(g_mix)[0].reshape(8, 128), f(g_ffn2)[0].reshape(8, 128)], axis=0),
        w_up1=f(w_up1)[0], w_up2=f(w_up2)[0], w_down1=f(w_down1)[0], w_down2=f(w_down2)[0],
        w_in=f(w_in)[0], lb_logits=f(lb_logits), g_hgrn=f(g_hgrn), sinks=f(sinks), w_out=f(w_out)[0],
        g_final=f(g_final).reshape(1, D), **consts)
    st0 = f(state_hgrn)[0]
    ck0 = f(cache_k)[0].reshape(DB, 128, 128)
    cv0 = f(cache_v)[0].reshape(DB, 128, 128)
    cp, cs = f(c_prompt), f(c_sample)
    in_maps = []
    for i in range(n_cores):
        m = dict(shared)
        m["xp"] = x_prompt[i * NP:(i + 1) * NP]
        m["xs"] = x_sample[i * NS:(i + 1) * NS].reshape(NS * 64, D)
        m["st_in"] = st0[i * NS:(i + 1) * NS]
        m["ck_in"] = ck0[i * NS:(i + 1) * NS]
        m["cv_in"] = cv0[i * NS:(i + 1) * NS]
        m["c_in"] = np.concatenate([cp[i * NP:(i + 1) * NP], cs[i * NS:(i + 1) * NS]], axis=0)
        in_maps.append(m)
    nc = build_program(NP, SEQ, NS)
    res = run_bass_kernel_spmd(nc, in_maps, core_ids=list(range(n_cores)))
    R = res.results
    cat = lambda k: np.concatenate([np.asarray(r[k]) for r in R], axis=0)
    y_prompt = cat("yp").reshape(B, SEQ, D)
    y_sample = cat("ys").reshape(DB, 64, D)
    sp = cat("o_sp").reshape(1, B, 4, 128, 128)
    kp = cat("o_kp").reshape(1, B, 128, 2, 64)
    vp = cat("o_vp").reshape(1, B, 128, 2, 64)
    ss = cat("o_ss").reshape(1, DB, 4, 128, 128)
    ks = cat("o_ks").reshape(1, DB, 128, 2, 64)
    vs = cat("o_vs").reshape(1, DB, 128, 2, 64)
    return tuple(np.ascontiguousarray(a, dtype=np.float32) for a in (y_prompt, y_sample, sp, kp, vp, ss, ks, vs))
```
